# Optimizing a Trainium2 kernel written in Bass

```python
import jax, jax.numpy as jnp
from jax import lax
import numpy as np

D_MODEL = 1024
BATCH = 4
SEQ = 8192
DEPTH = 1

HG_HEADS = 8
HG_KEY_DIM = 128
HG_VAL_DIM = D_MODEL // HG_HEADS
HG_CHUNK = 64
FOX_HEADS = 16
FOX_HEAD_DIM = 64
FOX_BLOCK = 128
FOX_F_BIAS_INIT = 2.0
D_FF = 2816
CONV_WIDTH = 3
EPS = 1e-6

HG_QK = HG_HEADS * HG_KEY_DIM
HG_V = HG_HEADS * HG_VAL_DIM
FOX_W = FOX_HEADS * FOX_HEAD_DIM
SPLITS = (HG_QK, HG_QK, HG_V, HG_V, FOX_W, FOX_W, FOX_W, FOX_HEADS, D_MODEL, D_MODEL)
D_IN = sum(SPLITS)

kernel_name = 'hgrn2_fox_gated_hybrid_block'


def rms_norm(x, g):
    xf = x.astype(jnp.float32)
    y = xf * lax.rsqrt(jnp.mean(xf * xf, axis=-1, keepdims=True) + EPS)
    return (y * g.astype(jnp.float32)).astype(x.dtype)


def hgrn2_mixer(q, f_logit, i, g, lb, g_norm):
    B, S, _ = q.shape
    n_chunks = S // HG_CHUNK
    f32 = jnp.float32
    q = jax.nn.silu(q.astype(f32))
    f = lb + (1.0 - lb) * jax.nn.sigmoid(f_logit.astype(f32))
    k = 1.0 - f
    log_f = jnp.log(f)

    def chunks(t, d):
        return t.astype(f32).reshape(B, n_chunks, HG_CHUNK, HG_HEADS, d).transpose(1, 0, 3, 2, 4)

    xs = (chunks(q, HG_KEY_DIM), chunks(k, HG_KEY_DIM), chunks(log_f, HG_KEY_DIM), chunks(i, HG_VAL_DIM))
    causal = jnp.tril(jnp.ones((HG_CHUNK, HG_CHUNK), dtype=bool))[:, :, None]

    def step(state, inp):
        qc, kc, gc, vc = inp
        b = jnp.cumsum(gc, axis=2)
        o_inter = jnp.einsum('bhtk,bhkv->bhtv', qc * jnp.exp(b), state)
        rel = b[:, :, :, None, :] - b[:, :, None, :, :]
        decay = jnp.exp(jnp.where(causal, rel, -jnp.inf))
        scores = jnp.einsum('bhtsk,bhsk->bhts', qc[:, :, :, None, :] * decay, kc)
        o_intra = jnp.einsum('bhts,bhsv->bhtv', scores, vc)
        b_end = b[:, :, -1:, :]
        state = jnp.exp(b_end[:, :, 0, :])[..., None] * state + jnp.einsum('bhsk,bhsv->bhkv', kc * jnp.exp(b_end - b), vc)
        return state, o_inter + o_intra

    state0 = jnp.zeros((B, HG_HEADS, HG_KEY_DIM, HG_VAL_DIM), f32)
    _, o = lax.scan(step, state0, xs)
    o = o.transpose(1, 0, 3, 2, 4).reshape(B, S, HG_HEADS, HG_VAL_DIM)
    o = rms_norm(o, g_norm) * jax.nn.silu(g.astype(f32)).reshape(B, S, HG_HEADS, HG_VAL_DIM)
    return o.reshape(B, S, HG_V).astype(i.dtype)


def fox_mixer(q, k, v, f_logit, f_bias):
    B, S, _ = q.shape
    n_blocks = S // FOX_BLOCK
    f32 = jnp.float32
    q = q.reshape(B, S, FOX_HEADS, FOX_HEAD_DIM) * FOX_HEAD_DIM ** -0.5
    k = k.reshape(B, S, FOX_HEADS, FOX_HEAD_DIM)
    v = v.reshape(B, S, FOX_HEADS, FOX_HEAD_DIM)
    log_f = jax.nn.log_sigmoid(f_logit.astype(f32) + f_bias.astype(f32))
    c = jnp.cumsum(log_f, axis=1).transpose(0, 2, 1)
    q_blocks = q.reshape(B, n_blocks, FOX_BLOCK, FOX_HEADS, FOX_HEAD_DIM).transpose(1, 0, 2, 3, 4)
    c_blocks = c.reshape(B, FOX_HEADS, n_blocks, FOX_BLOCK).transpose(2, 0, 1, 3)
    key_pos = jnp.arange(S)

    def attend(args):
        blk, qb, cb = args
        logits = jnp.einsum('bqhd,bshd->bhqs', qb, k).astype(f32)
        logits = logits + (cb[..., None] - c[:, :, None, :])
        q_pos = blk * FOX_BLOCK + jnp.arange(FOX_BLOCK)
        mask = key_pos[None, :] <= q_pos[:, None]
        p = jax.nn.softmax(jnp.where(mask, logits, -jnp.inf), axis=-1)
        return jnp.einsum('bhqs,bshd->bqhd', p.astype(v.dtype), v)

    o = lax.map(attend, (jnp.arange(n_blocks), q_blocks, c_blocks))
    return o.transpose(1, 0, 2, 3, 4).reshape(B, S, FOX_W)


def conv_glu_ffn(x, w_up, conv_w, conv_b, w_down):
    S = x.shape[1]
    u = x @ w_up
    u_pad = jnp.pad(u, ((0, 0), (CONV_WIDTH - 1, 0), (0, 0)))
    acc = conv_b
    for j in range(CONV_WIDTH):
        acc = acc + conv_w[j] * u_pad[:, j:j + S]
    gate, val = jnp.split(acc, 2, axis=-1)
    return (jax.nn.gelu(gate, approximate=False) * val) @ w_down


def setup_inputs(seed: int = 0) -> dict:
    key = jax.random.key(seed)
    ks = jax.random.split(key, 16)
    f32 = jnp.float32

    def dense(k, shape, fan_in):
        return jax.random.normal(k, shape, f32) * fan_in ** -0.5

    def gain(k, shape):
        return 1.0 + 0.02 * jax.random.normal(k, shape, f32)

    return {
        'x': jax.random.normal(ks[0], (BATCH, SEQ, D_MODEL), f32),
        'norm_mix': gain(ks[1], (DEPTH, D_MODEL)),
        'w_in': dense(ks[2], (DEPTH, D_MODEL, D_IN), D_MODEL),
        'fox_f_bias': FOX_F_BIAS_INIT + 0.5 * jax.random.normal(ks[3], (DEPTH, FOX_HEADS), f32),
        'hg_lb_logits': 0.1 * jax.random.normal(ks[4], (DEPTH + 1, HG_QK), f32),
        'hg_norm': gain(ks[5], (DEPTH, HG_VAL_DIM)),
        'w_branch_a': dense(ks[6], (DEPTH, HG_V, D_MODEL), HG_V),
        'w_branch_b': dense(ks[7], (DEPTH, FOX_W, D_MODEL), FOX_W),
        'w_out': dense(ks[8], (DEPTH, D_MODEL, D_MODEL), D_MODEL),
        'norm_ffn': gain(ks[9], (DEPTH, D_MODEL)),
        'w_up': dense(ks[10], (DEPTH, D_MODEL, 2 * D_FF), D_MODEL),
        'conv_w': dense(ks[11], (DEPTH, CONV_WIDTH, 2 * D_FF), CONV_WIDTH),
        'conv_b': 0.02 * jax.random.normal(ks[12], (DEPTH, 2 * D_FF), f32),
        'w_down': dense(ks[13], (DEPTH, D_FF, D_MODEL), D_FF),
        'norm_final': gain(ks[14], (D_MODEL,)),
    }


def reference(x, norm_mix, w_in, fox_f_bias, hg_lb_logits, hg_norm, w_branch_a, w_branch_b, w_out, norm_ffn, w_up, conv_w, conv_b, w_down, norm_final):
    h = x
    lb_all = jnp.cumsum(jax.nn.softmax(hg_lb_logits.astype(jnp.float32), axis=0), axis=0)
    cuts = np.cumsum(SPLITS)[:-1].tolist()
    for layer in range(DEPTH):
        n = rms_norm(h, norm_mix[layer])
        proj = n @ w_in[layer]
        hq, hf, hi, hg, fq, fk, fv, ff, ga, gb = jnp.split(proj, cuts, axis=-1)
        o_a = hgrn2_mixer(hq, hf, hi, hg, lb_all[layer], hg_norm[layer])
        o_b = fox_mixer(fq, fk, fv, ff, fox_f_bias[layer])
        merged = jax.nn.sigmoid(ga) * (o_a @ w_branch_a[layer]) + jax.nn.sigmoid(gb) * (o_b @ w_branch_b[layer])
        h = h + merged @ w_out[layer]
        h = h + conv_glu_ffn(rms_norm(h, norm_ffn[layer]), w_up[layer], conv_w[layer], conv_b[layer], w_down[layer])
    return rms_norm(h, norm_final)
```

```python
import contextlib
import numpy as np
import concourse.bass as bass
import concourse.mybir as mybir
from concourse.bass_utils import run_bass_kernel_spmd

F32 = mybir.dt.float32
BF16 = mybir.dt.bfloat16
AF = mybir.ActivationFunctionType
ALU = mybir.AluOpType
ENG = ('pe', 'act', 'dve', 'pool', 'sp')
EPS = 1e-6


class Prog:
    def __init__(self, nc, es):
        self.nc = nc
        self.es = es
        self.streams = {e: [] for e in ENG}
        self.sem = {}
        self.cnt = {}
        for e in ('pe', 'act', 'dve', 'pool'):
            self.sem['c_' + e] = es.enter_context(nc.semaphore('c_' + e))
            self.cnt['c_' + e] = 0
        self.waited = {e: {} for e in ENG}
        self.lastw = {}
        self.readers = {}
        self.n = 0
        self.pkeys = {}

    def _dsem(self, key, eng):
        key = (eng, key)
        if key not in self.pkeys:
            self.pkeys[key] = sum(1 for k in self.pkeys if k[0] == eng)
        sid = 'd%s%d' % (eng, self.pkeys[key])
        if sid not in self.sem:
            self.sem[sid] = self.es.enter_context(self.nc.semaphore(sid))
            self.cnt[sid] = 0
        return sid

    def _wait(self, eng, sid, val):
        if self.waited[eng].get(sid, 0) >= val:
            return
        self.waited[eng][sid] = val
        h = self.sem[sid]
        self.streams[eng].append(lambda e: e.wait_ge(h, val))

    def op(self, eng, fn, reads=(), writes=(), dma=None):
        own = 'c_' + eng
        deps = {}
        for k in reads:
            ev = self.lastw.get(k)
            if ev is not None:
                deps[ev[0]] = max(deps.get(ev[0], 0), ev[1])
        for k in writes:
            ev = self.lastw.get(k)
            if ev is not None:
                deps[ev[0]] = max(deps.get(ev[0], 0), ev[1])
            for sid, val in self.readers.get(k, {}).items():
                deps[sid] = max(deps.get(sid, 0), val)
        if eng == 'pe':
            deps.pop(own, None)
        for sid, val in deps.items():
            self._wait(eng, sid, val)
        if dma is None:
            sid = own
            self.cnt[sid] += 1
            inc = 1
        else:
            sid = self._dsem(dma, eng)
            self.cnt[sid] += 16
            inc = 16
        val = self.cnt[sid]
        h = self.sem[sid]
        self.streams[eng].append(lambda e: fn(e).then_inc(h, inc))
        for k in writes:
            self.lastw[k] = (sid, val)
            self.readers[k] = {}
        for k in reads:
            d = self.readers.setdefault(k, {})
            d[sid] = max(d.get(sid, 0), val)
        self.n += 1

    def I(self, eng, method, r, w, **kw):
        self.op(eng, lambda e: getattr(e, method)(**kw), r, w)

    def dma(self, eng, out, in_, key, r, w):
        self.op(eng, lambda e: e.dma_start(out=out, in_=in_), r, w, dma=key)

    def barrier(self):
        for e in ENG:
            for sid, c in self.cnt.items():
                if c > 0:
                    self._wait(e, sid, c)

    def emit(self):
        st = self.streams
        with self.nc.Block() as block:
            if st['pe']:
                @block.tensor
                def _(e):
                    for f in st['pe']:
                        f(e)
            if st['act']:
                @block.scalar
                def _(e):
                    for f in st['act']:
                        f(e)
            if st['dve']:
                @block.vector
                def _(e):
                    for f in st['dve']:
                        f(e)
            if st['pool']:
                @block.gpsimd
                def _(e):
                    for f in st['pool']:
                        f(e)
            if st['sp']:
                @block.sync
                def _(e):
                    for f in st['sp']:
                        f(e)
        self.streams = {e: [] for e in ENG}
        self.pkeys = {}


class WStream:
    def __init__(self, P, bufs, wsc, schedule, res=3):
        self.P = P
        self.bufs = bufs
        self.wsc = wsc
        self.sched = schedule
        self.res = res
        self.issued = 0
        self.ptr = -1

    def _issue(self, i):
        name, idx, nk = self.sched[i]
        b = i % len(self.bufs)
        src = self.wsc[name][idx].rearrange("p (k c) -> p k c", c=512)[:, 0:nk, :]
        self.P.dma('sp', self.bufs[b][:, 0:nk, :], src, ('w', b), [], [('w', b)])

    def get(self, name, idx):
        for back in range(0, self.res):
            i = self.ptr - back
            if i >= 0 and self.sched[i][0] == name and self.sched[i][1] == idx:
                return self.bufs[i % len(self.bufs)], ('w', i % len(self.bufs))
        self.ptr += 1
        assert self.sched[self.ptr][0] == name and self.sched[self.ptr][1] == idx, (self.sched[self.ptr], name, idx)
        look = len(self.bufs) - self.res
        while self.issued < len(self.sched) and self.issued <= self.ptr + look:
            self._issue(self.issued)
            self.issued += 1
        i = self.ptr
        return self.bufs[i % len(self.bufs)], ('w', i % len(self.bufs))


def build(T=8192, NGP=8, debug=False, phases=('0', 'A', 'B', 'C')):
    nc = bass.Bass("TRN2", target_bir_lowering=False)
    NG = T // 512
    GF = max(NGP - 1, 0)
    TO = (NG - NGP) * 512
    NBLK = T // 128

    def din(name, shape, dt=F32):
        return nc.dram_tensor(name, list(shape), dt, kind="ExternalInput").ap()

    x = din('x', [T, 1024])
    w_in = din('w_in', [1024, 9232])
    w_a = din('w_a', [1024, 1024])
    w_b = din('w_b', [1024, 1024])
    w_o = din('w_o', [1024, 1024])
    w_up = din('w_up', [1024, 5632])
    w_dn = din('w_dn', [2816, 1024])
    gmix_d = din('gmix', [128, 8])
    gffn_d = din('gffn', [128, 8])
    gfinb_d = din('gfinb', [128, 1024])
    ffb_d = din('ffb', [16, 1])
    lbl_d = din('lbl', [128, 2, 8])
    gnorm_d = din('gnorm', [128, 1])
    cw_d = din('cw', [128, 44, 3])
    cb_d = din('cb', [128, 44])
    ident_d = din('ident', [128, 128])
    mneg_d = din('mneg', [128, 128])
    m01_d = din('m01', [128, 128])
    valid_d = din('valid', [128, T // 128])
    y = nc.dram_tensor('y', [TO, 1024], F32, kind='ExternalOutput').ap()

    skind = 'ExternalOutput' if debug else 'Internal'

    def dscr(name, shape, dt=BF16):
        return nc.dram_tensor(name, list(shape), dt, kind=skind).ap()

    wsc = {
        'win': dscr('s_win', [18, 128, 4096]),
        'wa': dscr('s_wa', [2, 128, 4096]),
        'wb': dscr('s_wb', [2, 128, 4096]),
        'wo': dscr('s_wo', [2, 128, 4096]),
        'wup': dscr('s_wup', [11, 128, 4096]),
        'wdn': dscr('s_wdn', [6, 128, 4096]),
    }
    nTd = dscr('s_nT', [NG, 128, 4096])
    QTd = dscr('s_QT', [16, 69, T])
    KTd = dscr('s_KT', [16, 69, T])
    Vd = dscr('s_V', [16, 128, NBLK, 65])
    OTd = dscr('s_OT', [16, 64, T])

    with contextlib.ExitStack() as es:
        P = Prog(nc, es)

        def sb(name, shape, dt, stack=es):
            return stack.enter_context(nc.sbuf_tensor('t_' + name, list(shape), dt))

        psA = contextlib.ExitStack()
        ps = [psA.enter_context(nc.psum_tensor('ps%d' % i, [128, 512], F32)) for i in range(6)]
        pst = psA.enter_context(nc.psum_tensor('pst', [128, 1024], BF16))
        pst2 = psA.enter_context(nc.psum_tensor('pst2', [128, 1024], BF16))

        identf = sb('identf', [128, 128], F32)
        identb = sb('identb', [128, 128], BF16)
        mnegb = sb('mnegb', [128, 128], BF16)
        m01 = sb('m01', [128, 128], F32)
        onesf = sb('onesf', [128, 512], F32)
        onesb = sb('onesb', [128, 128], BF16)
        gmix = sb('gmix', [128, 8], F32)
        gffn = sb('gffn', [128, 8], F32)
        gfinb = sb('gfinb', [128, 1024], F32)
        ffb = sb('ffb', [16, 1], F32)
        negfb = sb('negfb', [16, 1], F32)
        lbl = sb('lbl', [128, 2, 8], F32)
        lbd = sb('lbd', [128, 8], F32)
        lb = sb('lb', [128, 8], F32)
        oml = sb('oml', [128, 8], F32)
        gnorm = sb('gnorm', [128, 1], F32)
        cw = sb('cw', [128, 44, 3], F32)
        cb = sb('cb', [128, 44], F32)
        epst = sb('epst', [128, 1], F32)
        wff32 = sb('wff32', [128, 8, 16], F32)
        wffb = sb('wffb', [128, 8, 16], BF16)
        valid = sb('valid', [128, T // 128], F32)

        bg_jobs = []
        cl = [('identf', identf, ident_d), ('mneg32', None, None), ('m01', m01, m01_d), ('gmix', gmix, gmix_d),
              ('gffn', gffn, gffn_d), ('gfinb', gfinb, gfinb_d), ('ffb', ffb, ffb_d), ('lbl', lbl, lbl_d),
              ('gnorm', gnorm, gnorm_d), ('cw', cw, cw_d), ('cb', cb, cb_d), ('valid', valid, valid_d)]
        with contextlib.ExitStack() as es0:
            mneg32 = sb('mneg32', [128, 128], F32, es0)
            for name, t, d in cl:
                if t is None:
                    t, d = mneg32, mneg_d
                P.dma('sp', t[:], d, 'c', [], [name])
            P.dma('sp', wff32[:], w_in.rearrange("(kc p) c -> p kc c", p=128)[:, :, 7168:7184], 'c', [], ['wff32'])
            P.barrier()
            P.I('dve', 'tensor_copy', ['identf'], ['identb'], out=identb[:], in_=identf[:])
            P.I('dve', 'tensor_copy', ['mneg32'], ['mnegb'], out=mnegb[:], in_=mneg32[:])
            P.I('dve', 'tensor_copy', ['wff32'], ['wffb'], out=wffb[:], in_=wff32[:])
            P.I('dve', 'memset', [], ['onesf'], ap=onesf[:], constant=1.0)
            P.I('dve', 'memset', [], ['onesb'], ap=onesb[:], constant=1.0)
            P.I('dve', 'memset', [], ['epst'], ap=epst[:], constant=EPS)
            P.I('dve', 'tensor_scalar', ['ffb'], ['negfb'], out=negfb[:], in0=ffb[:], scalar1=-1.0, scalar2=None, op0=ALU.mult)
            P.I('dve', 'tensor_tensor', ['lbl'], ['lbd'], out=lbd[:], in0=lbl[:, 0, :], in1=lbl[:, 1, :], op=ALU.subtract)
            P.I('act', 'activation', ['lbd'], ['lb'], out=lb[:], in_=lbd[:], func=AF.Sigmoid)
            P.I('dve', 'tensor_scalar', ['lb'], ['oml'], out=oml[:], in0=lb[:], scalar1=-1.0, scalar2=1.0, op0=ALU.mult, op1=ALU.add)

            if '0' in phases:
                st32 = [sb('st32_%d' % i, [128, 8, 512], F32, es0) for i in range(2)]
                stb = [sb('stb_%d' % i, [128, 8, 512], BF16, es0) for i in range(2)]
                pieces = []
                win_r = w_in.rearrange("(kc p) c -> p kc c", p=128)
                for i in range(14):
                    pieces.append(('win', i, win_r[:, :, i * 512:(i + 1) * 512], 8))
                for i in range(4):
                    pieces.append(('win', 14 + i, win_r[:, :, 7184 + i * 512:7184 + (i + 1) * 512], 8))
                for nm, ap in (('wa', w_a), ('wb', w_b), ('wo', w_o)):
                    r = ap.rearrange("(kc p) c -> p kc c", p=128)
                    for i in range(2):
                        pieces.append((nm, i, r[:, :, i * 512:(i + 1) * 512], 8))
                r = w_up.rearrange("(kc p) c -> p kc c", p=128)
                for i in range(11):
                    pieces.append(('wup', i, r[:, :, i * 512:(i + 1) * 512], 8))
                r = w_dn.rearrange("(kc p) c -> p kc c", p=128)
                for kg in range(3):
                    nk = 8 if kg < 2 else 6
                    for half in range(2):
                        pieces.append(('wdn', kg * 2 + half, r[:, kg * 8:kg * 8 + nk, half * 512:(half + 1) * 512], nk))
                ceng = ('dve', 'pool', 'act')
                bg_pieces = [p_ for p_ in pieces if not (p_[0] == 'win' and 8 <= p_[1] <= 13)]
                pieces = []
                for nm, idx, src, nk in bg_pieces:
                    hk = nk // 2
                    for kh in range(2):
                        bg_jobs.append((nm, idx, src[:, kh * hk:(kh + 1) * hk, :], hk, kh * hk))
                for i, (nm, idx, src, nk) in enumerate(pieces):
                    s = i % 2
                    P.dma('sp', st32[s][:, 0:nk, :], src, ('st32', s), [], [('st32', s)])
                    ce = ceng[i % 3]
                    if ce == 'act':
                        P.I('act', 'activation', [('st32', s)], [('stb', s)], out=stb[s][:, 0:nk, :], in_=st32[s][:, 0:nk, :], func=AF.Copy)
                    else:
                        P.I(ce, 'tensor_copy', [('st32', s)], [('stb', s)], out=stb[s][:, 0:nk, :], in_=st32[s][:, 0:nk, :])
                    dst = wsc[nm][idx].rearrange("p (k c) -> p k c", c=512)[:, 0:nk, :]
                    P.dma('pool', dst, stb[s][:, 0:nk, :], ('stb', s), [('stb', s)], [])
            P.barrier()
            P.emit()

        def rms_N(pfx, xt, xkey, junk, xn, xnkey, ssq, std, rstd):
            for b in range(4):
                P.I('act', 'activation', [xkey], [pfx + 'junk', (pfx + 'ss', b)], out=junk[:], in_=xt[:, b, :], func=AF.Square,
                    accum_out=ssq[:, b:b + 1])
            P.I('act', 'activation', [(pfx + 'ss', b) for b in range(4)], [pfx + 'std'], out=std[:], in_=ssq[:], func=AF.Sqrt,
                scale=1.0 / 1024.0, bias=epst[:, 0:1])
            P.I('dve', 'reciprocal', [pfx + 'std'], [pfx + 'rstd'], out=rstd[:], in_=std[:])
            for b in range(4):
                if b % 2 == 0:
                    P.I('dve', 'tensor_scalar', [xkey, pfx + 'rstd'], [(xnkey, b)], out=xn[:, b, :], in0=xt[:, b, :],
                        scalar1=rstd[:, b:b + 1], scalar2=None, op0=ALU.mult)
                else:
                    P.I('act', 'activation', [xkey, pfx + 'rstd'], [(xnkey, b)], out=xn[:, b, :], in_=xt[:, b, :],
                        func=AF.Copy, scale=rstd[:, b:b + 1])

        def rms_T(xn, xnkey, nT, nTkey, gcols, psts):
            for kc in range(8):
                pt, pk = psts[kc % len(psts)]
                for b in range(4):
                    P.I('pe', 'transpose', [(xnkey, b)], [pk], out=pt[:, b * 128:(b + 1) * 128],
                        in_=xn[:, b, kc * 128:(kc + 1) * 128], identity=identb[:])
                P.I('act', 'activation', [pk], [(nTkey, kc)], out=nT[:, kc, :], in_=pt[:, 0:512],
                    func=AF.Copy, scale=gcols[:, kc:kc + 1])

        def rms_to_T(pfx, xt, xkey, junk, xn, xnkey, ssq, std, rstd, nT, nTkey, gcols):
            rms_N(pfx, xt, xkey, junk, xn, xnkey, ssq, std, rstd)
            rms_T(xn, xnkey, nT, nTkey, gcols, [(pst, ('pst', 0))])

        if 'A' in phases:
            with contextlib.ExitStack() as esA:
                xt1 = sb('a_xt', [128, 4, 1024], F32, esA)
                st32h = [sb('a_st32_%d' % i, [128, 4, 512], F32, esA) for i in range(2)]
                stbh = [sb('a_stb_%d' % i, [128, 4, 512], BF16, esA) for i in range(2)]
                bgc = [0]

                def bg_convert(n):
                    for _ in range(n):
                        if not bg_jobs:
                            return
                        nm, idx, src, hk, k0 = bg_jobs.pop(0)
                        sl = bgc[0] % 2
                        bgc[0] += 1
                        P.dma('sp', st32h[sl][:, 0:hk, :], src, ('a_st32', sl), [], [('a_st32', sl)])
                        P.I('pool', 'tensor_copy', [('a_st32', sl)], [('a_stb', sl)], out=stbh[sl][:, 0:hk, :], in_=st32h[sl][:, 0:hk, :])
                        dst = wsc[nm][idx].rearrange("p (k c) -> p k c", c=512)[:, k0:k0 + hk, :]
                        P.dma('pool', dst, stbh[sl][:, 0:hk, :], ('a_stbst', sl), [('a_stb', sl)], [])

                junk = sb('a_junk', [128, 1024], BF16, esA)
                xn = sb('a_xn', [128, 4, 1024], BF16, esA)
                nT = [sb('a_nT%d' % i, [128, 8, 512], BF16, esA) for i in range(2)]
                wres = [sb('a_w%d' % i, [128, 8, 512], BF16, esA) for i in range(6)]
                QTall = sb('a_QT', [64, 16, 512], BF16, esA)
                KTall = sb('a_KT', [64, 16, 512], BF16, esA)
                Vt = [sb('a_Vt%d' % i, [128, 16, 4, 65], BF16, esA) for i in range(2)]
                ssq = sb('a_ss', [128, 4], F32, esA)
                std = sb('a_std', [128, 4], F32, esA)
                rstd = sb('a_rstd', [128, 4], F32, esA)
                ffe = sb('a_ffe', [16, 512], F32, esA)
                ffl = sb('a_ffl', [16, 512], F32, esA)
                cg = [sb('a_cg%d' % i, [16, 4, 128], F32, esA) for i in range(2)]
                t32a = sb('a_t32a', [16, 512], F32, esA)
                t32b = sb('a_t32b', [16, 512], F32, esA)
                t32c = sb('a_t32c', [16, 512], F32, esA)
                afull = sb('a_afull', [16, 512], F32, esA)
                afr = sb('a_afr', [16, 512], F32, esA)
                kaug = [sb('a_kaug%d' % i, [16, 3, 512], BF16, esA) for i in range(2)]
                qaug = [sb('a_qaug%d' % i, [16, 2, 512], BF16, esA) for i in range(2)]
                ones3 = sb('a_ones3', [16, 3, 512], BF16, esA)

                P.I('dve', 'memset', [], ['a_ones3'], ap=ones3[:], constant=1.0)
                for s in range(2):
                    P.I('pool', 'memset', [], [('a_Vt', s, b_, h_) for b_ in range(4) for h_ in range(2)], ap=Vt[s][:], constant=1.0)
                P.dma('sp', xt1[:], x[0:512, :].rearrange("(b p) d -> p b d", p=128), 'a_x', [], ['a_x'])
                win_rA = w_in.rearrange("(kc p) c -> p kc c", p=128)
                cengA = ('dve', 'act')
                for i in (2, 3, 4, 5, 0, 1):
                    for kh in range(2):
                        sl = bgc[0] % 2
                        bgc[0] += 1
                        P.dma('sp', st32h[sl][:], win_rA[:, kh * 4:(kh + 1) * 4, (8 + i) * 512:(9 + i) * 512], ('a_st32', sl), [], [('a_st32', sl)])
                        ce = cengA[bgc[0] % 2]
                        if ce == 'act':
                            P.I('act', 'activation', [('a_st32', sl)], [('aw', i)], out=wres[i][:, kh * 4:(kh + 1) * 4, :], in_=st32h[sl][:], func=AF.Copy)
                        else:
                            P.I(ce, 'tensor_copy', [('a_st32', sl)], [('aw', i)], out=wres[i][:, kh * 4:(kh + 1) * 4, :], in_=st32h[sl][:])
                bank = [0]

                def nb():
                    bank[0] = (bank[0] + 1) % 6
                    return bank[0]

                apst = [(pst, ('pst', 0)), (pst2, ('pst', 1))]

                def normN(g):
                    rms_N('a_', xt1, 'a_x', junk, xn, 'a_xn', ssq, std, rstd)
                    if g + 1 < NG:
                        P.dma('sp', xt1[:], x[(g + 1) * 512:(g + 2) * 512, :].rearrange("(b p) d -> p b d", p=128), 'a_x', [], ['a_x'])

                def normT(g):
                    s_ = g % 2
                    rms_T(xn, 'a_xn', nT[s_], ('a_nT', s_), gmix, apst)
                    P.dma('pool', nTd[g].rearrange("p (k c) -> p k c", c=512), nT[s_][:], ('a_nTst', s_), [(('a_nT', s_), kc) for kc in range(8)], [])

                normN(0)
                normT(0)
                for g in range(NG):
                    s = g % 2
                    gsl = slice(g * 512, (g + 1) * 512)
                    bg_convert(5)
                    nkeys = [(('a_nT', s), kc) for kc in range(8)]
                    for which, wbase, dst, dkey, scl in (('q', 0, QTall, 'a_QT', 0.125), ('k', 2, KTall, 'a_KT', 1.0)):
                        if which == 'q' and g < GF:
                            continue
                        for hp in range(8):
                            wi = wbase + hp // 4
                            off = (hp % 4) * 128
                            bi = nb()
                            for kc in range(8):
                                P.I('pe', 'matmul', [('aw', wi), (('a_nT', s), kc)], [('ps', bi)], out=ps[bi][:, :],
                                    lhsT=wres[wi][:, kc, off:off + 128], rhs=nT[s][:, kc, :], start=(kc == 0), stop=(kc == 7))
                            P.I('dve', 'tensor_scalar', [('ps', bi)], [(dkey, 2 * hp)], out=dst[0:64, 2 * hp, :], in0=ps[bi][0:64, :],
                                scalar1=scl, scalar2=None, op0=ALU.mult)
                            P.I('act', 'activation', [('ps', bi)], [(dkey, 2 * hp + 1)], out=dst[0:64, 2 * hp + 1, :],
                                in_=ps[bi][64:128, :], func=AF.Copy, scale=scl)
                    if g + 1 < NG:
                        normN(g + 1)
                    for b in range(4):
                        for half in range(2):
                            bi = nb()
                            for kc in range(8):
                                P.I('pe', 'matmul', [('aw', 4 + half), (('a_nT', s), kc)], [('ps', bi)], out=ps[bi][:, :],
                                    lhsT=nT[s][:, kc, b * 128:(b + 1) * 128], rhs=wres[4 + half][:, kc, :], start=(kc == 0), stop=(kc == 7))
                            e = 'dve' if half == 0 else 'act'
                            outv = Vt[s][:, half * 8:(half + 1) * 8, b, 0:64]
                            inv = ps[bi][:, :].rearrange("p (h d) -> p h d", d=64)
                            if e == 'dve':
                                P.I('dve', 'tensor_copy', [('ps', bi)], [('a_Vt', s, b, half)], out=outv, in_=inv)
                            else:
                                P.I('act', 'activation', [('ps', bi)], [('a_Vt', s, b, half)], out=outv, in_=inv, func=AF.Copy)
                    for b in range(4):
                        P.I('dve', 'tensor_scalar', ['valid', 'onesf'], [('a_Vt', s, b, 0), ('a_Vt', s, b, 1)], out=Vt[s][:, :, b, 64:65],
                            in0=onesf[:, 0:16].rearrange("p (h o) -> p h o", o=1), scalar1=valid[:, g * 4 + b:g * 4 + b + 1], scalar2=None, op0=ALU.mult)
                    bff = nb()
                    for kc in range(8):
                        P.I('pe', 'matmul', ['wffb', (('a_nT', s), kc)], [('ps', bff)], out=ps[bff][0:16, :], lhsT=wffb[:, kc, :],
                            rhs=nT[s][:, kc, :], start=(kc == 0), stop=(kc == 7))
                    P.I('act', 'activation', [('ps', bff), 'negfb'], ['a_ffe'], out=ffe[:], in_=ps[bff][0:16, :], func=AF.Exp,
                        scale=-1.0, bias=negfb[:, 0:1])
                    P.I('act', 'activation', ['a_ffe'], ['a_ffl'], out=ffl[:], in_=ffe[:], func=AF.Ln, scale=1.0, bias=onesf[0:16, 0:1])
                    cgf = cg[s][:].rearrange("p a b -> p (a b)")
                    init = 0.0 if g == 0 else cg[1 - s][:, 3, 127:128]
                    P.I('dve', 'tensor_tensor_scan', ['a_ffl', ('a_cg', 1 - s), 'onesf'], [('a_cg', s)], out=cgf, data0=onesf[0:16, :],
                        data1=ffl[:], initial=init, op0=ALU.mult, op1=ALU.subtract)
                    P.I('dve', 'tensor_scalar', [('a_cg', s)], ['a_t32a'], out=t32a[:], in0=cgf, scalar1=-1.0, scalar2=None, op0=ALU.mult)
                    P.I('dve', 'tensor_copy', ['a_t32a'], [('a_kaug0', s)], out=kaug[s][:, 0, :], in_=t32a[:])
                    P.I('dve', 'tensor_tensor', ['a_t32a', ('a_kaug0', s)], ['a_t32b'], out=t32b[:], in0=t32a[:], in1=kaug[s][:, 0, :], op=ALU.subtract)
                    P.I('dve', 'tensor_copy', ['a_t32b'], [('a_kaug1', s)], out=kaug[s][:, 1, :], in_=t32b[:])
                    P.I('dve', 'tensor_tensor', ['a_t32b', ('a_kaug1', s)], ['a_t32c'], out=t32c[:], in0=t32b[:], in1=kaug[s][:, 1, :], op=ALU.subtract)
                    P.I('dve', 'tensor_copy', ['a_t32c'], [('a_kaug2', s)], out=kaug[s][:, 2, :], in_=t32c[:])
                    for b in range(4 if g >= GF else 0):
                        P.I('dve', 'tensor_scalar', [('a_cg', s), 'onesf'], [('a_afull', b)], out=afull[:, b * 128:(b + 1) * 128],
                            in0=onesf[0:16, 0:128], scalar1=cg[s][:, b, 63:64], scalar2=None, op0=ALU.mult)
                    afk = [('a_afull', b) for b in range(4)]
                    if g >= GF:
                        P.I('dve', 'tensor_copy', afk, [('a_qaug0', s)], out=qaug[s][:, 0, :], in_=afull[:])
                        P.I('dve', 'tensor_tensor', afk + [('a_qaug0', s)], ['a_afr'], out=afr[:], in0=afull[:], in1=qaug[s][:, 0, :], op=ALU.subtract)
                        P.I('dve', 'tensor_copy', ['a_afr'], [('a_qaug1', s)], out=qaug[s][:, 1, :], in_=afr[:])
                        P.dma('pool', QTd[:, 0:64, gsl].rearrange("h r t -> r h t"), QTall[:], 'a_QTst', [('a_QT', h) for h in range(16)], [])
                        P.dma('pool', QTd[:, 64:66, gsl], qaug[s][:], ('a_qaugst', s), [('a_qaug0', s), ('a_qaug1', s)], [])
                        P.dma('pool', QTd[:, 66:69, gsl], ones3[:], 'a_o2st', ['a_ones3'], [])
                    P.dma('pool', KTd[:, 0:64, gsl].rearrange("h r t -> r h t"), KTall[:], 'a_KTst', [('a_KT', h) for h in range(16)], [])
                    P.dma('pool', KTd[:, 64:66, gsl], ones3[:, 0:2, :], 'a_o1st', ['a_ones3'], [])
                    P.dma('pool', KTd[:, 66:69, gsl], kaug[s][:], ('a_kaugst', s), [('a_kaug0', s), ('a_kaug1', s), ('a_kaug2', s)], [])
                    if g + 1 < NG:
                        normT(g + 1)
                    for hh in range(2):
                        P.dma('pool', Vd[hh * 8:(hh + 1) * 8, :, g * 4:(g + 1) * 4, :].rearrange("h p b d -> p h (b d)"),
                              Vt[s][:, hh * 8:(hh + 1) * 8, :, :].rearrange("p h b d -> p h (b d)"), ('a_Vst', s), [('a_Vt', s, b_, h_) for b_ in range(4) for h_ in range(2)], [])
                bg_convert(len(bg_jobs))
                P.barrier()
                P.emit()

        psA.close()
        if 'B' in phases:
            with contextlib.ExitStack() as esB:
                SS = [esB.enter_context(nc.psum_tensor('b_SS%d' % i, [128, 1024], F32)) for i in range(3)]
                PO = [esB.enter_context(nc.psum_tensor('b_PO%d' % i, [128, 512], F32)) for i in range(2)]
                KT = [sb('b_KT%d' % i, [69, T], BF16, esB) for i in range(2)]
                QT = [sb('b_QT%d' % i, [69, T], BF16, esB) for i in range(2)]
                V = [sb('b_V%d' % i, [128, NBLK, 65], BF16, esB) for i in range(2)]
                PP = [sb('b_P%d' % i, [128, 1024], BF16, esB) for i in range(3)]
                osb = [sb('b_osb%d' % i, [65, 512], F32, esB) for i in range(2)]
                rden = [sb('b_rden%d' % i, [65, 512], F32, esB) for i in range(2)]
                OT = [sb('b_OT%d' % i, [64, 512], BF16, esB) for i in range(2)]
                rot = [0]
                fin = [0]
                deferred = []
                LOOK = 2

                def loadhead(h):
                    s = h % 2
                    P.dma('sp', KT[s][:], KTd[h], ('b_KT', s), [], [('b_KT', s)])
                    P.dma('sp', QT[s][:, GF * 512:], QTd[h, :, GF * 512:], ('b_QT', s), [], [('b_QT', s)])
                    P.dma('sp', V[s][:], Vd[h], ('b_V', s), [], [('b_V', s)])

                gcount = {}

                def goi(h, g):
                    return (h * (NG - GF) + (g - GF)) % 2

                DEFER = max(1, min(8, 2 * GF + 1))

                def qk(h, g, p):
                    s = h % 2
                    ri = rot[0] % 3
                    rot[0] += 1
                    offs = []
                    halo = (NGP >= 1 and g == GF and p >= 1)
                    for half in range(2):
                        kb = 2 * p + half
                        j = kb - 4 * g
                        off = 384 if halo else max(0, j) * 128
                        need_mask = (j == 3) if halo else (j >= 0)
                        offs.append(off)
                        c0 = half * 512
                        P.I('pe', 'matmul', [('b_KT', s), ('b_QT', s)], [('b_ss', ri)], out=SS[ri][:, c0 + off:c0 + 512],
                            lhsT=KT[s][:, kb * 128:(kb + 1) * 128], rhs=QT[s][:, g * 512 + off:(g + 1) * 512], start=True, stop=(not need_mask))
                        if need_mask:
                            P.I('pe', 'matmul', ['identb', 'mnegb'], [('b_ss', ri)], out=SS[ri][:, c0 + off:c0 + off + 128], lhsT=identb[:],
                                rhs=mnegb[:], start=False, stop=True)
                    if halo:
                        P.I('act', 'activation', [('b_ss', ri)], [('b_P', ri)], out=PP[ri][:, :].rearrange("p (h c) -> p h c", c=512)[:, :, 384:512],
                            in_=SS[ri][:, :].rearrange("p (h c) -> p h c", c=512)[:, :, 384:512], func=AF.Exp)
                    elif 2 * p + 1 < 4 * g:
                        P.I('act', 'activation', [('b_ss', ri)], [('b_P', ri)], out=PP[ri][:, :], in_=SS[ri][:, :], func=AF.Exp)
                    else:
                        for half in range(2):
                            c0 = half * 512 + offs[half]
                            c1 = (half + 1) * 512
                            P.I('act', 'activation', [('b_ss', ri)], [('b_P', ri)], out=PP[ri][:, c0:c1], in_=SS[ri][:, c0:c1], func=AF.Exp)
                    return (h, g, p, ri, offs)

                def pv(h, g, p, ri, offs):
                    s = h % 2
                    oi = goi(h, g)
                    nkb = 4 * g + 4
                    for half in range(2):
                        kb = 2 * p + half
                        off = offs[half]
                        c0 = half * 512
                        P.I('pe', 'matmul', [('b_V', s), ('b_P', ri)], [('b_po', oi)], out=PO[oi][0:65, off:512],
                            lhsT=V[s][:, kb, :], rhs=PP[ri][:, c0 + off:c0 + 512], start=(kb == 0), stop=(kb == nkb - 1))
                    if p == nkb // 2 - 1:
                        finalize(h, g)

                def finalize(h, g):
                    oi = goi(h, g)
                    fi = fin[0] % 2
                    fin[0] += 1
                    P.I('dve', 'tensor_copy', [('b_po', oi)], [('b_osb', fi)], out=osb[fi][:], in_=PO[oi][0:65, :])
                    P.I('dve', 'tensor_scalar', [('b_osb', fi)], [('b_rden0', fi)], out=rden[fi][64:65, :], in0=osb[fi][64:65, :],
                        scalar1=1e-35, scalar2=None, op0=ALU.max)
                    P.I('dve', 'reciprocal', [('b_rden0', fi)], [('b_rden', fi)], out=rden[fi][64:65, :], in_=rden[fi][64:65, :])

                    def part2():
                        P.I('pe', 'matmul', ['onesf', ('b_rden', fi)], [('b_po', oi)], out=PO[oi][0:64, 0:512], lhsT=onesf[64:65, 0:64],
                            rhs=rden[fi][64:65, :], start=True, stop=True)
                        P.I('dve', 'tensor_tensor', [('b_po', oi), ('b_osb', fi)], [('b_OT', fi)], out=OT[fi][:], in0=osb[fi][0:64, :],
                            in1=PO[oi][0:64, 0:512], op=ALU.mult)
                        P.dma('pool', OTd[h, :, g * 512:(g + 1) * 512], OT[fi][:], ('b_OTst', fi), [('b_OT', fi)], [])

                    deferred.append([DEFER, part2])

                def tick():
                    for d in deferred:
                        d[0] -= 1
                    while deferred and deferred[0][0] <= 0:
                        deferred.pop(0)[1]()

                loadhead(0)
                pend = []
                for h in range(16):
                    cnt = 0
                    for g in range(GF, NG):
                        for p in range((4 * g + 4) // 2):
                            pend.append(qk(h, g, p))
                            if len(pend) > LOOK:
                                pv(*pend.pop(0))
                            tick()
                            cnt += 1
                            if cnt == LOOK + 1 and h + 1 < 16:
                                loadhead(h + 1)
                while pend:
                    pv(*pend.pop(0))
                    tick()
                while deferred:
                    deferred.pop(0)[1]()
                P.barrier()
                P.emit()

        if 'C' in phases:
            with contextlib.ExitStack() as esC:
                ps = [esC.enter_context(nc.psum_tensor('psc%d' % i, [128, 512], F32)) for i in range(7)]
                pst = esC.enter_context(nc.psum_tensor('pstc', [128, 1024], BF16))
                xt = sb('c_xt', [128, 4, 1024], F32, esC)
                bAs = [sb('c_bA%d' % i, [128, 8, 512], BF16, esC) for i in range(2)]
                bB = sb('c_bB', [128, 8, 512], BF16, esC)
                bC = sb('c_bC', [128, 4, 1024], BF16, esC)
                bD = sb('c_bD', [128, 8, 512], BF16, esC)
                bE = sb('c_bE', [128, 8, 512], BF16, esC)
                actT = sb('c_act', [128, 22, 512], BF16, esC)
                wb_ = [sb('c_w%d' % i, [128, 8, 512], BF16, esC) for i in range(5)]
                tmp = [sb('c_tmp%d' % i, [128, 516], F32, esC) for i in range(12)]
                sa = sb('c_sa', [128, 8, 512], BF16, esC)
                sbb = sb('c_sb', [128, 8, 512], BF16, esC)
                S = sb('c_S', [128, 8, 128], F32, esC)
                S1 = sb('c_S1', [128, 8, 128], F32, esC)
                ktok = sb('c_ktok', [128, 8, 128], BF16, esC)
                At = sb('c_At', [128, 8, 128], BF16, esC)
                Sp = sb('c_Sp', [128, 8, 128], BF16, esC)
                m01x4 = sb('c_m01x4', [128, 4, 128], F32, esC)
                rmask = sb('c_rmask', [128, 512], F32, esC)
                dc = sb('c_dc', [128, 8, 6, 4], F32, esC)
                sq = [sb('c_sq%d' % i, [128, 512], BF16, esC) for i in range(4)]
                junk = sb('c_junk', [128, 1024], BF16, esC)
                ssq = sb('c_ss', [128, 4], F32, esC)
                std = sb('c_std', [128, 4], F32, esC)
                rstd = sb('c_rstd', [128, 4], F32, esC)
                carry = sb('c_carry', [128, 44, 2], F32, esC)

                P.I('dve', 'memset', [], [('c_S', h) for h in range(8)], ap=S[:], constant=0.0)
                for i in range(4):
                    P.I('dve', 'tensor_copy', ['m01'], ['m01x4'], out=m01x4[:, i, :], in_=m01[:])
                P.I('dve', 'memset', [], ['rmask'], ap=rmask[:], constant=1.0)
                P.I('dve', 'memset', ['rmask'], ['rmask'], ap=rmask[:].rearrange("p (a c) -> p a c", c=128)[:, :, 0:1], constant=0.0)
                P.I('pool', 'memset', [], [('c_carry', ct) for ct in range(44)], ap=carry[:], constant=0.0)

                sched_g = []
                for i in (0, 2, 4, 5, 1, 3, 6, 7):
                    sched_g.append(('win', i, 8))
                sched_g += [('win', 14, 8), ('win', 15, 8), ('win', 16, 8), ('win', 17, 8), ('wb', 0, 8), ('wb', 1, 8), ('wa', 0, 8), ('wa', 1, 8)]
                sched_g += [('wo', 0, 8), ('wo', 1, 8)]
                upseq = []
                for j in range(22):
                    for pc in (j // 4, (22 + j) // 4):
                        if pc not in upseq[-3:]:
                            upseq.append(pc)
                sched_g += [('wup', pc, 8) for pc in upseq]
                for half in range(2):
                    for kg in range(3):
                        sched_g.append(('wdn', kg * 2 + half, 8 if kg < 2 else 6))
                sched_p = [('win', 2, 8), ('win', 4, 8), ('win', 3, 8), ('win', 5, 8)]
                sched_h = [e for e in sched_g if e[0] != 'wdn']
                if NGP >= 1:
                    W = WStream(P, wb_, wsc, sched_p * GF + sched_h + sched_g * (NG - GF - 1), res=3)
                else:
                    W = WStream(P, wb_, wsc, sched_g * NG, res=3)
                bank = [0]

                def nb():
                    bank[0] = (bank[0] + 1) % 7
                    return bank[0]

                for g in range(NG):
                    gsl = slice(g * 512, (g + 1) * 512)
                    full = g >= GF
                    bA = bAs[g % 2]
                    bAk = ('c_bA', g % 2)
                    if g == 0:
                        P.dma('pool', bA[:], nTd[0].rearrange("p (k c) -> p k c", c=512), ('c_nT', 0), [], [bAk])
                    if g + 1 < NG:
                        P.dma('pool', bAs[(g + 1) % 2][:], nTd[g + 1].rearrange("p (k c) -> p k c", c=512), ('c_nT', (g + 1) % 2), [], [('c_bA', (g + 1) % 2)])
                    if full:
                        P.dma('pool', xt[:], x[gsl, :].rearrange("(b p) d -> p b d", p=128), 'c_x', [], ['c_xt'])
                    if full:
                        P.dma('pool', bB[:], OTd[:, :, gsl].rearrange("(j two) d t -> (two d) j t", two=2), 'c_oT', [], ['c_bB'])
                    jobs = []
                    for half in range(2):
                        for b in range(4):
                            def vjob(half=half, b=b):
                                wt, wk = W.get('win', 4 + half)
                                bi = nb()
                                for kc in range(8):
                                    P.I('pe', 'matmul', [wk, bAk], [('ps', bi)], out=ps[bi][:, :], lhsT=bA[:, kc, b * 128:(b + 1) * 128],
                                        rhs=wt[:, kc, :], start=(kc == 0), stop=(kc == 7))
                                if b % 2 == 0:
                                    P.I('dve', 'tensor_copy', [('ps', bi)], [('c_bC', b, half)], out=bC[:, b, half * 512:(half + 1) * 512], in_=ps[bi][:, :])
                                else:
                                    P.I('act', 'activation', [('ps', bi)], [('c_bC', b, half)], out=bC[:, b, half * 512:(half + 1) * 512], in_=ps[bi][:, :], func=AF.Copy)
                            jobs.append(vjob)
                    for half in range(2 if full else 0):
                        for hh in range(4):
                            def gjob(half=half, hh=hh):
                                wt, wk = W.get('win', 6 + half)
                                h = half * 4 + hh
                                bi = nb()
                                for kc in range(8):
                                    P.I('pe', 'matmul', [wk, bAk], [('ps', bi)], out=ps[bi][:, :], lhsT=wt[:, kc, hh * 128:(hh + 1) * 128],
                                        rhs=bA[:, kc, :], start=(kc == 0), stop=(kc == 7))
                                P.I('act', 'activation', [('ps', bi)], ['c_bD'], out=bD[:, h, :], in_=ps[bi][:, :], func=AF.Silu)
                            jobs.append(gjob)
                    gate_jobs = []
                    for gi, (pbase, dstt, dkey) in enumerate(((14, sa, 'c_sa'), (16, sbb, 'c_sb')) if full else ()):
                        for c in range(8):
                            def job(pbase=pbase, dstt=dstt, dkey=dkey, c=c):
                                wt, wk = W.get('win', pbase + c // 4)
                                bi = nb()
                                for kc in range(8):
                                    P.I('pe', 'matmul', [wk, bAk], [('ps', bi)], out=ps[bi][:, :], lhsT=wt[:, kc, (c % 4) * 128:(c % 4 + 1) * 128],
                                        rhs=bA[:, kc, :], start=(kc == 0), stop=(kc == 7))
                                P.I('act', 'activation', [('ps', bi)], [(dkey, c)], out=dstt[:, c, :], in_=ps[bi][:, :], func=AF.Sigmoid)
                            gate_jobs.append(job)
                    for _ in range(6 if full else 0):
                        jobs.append(gate_jobs.pop(0))
                    fillplan = {0: [2, 1, 1, 1, 1, 1, 1], 1: [2, 2, 2, 2, 2, 2, 2]} if full else {0: [1, 1, 1, 1, 0, 0, 0], 1: [1, 1, 1, 1, 0, 0, 0]}

                    def fill(hb, step):
                        for _ in range(fillplan[hb][step]):
                            if jobs:
                                jobs.pop(0)()

                    for hb in range(2):
                        hs = [hb * 4 + i for i in range(4)]
                        Tt = {h: [tmp[(h % 4) * 3 + i][:, 0:512] for i in range(3)] for h in hs}
                        Tk = {h: ['c_tmp%d' % ((h % 4) * 3 + i) for i in range(3)] for h in hs}
                        T3b = {h: tmp[(h % 4) * 3 + 2][:, 0:512].rearrange("p (a c) -> p a c", c=128) for h in hs}
                        for h in hs:
                            off = (h % 4) * 128
                            if full:
                                wq_, wqk = W.get('win', 0 + h // 4)
                                bq = nb()
                                for kc in range(8):
                                    P.I('pe', 'matmul', [wqk, bAk], [('ps', bq)], out=ps[bq][:, :], lhsT=wq_[:, kc, off:off + 128], rhs=bA[:, kc, :],
                                        start=(kc == 0), stop=(kc == 7))
                                P.I('act', 'activation', [('ps', bq)], [('c_act', h)], out=actT[:, h, :], in_=ps[bq][:, :], func=AF.Silu)
                            wf_, wfk = W.get('win', 2 + h // 4)
                            bf = nb()
                            for kc in range(8):
                                P.I('pe', 'matmul', [wfk, bAk], [('ps', bf)], out=ps[bf][:, :], lhsT=wf_[:, kc, off:off + 128], rhs=bA[:, kc, :],
                                    start=(kc == 0), stop=(kc == 7))
                            P.I('act', 'activation', [('ps', bf)], [Tk[h][0]], out=Tt[h][0], in_=ps[bf][:, :], func=AF.Sigmoid)
                        for h in hs:
                            P.I('dve', 'tensor_scalar', [Tk[h][0], 'oml', 'lb'], [Tk[h][0]], out=Tt[h][0], in0=Tt[h][0], scalar1=oml[:, h:h + 1],
                                scalar2=lb[:, h:h + 1], op0=ALU.mult, op1=ALU.add)
                        fill(hb, 0)
                        for h in hs:
                            P.I('act', 'activation', [Tk[h][0]], [Tk[h][1]], out=Tt[h][1], in_=Tt[h][0], func=AF.Ln)
                            P.I('dve', 'tensor_scalar', [Tk[h][0]], [('c_act', 8 + h)], out=actT[:, 8 + h, :], in0=Tt[h][0], scalar1=-1.0, scalar2=1.0,
                                op0=ALU.mult, op1=ALU.add)
                        fill(hb, 1)
                        for h in hs:
                            P.I('dve', 'tensor_tensor_scan', [Tk[h][1], 'rmask'], [Tk[h][2]], out=Tt[h][2], data0=rmask[:, :], data1=Tt[h][1], initial=0.0,
                                op0=ALU.mult, op1=ALU.add)
                        fill(hb, 2)
                        for h in hs:
                            dk = ('c_dc', h)
                            P.I('dve', 'tensor_copy', [Tk[h][2]], [(dk, 0)], out=dc[:, h, 0, :], in_=T3b[h][:, :, 63])
                            P.I('dve', 'tensor_copy', [Tk[h][2]], [(dk, 5)], out=dc[:, h, 5, :], in_=T3b[h][:, :, 127])
                            P.I('dve', 'tensor_tensor', [Tk[h][2], (dk, 0)], [Tk[h][2]], out=T3b[h], in0=T3b[h],
                                in1=dc[:, h, 0, :].unsqueeze(2).to_broadcast([128, 4, 128]), op=ALU.subtract)
                        fill(hb, 3)
                        for h in hs:
                            if full:
                                P.I('act', 'activation', [Tk[h][2]], [Tk[h][1]], out=Tt[h][1], in_=Tt[h][2], func=AF.Exp)
                            P.I('act', 'activation', [Tk[h][2]], [Tk[h][2]], out=Tt[h][2], in_=Tt[h][2], func=AF.Exp, scale=-1.0)
                        fill(hb, 4)
                        for h in hs:
                            if full:
                                P.I('dve', 'tensor_tensor', [('c_act', h), Tk[h][1]], [('c_act', h)], out=actT[:, h, :], in0=actT[:, h, :], in1=Tt[h][1], op=ALU.mult)
                            P.I('dve', 'tensor_tensor', [('c_act', 8 + h), Tk[h][2]], [('c_act', 8 + h)], out=actT[:, 8 + h, :], in0=actT[:, 8 + h, :], in1=Tt[h][2],
                                op=ALU.mult)
                        fill(hb, 5)
                        for h in hs:
                            dk = ('c_dc', h)
                            P.I('dve', 'tensor_tensor', [(dk, 5), (dk, 0)], [(dk, 3)], out=dc[:, h, 3, :], in0=dc[:, h, 5, :], in1=dc[:, h, 0, :], op=ALU.subtract)
                            if full:
                                P.I('act', 'activation', [(dk, 0)], [(dk, 1)], out=dc[:, h, 1, :], in_=dc[:, h, 0, :], func=AF.Exp)
                            P.I('act', 'activation', [(dk, 5)], [(dk, 2)], out=dc[:, h, 2, :], in_=dc[:, h, 5, :], func=AF.Exp)
                            P.I('act', 'activation', [(dk, 3)], [(dk, 4)], out=dc[:, h, 4, :], in_=dc[:, h, 3, :], func=AF.Exp)
                        fill(hb, 6)
                    while jobs:
                        jobs.pop(0)()
                    for b in range(4):
                        bs = slice(b * 128, (b + 1) * 128)
                        for h in range(8):
                            P.I('pe', 'transpose', [('c_act', 8 + h)], [('pst', 0)], out=pst[:, h * 128:(h + 1) * 128], in_=actT[:, 8 + h, bs], identity=identb[:])
                        P.I('act', 'activation', [('pst', 0)], ['c_ktok'], out=ktok[:].rearrange("p h c -> p (h c)"), in_=pst[:, :], func=AF.Copy)
                        for half in range(2 if full else 0):
                            bx = nb()
                            for hh in range(4):
                                h = half * 4 + hh
                                P.I('pe', 'matmul', [('c_act', 8 + h), ('c_act', h)], [('ps', bx)], out=ps[bx][:, hh * 128:(hh + 1) * 128],
                                    lhsT=actT[:, 8 + h, bs], rhs=actT[:, h, bs], start=True, stop=True)
                            P.I('dve', 'tensor_tensor', [('ps', bx), 'm01x4'], [('c_At', half)], out=At[:, half * 4:(half + 1) * 4, :],
                                in0=ps[bx][:, :].rearrange("p (h c) -> p h c", c=128), in1=m01x4[:], op=ALU.mult)
                        for h in range(8):
                            if full:
                                P.I('act', 'activation', [('c_S', h), (('c_dc', h), 1)], [('c_Sp', h)], out=Sp[:, h, :], in_=S[:, h, :], func=AF.Copy,
                                    scale=dc[:, h, 1, b:b + 1])
                            P.I('pool', 'tensor_scalar', [('c_S', h), (('c_dc', h), 2)], [('c_S1', h)], out=S1[:, h, :], in0=S[:, h, :],
                                scalar1=dc[:, h, 2, b:b + 1], scalar2=1.0, op0=ALU.mult, op1=ALU.mult)
                        for _ in range((3 if b < 2 else 2) if full else 0):
                            gate_jobs.pop(0)()
                        for half in range(2 if full else 0):
                            by = nb()
                            for hh in range(4):
                                h = half * 4 + hh
                                vv = bC[:, b, h * 128:(h + 1) * 128]
                                P.I('pe', 'matmul', [('c_bC', b, half), ('c_At', half)], [('ps', by)], out=ps[by][:, hh * 128:(hh + 1) * 128],
                                    lhsT=vv, rhs=At[:, h, :], start=True, stop=False)
                                P.I('pe', 'matmul', [('c_Sp', h), ('c_act', h)], [('ps', by)], out=ps[by][:, hh * 128:(hh + 1) * 128],
                                    lhsT=Sp[:, h, :], rhs=actT[:, h, bs], start=False, stop=True)
                            oute = bE[:, half * 4:(half + 1) * 4, bs]
                            ine = ps[by][:, :].rearrange("p (h c) -> p h c", c=128)
                            if half == 0:
                                P.I('act', 'activation', [('ps', by)], [('c_bEr', half, b)], out=oute, in_=ine, func=AF.Copy)
                            else:
                                P.I('dve', 'tensor_copy', [('ps', by)], [('c_bEr', half, b)], out=oute, in_=ine)
                        for half in range(2):
                            bz = nb()
                            for hh in range(4):
                                h = half * 4 + hh
                                vv = bC[:, b, h * 128:(h + 1) * 128]
                                P.I('pe', 'matmul', ['c_ktok', ('c_bC', b, half)], [('ps', bz)], out=ps[bz][:, hh * 128:(hh + 1) * 128],
                                    lhsT=ktok[:, h, :], rhs=vv, start=True, stop=True)
                            for hh in range(4):
                                h = half * 4 + hh
                                P.I('dve', 'scalar_tensor_tensor', [('ps', bz), (('c_dc', h), 4), ('c_S1', h)], [('c_S', h)], out=S[:, h, :],
                                    in0=ps[bz][:, hh * 128:(hh + 1) * 128], scalar=dc[:, h, 4, b:b + 1], in1=S1[:, h, :], op0=ALU.mult, op1=ALU.add)
                    if not full:
                        continue
                    def norm_batch(hb):
                        hs = [hb * 4 + i for i in range(4)]
                        ek = {h: [('c_bEr', h // 4, b) for b in range(4)] for h in hs}
                        bns = {}
                        for h in hs:
                            P.I('act', 'activation', ek[h], [('c_sq', h % 4)], out=sq[h % 4][:], in_=bE[:, h, :], func=AF.Square)
                        for h in hs:
                            bns[h] = nb()
                            P.I('pe', 'matmul', ['onesb', ('c_sq', h % 4)], [('ps', bns[h])], out=ps[bns[h]][:, :], lhsT=onesb[:], rhs=sq[h % 4][:], start=True, stop=True)
                        for h in hs:
                            P.I('act', 'activation', [('ps', bns[h])], ['c_tmp%d' % (h % 4)], out=tmp[h % 4][:, 0:512], in_=ps[bns[h]][:, :], func=AF.Ln,
                                scale=1.0 / 128.0, bias=epst[:, 0:1])
                        for h in hs:
                            P.I('act', 'activation', ['c_tmp%d' % (h % 4)], ['c_tmp%d' % (h % 4)], out=tmp[h % 4][:, 0:512], in_=tmp[h % 4][:, 0:512], func=AF.Exp,
                                scale=-0.5)
                        for h in hs:
                            P.I('dve', 'tensor_tensor', ek[h] + ['c_tmp%d' % (h % 4)], ['c_tmp%d' % (4 + h % 4)], out=tmp[4 + h % 4][:, 0:512], in0=bE[:, h, :],
                                in1=tmp[h % 4][:, 0:512], op=ALU.mult)
                        for h in hs:
                            P.I('dve', 'scalar_tensor_tensor', ['c_tmp%d' % (4 + h % 4), 'gnorm', 'c_bD'], [('c_bE', h)], out=bE[:, h, :], in0=tmp[4 + h % 4][:, 0:512],
                                scalar=gnorm[:, 0:1], in1=bD[:, h, :], op0=ALU.mult, op1=ALU.mult)
                    bEk = [('c_bE', h) for h in range(8)]
                    def wb_half(half):
                        wt, wk = W.get('wb', half)
                        for c in range(4):
                            cc = half * 4 + c
                            bi = nb()
                            for kc in range(8):
                                P.I('pe', 'matmul', [wk, 'c_bB'], [('ps', bi)], out=ps[bi][:, :], lhsT=wt[:, kc, c * 128:(c + 1) * 128], rhs=bB[:, kc, :],
                                    start=(kc == 0), stop=(kc == 7))
                            P.I('dve', 'tensor_tensor', [('ps', bi), ('c_sb', cc)], [('c_sb', cc)], out=sbb[:, cc, :], in0=ps[bi][:, :], in1=sbb[:, cc, :], op=ALU.mult)
                    wb_half(0)
                    norm_batch(0)
                    wb_half(1)
                    norm_batch(1)
                    for half in range(2):
                        wt, wk = W.get('wa', half)
                        for c in range(4):
                            cc = half * 4 + c
                            bi = nb()
                            for kc in range(8):
                                P.I('pe', 'matmul', [wk] + bEk, [('ps', bi)], out=ps[bi][:, :], lhsT=wt[:, kc, c * 128:(c + 1) * 128], rhs=bE[:, kc, :],
                                    start=(kc == 0), stop=(kc == 7))
                            P.I('dve', 'tensor_tensor', [('ps', bi), ('c_sa', cc)], [('c_sa', cc)], out=sa[:, cc, :], in0=ps[bi][:, :], in1=sa[:, cc, :], op=ALU.mult)
                            P.I('pool', 'tensor_tensor', [('c_sb', cc), ('c_sa', cc)], ['c_bD'], out=bD[:, cc, :], in0=sbb[:, cc, :], in1=sa[:, cc, :], op=ALU.add)
                    for half in range(2):
                        wt, wk = W.get('wo', half)
                        for b in range(4):
                            bi = nb()
                            for kc in range(8):
                                P.I('pe', 'matmul', [wk, 'c_bD'], [('ps', bi)], out=ps[bi][:, :], lhsT=bD[:, kc, b * 128:(b + 1) * 128], rhs=wt[:, kc, :],
                                    start=(kc == 0), stop=(kc == 7))
                            P.I('dve', 'tensor_tensor', [('ps', bi), 'c_xt'], ['c_xt'], out=xt[:, b, half * 512:(half + 1) * 512], in0=ps[bi][:, :],
                                in1=xt[:, b, half * 512:(half + 1) * 512], op=ALU.add)
                    rms_to_T('c_', xt, 'c_xt', junk, bC, 'c_bCn', ssq, std, rstd, bB, 'c_bBn', gffn)
                    n2keys = [('c_bBn', kc) for kc in range(8)] + ['c_bB']
                    if NGP >= 1 and g == GF:
                        for j in range(22):
                            for ct in (j, 22 + j):
                                wt, wk = W.get('wup', ct // 4)
                                off = (ct % 4) * 128
                                bi = nb()
                                for kc in range(8):
                                    P.I('pe', 'matmul', [wk] + n2keys, [('ps', bi)], out=ps[bi][:, 0:128], lhsT=wt[:, kc, off:off + 128], rhs=bB[:, kc, 384:512],
                                        start=(kc == 0), stop=(kc == 7))
                                P.I('dve', 'tensor_copy', [('ps', bi)], [('c_carry', ct)], out=carry[:, ct, :], in_=ps[bi][:, 126:128])
                        continue
                    for jb in range(0, 22, 4):
                        js = list(range(jb, min(jb + 4, 22)))
                        for j in js:
                            for which, ct in (('g', j), ('v', 22 + j)):
                                wt, wk = W.get('wup', ct // 4)
                                off = (ct % 4) * 128
                                bi = nb()
                                for kc in range(8):
                                    P.I('pe', 'matmul', [wk] + n2keys, [('ps', bi)], out=ps[bi][:, :], lhsT=wt[:, kc, off:off + 128], rhs=bB[:, kc, :],
                                        start=(kc == 0), stop=(kc == 7))
                                ai = (j % 4) + (0 if which == 'g' else 4)
                                ac, ak = tmp[ai], 'c_tmp%d' % ai
                                ck = ('c_carry', ct)
                                P.I('act', 'activation', [('ps', bi), 'cw', 'cb'], [ak], out=ac[:, 0:512], in_=ps[bi][:, :], func=AF.Identity,
                                    scale=cw[:, ct, 2:3], bias=cb[:, ct:ct + 1])
                                P.I('dve', 'scalar_tensor_tensor', [('ps', bi), ak, 'cw'], [ak], out=ac[:, 1:512], in0=ps[bi][:, 0:511], scalar=cw[:, ct, 1:2],
                                    in1=ac[:, 1:512], op0=ALU.mult, op1=ALU.add)
                                P.I('dve', 'scalar_tensor_tensor', [('ps', bi), ak, 'cw'], [ak], out=ac[:, 2:512], in0=ps[bi][:, 0:510], scalar=cw[:, ct, 0:1],
                                    in1=ac[:, 2:512], op0=ALU.mult, op1=ALU.add)
                                P.I('dve', 'scalar_tensor_tensor', [ck, ak, 'cw'], [ak], out=ac[:, 0:2], in0=carry[:, ct, 0:2], scalar=cw[:, ct, 0:1],
                                    in1=ac[:, 0:2], op0=ALU.mult, op1=ALU.add)
                                P.I('dve', 'scalar_tensor_tensor', [ck, ak, 'cw'], [ak], out=ac[:, 0:1], in0=carry[:, ct, 1:2], scalar=cw[:, ct, 1:2],
                                    in1=ac[:, 0:1], op0=ALU.mult, op1=ALU.add)
                                P.I('dve', 'tensor_copy', [('ps', bi)], [ck], out=carry[:, ct, :], in_=ps[bi][:, 510:512])
                        for j in js:
                            gk, vk = 'c_tmp%d' % (j % 4), 'c_tmp%d' % (4 + j % 4)
                            P.I('act', 'activation', [gk], [gk], out=tmp[j % 4][:, 0:512], in_=tmp[j % 4][:, 0:512], func=AF.Gelu)
                            P.I('pool', 'tensor_tensor', [gk, vk], [('c_act', j)], out=actT[:, j, :], in0=tmp[j % 4][:, 0:512], in1=tmp[4 + j % 4][:, 0:512], op=ALU.mult)
                    for half in range(2):
                        banks = [nb() for _ in range(4)]
                        for kg in range(3):
                            nk = 8 if kg < 2 else 6
                            wt, wk = W.get('wdn', kg * 2 + half)
                            for b in range(4):
                                for kc in range(nk):
                                    j = kg * 8 + kc
                                    P.I('pe', 'matmul', [wk, ('c_act', j)], [('ps', banks[b])], out=ps[banks[b]][:, :], lhsT=actT[:, j, b * 128:(b + 1) * 128],
                                        rhs=wt[:, kc, :], start=(j == 0), stop=(j == 21))
                        for b in range(4):
                            P.I('dve', 'tensor_tensor', [('ps', banks[b]), 'c_xt'], ['c_xt'], out=xt[:, b, half * 512:(half + 1) * 512], in0=ps[banks[b]][:, :],
                                in1=xt[:, b, half * 512:(half + 1) * 512], op=ALU.add)
                    for b in range(4):
                        P.I('act', 'activation', ['c_xt'], ['c_junk', ('c_ss', b)], out=junk[:], in_=xt[:, b, :], func=AF.Square, accum_out=ssq[:, b:b + 1])
                    P.I('act', 'activation', [('c_ss', b) for b in range(4)], ['c_std'], out=std[:], in_=ssq[:], func=AF.Sqrt, scale=1.0 / 1024.0,
                        bias=epst[:, 0:1])
                    P.I('dve', 'reciprocal', ['c_std'], ['c_rstd'], out=rstd[:], in_=std[:])
                    for b in range(4):
                        P.I('dve', 'scalar_tensor_tensor', ['c_xt', 'c_rstd', 'gfinb'], ['c_xt'], out=xt[:, b, :], in0=xt[:, b, :], scalar=rstd[:, b:b + 1],
                            in1=gfinb[:], op0=ALU.mult, op1=ALU.mult)
                    if g >= NGP:
                        P.dma('pool', y[(g - NGP) * 512:(g - NGP + 1) * 512, :].rearrange("(b p) d -> p b d", p=128), xt[:], 'c_yst', ['c_xt'], [])
                P.barrier()
                P.emit()
        ninstr = P.n
    return nc, ninstr


_CACHE = {}


def _host_consts():
    s = np.arange(128)[:, None]
    t = np.arange(128)[None, :]
    m01 = (s <= t).astype(np.float32)
    mneg = np.where(s <= t, 0.0, -30000.0).astype(np.float32)
    return np.eye(128, dtype=np.float32), mneg, m01


def make_in_maps(inputs):
    f = lambda a: np.ascontiguousarray(np.asarray(a, dtype=np.float32))
    x = f(inputs['x'])
    ident, mneg, m01 = _host_consts()
    shared = {
        'w_in': f(inputs['w_in'][0]), 'w_a': f(inputs['w_branch_a'][0]), 'w_b': f(inputs['w_branch_b'][0]),
        'w_o': f(inputs['w_out'][0]), 'w_up': f(inputs['w_up'][0]), 'w_dn': f(inputs['w_down'][0]),
        'gmix': f(np.asarray(inputs['norm_mix'])[0].reshape(8, 128).T),
        'gffn': f(np.asarray(inputs['norm_ffn'])[0].reshape(8, 128).T),
        'gfinb': f(np.broadcast_to(np.asarray(inputs['norm_final']).reshape(1, 1024), (128, 1024))),
        'ffb': f(np.asarray(inputs['fox_f_bias'])[0].reshape(16, 1)),
        'lbl': f(np.asarray(inputs['hg_lb_logits']).reshape(2, 8, 128).transpose(2, 0, 1)),
        'gnorm': f(np.asarray(inputs['hg_norm'])[0].reshape(128, 1)),
        'cw': f(np.asarray(inputs['conv_w'])[0].reshape(3, 44, 128).transpose(2, 1, 0)),
        'cb': f(np.asarray(inputs['conv_b'])[0].reshape(44, 128).T),
        'ident': ident, 'mneg': mneg, 'm01': m01,
    }
    maps = []
    T = x.shape[1]
    H = T // 2
    for c in range(8):
        m = dict(shared)
        b, half = c % 4, c // 4
        if half == 0:
            xl = np.zeros((T, 1024), np.float32)
            xl[H:] = x[b, :H]
            vl = np.zeros(T, np.float32)
            vl[H:] = 1.0
        else:
            xl = x[b]
            vl = np.ones(T, np.float32)
        m['x'] = np.ascontiguousarray(xl)
        m['valid'] = np.ascontiguousarray(vl.reshape(T // 128, 128).T)
        maps.append(m)
    return maps


def kernel(**inputs):
    if 'nc' not in _CACHE:
        _CACHE['nc'] = build()[0]
    nc = _CACHE['nc']
    in_maps = make_in_maps(inputs)
    res = run_bass_kernel_spmd(nc, in_maps, core_ids=list(range(8)))
    out = np.stack([np.concatenate([np.asarray(res.results[b]['y'], dtype=np.float32),
                                    np.asarray(res.results[4 + b]['y'], dtype=np.float32)], axis=0) for b in range(4)], axis=0)
    return out
```

```python
import contextlib
import numpy as np
import concourse.bass as bass
import concourse.mybir as mybir
from concourse.bass_utils import run_bass_kernel_spmd

F32 = mybir.dt.float32
BF16 = mybir.dt.bfloat16
AF = mybir.ActivationFunctionType
ALU = mybir.AluOpType
ENG = ('pe', 'act', 'dve', 'pool', 'sp')
EPS = 1e-6


class Prog:
    def __init__(self, nc, es):
        self.nc = nc
        self.es = es
        self.streams = {e: [] for e in ENG}
        self.sem = {}
        self.cnt = {}
        for e in ('pe', 'act', 'dve', 'pool'):
            self.sem['c_' + e] = es.enter_context(nc.semaphore('c_' + e))
            self.cnt['c_' + e] = 0
        self.waited = {e: {} for e in ENG}
        self.lastw = {}
        self.readers = {}
        self.n = 0
        self.pkeys = {}

    def _dsem(self, key, eng):
        key = (eng, key)
        if key not in self.pkeys:
            self.pkeys[key] = sum(1 for k in self.pkeys if k[0] == eng)
        sid = 'd%s%d' % (eng, self.pkeys[key])
        if sid not in self.sem:
            self.sem[sid] = self.es.enter_context(self.nc.semaphore(sid))
            self.cnt[sid] = 0
        return sid

    def _wait(self, eng, sid, val):
        if self.waited[eng].get(sid, 0) >= val:
            return
        self.waited[eng][sid] = val
        h = self.sem[sid]
        self.streams[eng].append(lambda e: e.wait_ge(h, val))

    def op(self, eng, fn, reads=(), writes=(), dma=None):
        own = 'c_' + eng
        deps = {}
        for k in reads:
            ev = self.lastw.get(k)
            if ev is not None:
                deps[ev[0]] = max(deps.get(ev[0], 0), ev[1])
        for k in writes:
            ev = self.lastw.get(k)
            if ev is not None:
                deps[ev[0]] = max(deps.get(ev[0], 0), ev[1])
            for sid, val in self.readers.get(k, {}).items():
                deps[sid] = max(deps.get(sid, 0), val)
        if eng == 'pe':
            deps.pop(own, None)
        for sid, val in deps.items():
            self._wait(eng, sid, val)
        if dma is None:
            sid = own
            self.cnt[sid] += 1
            inc = 1
        else:
            sid = self._dsem(dma, eng)
            self.cnt[sid] += 16
            inc = 16
        val = self.cnt[sid]
        h = self.sem[sid]
        self.streams[eng].append(lambda e: fn(e).then_inc(h, inc))
        for k in writes:
            self.lastw[k] = (sid, val)
            self.readers[k] = {}
        for k in reads:
            d = self.readers.setdefault(k, {})
            d[sid] = max(d.get(sid, 0), val)
        self.n += 1

    def I(self, eng, method, r, w, **kw):
        self.op(eng, lambda e: getattr(e, method)(**kw), r, w)

    def dma(self, eng, out, in_, key, r, w):
        self.op(eng, lambda e: e.dma_start(out=out, in_=in_), r, w, dma=key)

    def barrier(self):
        for e in ENG:
            for sid, c in self.cnt.items():
                if c > 0:
                    self._wait(e, sid, c)

    def emit(self):
        st = self.streams
        with self.nc.Block() as block:
            if st['pe']:
                @block.tensor
                def _(e):
                    for f in st['pe']:
                        f(e)
            if st['act']:
                @block.scalar
                def _(e):
                    for f in st['act']:
                        f(e)
            if st['dve']:
                @block.vector
                def _(e):
                    for f in st['dve']:
                        f(e)
            if st['pool']:
                @block.gpsimd
                def _(e):
                    for f in st['pool']:
                        f(e)
            if st['sp']:
                @block.sync
                def _(e):
                    for f in st['sp']:
                        f(e)
        self.streams = {e: [] for e in ENG}
        self.pkeys = {}


class WStream:
    def __init__(self, P, bufs, wsc, schedule, res=3):
        self.P = P
        self.bufs = bufs
        self.wsc = wsc
        self.sched = schedule
        self.res = res
        self.issued = 0
        self.ptr = -1

    def _issue(self, i):
        name, idx, nk = self.sched[i]
        b = i % len(self.bufs)
        src = self.wsc[name][idx].rearrange("p (k c) -> p k c", c=512)[:, 0:nk, :]
        self.P.dma('sp', self.bufs[b][:, 0:nk, :], src, ('w', b), [], [('w', b)])

    def get(self, name, idx):
        for back in range(0, self.res):
            i = self.ptr - back
            if i >= 0 and self.sched[i][0] == name and self.sched[i][1] == idx:
                return self.bufs[i % len(self.bufs)], ('w', i % len(self.bufs))
        self.ptr += 1
        assert self.sched[self.ptr][0] == name and self.sched[self.ptr][1] == idx, (self.sched[self.ptr], name, idx)
        look = len(self.bufs) - self.res
        while self.issued < len(self.sched) and self.issued <= self.ptr + look:
            self._issue(self.issued)
            self.issued += 1
        i = self.ptr
        return self.bufs[i % len(self.bufs)], ('w', i % len(self.bufs))


def build(T=8192, NGP=8, debug=False, phases=('0', 'A', 'B', 'C')):
    nc = bass.Bass("TRN2", target_bir_lowering=False)
    NG = T // 512
    GF = max(NGP - 1, 0)
    TO = (NG - NGP) * 512
    NBLK = T // 128

    def din(name, shape, dt=F32):
        return nc.dram_tensor(name, list(shape), dt, kind="ExternalInput").ap()

    x = din('x', [T, 1024])
    w_in = din('w_in', [1024, 9232])
    w_a = din('w_a', [1024, 1024])
    w_b = din('w_b', [1024, 1024])
    w_o = din('w_o', [1024, 1024])
    w_up = din('w_up', [1024, 5632])
    w_dn = din('w_dn', [2816, 1024])
    gmix_d = din('gmix', [128, 8])
    gffn_d = din('gffn', [128, 8])
    gfinb_d = din('gfinb', [128, 1024])
    ffb_d = din('ffb', [16, 1])
    lbl_d = din('lbl', [128, 2, 8])
    gnorm_d = din('gnorm', [128, 1])
    cw_d = din('cw', [128, 44, 3])
    cb_d = din('cb', [128, 44])
    ident_d = din('ident', [128, 128])
    mneg_d = din('mneg', [128, 128])
    m01_d = din('m01', [128, 128])
    valid_d = din('valid', [128, T // 128])
    y = nc.dram_tensor('y', [TO, 1024], F32, kind='ExternalOutput').ap()

    skind = 'ExternalOutput' if debug else 'Internal'

    def dscr(name, shape, dt=BF16):
        return nc.dram_tensor(name, list(shape), dt, kind=skind).ap()

    wsc = {
        'win': dscr('s_win', [18, 128, 4096]),
        'wa': dscr('s_wa', [2, 128, 4096]),
        'wb': dscr('s_wb', [2, 128, 4096]),
        'wo': dscr('s_wo', [2, 128, 4096]),
        'wup': dscr('s_wup', [11, 128, 4096]),
        'wdn': dscr('s_wdn', [6, 128, 4096]),
    }
    nTd = dscr('s_nT', [NG, 128, 4096])
    QTd = dscr('s_QT', [16, 69, T])
    KTd = dscr('s_KT', [16, 69, T])
    Vd = dscr('s_V', [16, 128, NBLK, 65])
    OTd = dscr('s_OT', [16, 64, T])

    with contextlib.ExitStack() as es:
        P = Prog(nc, es)

        def sb(name, shape, dt, stack=es):
            return stack.enter_context(nc.sbuf_tensor('t_' + name, list(shape), dt))

        psA = contextlib.ExitStack()
        ps = [psA.enter_context(nc.psum_tensor('ps%d' % i, [128, 512], F32)) for i in range(6)]
        pst = psA.enter_context(nc.psum_tensor('pst', [128, 1024], BF16))
        pst2 = psA.enter_context(nc.psum_tensor('pst2', [128, 1024], BF16))

        identf = sb('identf', [128, 128], F32)
        identb = sb('identb', [128, 128], BF16)
        mnegb = sb('mnegb', [128, 128], BF16)
        m01 = sb('m01', [128, 128], F32)
        onesf = sb('onesf', [128, 512], F32)
        onesb = sb('onesb', [128, 128], BF16)
        gmix = sb('gmix', [128, 8], F32)
        gffn = sb('gffn', [128, 8], F32)
        gfinb = sb('gfinb', [128, 1024], F32)
        ffb = sb('ffb', [16, 1], F32)
        negfb = sb('negfb', [16, 1], F32)
        lbl = sb('lbl', [128, 2, 8], F32)
        lbd = sb('lbd', [128, 8], F32)
        lb = sb('lb', [128, 8], F32)
        oml = sb('oml', [128, 8], F32)
        gnorm = sb('gnorm', [128, 1], F32)
        cw = sb('cw', [128, 44, 3], F32)
        cb = sb('cb', [128, 44], F32)
        epst = sb('epst', [128, 1], F32)
        wff32 = sb('wff32', [128, 8, 16], F32)
        wffb = sb('wffb', [128, 8, 16], BF16)
        valid = sb('valid', [128, T // 128], F32)

        bg_jobs = []
        cl = [('identf', identf, ident_d), ('mneg32', None, None), ('m01', m01, m01_d), ('gmix', gmix, gmix_d),
              ('gffn', gffn, gffn_d), ('gfinb', gfinb, gfinb_d), ('ffb', ffb, ffb_d), ('lbl', lbl, lbl_d),
              ('gnorm', gnorm, gnorm_d), ('cw', cw, cw_d), ('cb', cb, cb_d), ('valid', valid, valid_d)]
        with contextlib.ExitStack() as es0:
            mneg32 = sb('mneg32', [128, 128], F32, es0)
            for name, t, d in cl:
                if t is None:
                    t, d = mneg32, mneg_d
                P.dma('sp', t[:], d, 'c', [], [name])
            P.dma('sp', wff32[:], w_in.rearrange("(kc p) c -> p kc c", p=128)[:, :, 7168:7184], 'c', [], ['wff32'])
            P.barrier()
            P.I('dve', 'tensor_copy', ['identf'], ['identb'], out=identb[:], in_=identf[:])
            P.I('dve', 'tensor_copy', ['mneg32'], ['mnegb'], out=mnegb[:], in_=mneg32[:])
            P.I('dve', 'tensor_copy', ['wff32'], ['wffb'], out=wffb[:], in_=wff32[:])
            P.I('dve', 'memset', [], ['onesf'], ap=onesf[:], constant=1.0)
            P.I('dve', 'memset', [], ['onesb'], ap=onesb[:], constant=1.0)
            P.I('dve', 'memset', [], ['epst'], ap=epst[:], constant=EPS)
            P.I('dve', 'tensor_scalar', ['ffb'], ['negfb'], out=negfb[:], in0=ffb[:], scalar1=-1.0, scalar2=None, op0=ALU.mult)
            P.I('dve', 'tensor_tensor', ['lbl'], ['lbd'], out=lbd[:], in0=lbl[:, 0, :], in1=lbl[:, 1, :], op=ALU.subtract)
            P.I('act', 'activation', ['lbd'], ['lb'], out=lb[:], in_=lbd[:], func=AF.Sigmoid)
            P.I('dve', 'tensor_scalar', ['lb'], ['oml'], out=oml[:], in0=lb[:], scalar1=-1.0, scalar2=1.0, op0=ALU.mult, op1=ALU.add)

            if '0' in phases:
                st32 = [sb('st32_%d' % i, [128, 8, 512], F32, es0) for i in range(2)]
                stb = [sb('stb_%d' % i, [128, 8, 512], BF16, es0) for i in range(2)]
                pieces = []
                win_r = w_in.rearrange("(kc p) c -> p kc c", p=128)
                for i in range(14):
                    pieces.append(('win', i, win_r[:, :, i * 512:(i + 1) * 512], 8))
                for i in range(4):
                    pieces.append(('win', 14 + i, win_r[:, :, 7184 + i * 512:7184 + (i + 1) * 512], 8))
                for nm, ap in (('wa', w_a), ('wb', w_b), ('wo', w_o)):
                    r = ap.rearrange("(kc p) c -> p kc c", p=128)
                    for i in range(2):
                        pieces.append((nm, i, r[:, :, i * 512:(i + 1) * 512], 8))
                r = w_up.rearrange("(kc p) c -> p kc c", p=128)
                for i in range(11):
                    pieces.append(('wup', i, r[:, :, i * 512:(i + 1) * 512], 8))
                r = w_dn.rearrange("(kc p) c -> p kc c", p=128)
                for kg in range(3):
                    nk = 8 if kg < 2 else 6
                    for half in range(2):
                        pieces.append(('wdn', kg * 2 + half, r[:, kg * 8:kg * 8 + nk, half * 512:(half + 1) * 512], nk))
                ceng = ('dve', 'pool', 'act')
                bg_pieces = [p_ for p_ in pieces if not (p_[0] == 'win' and 8 <= p_[1] <= 13)]
                pieces = []
                for nm, idx, src, nk in bg_pieces:
                    hk = nk // 2
                    for kh in range(2):
                        bg_jobs.append((nm, idx, src[:, kh * hk:(kh + 1) * hk, :], hk, kh * hk))
                for i, (nm, idx, src, nk) in enumerate(pieces):
                    s = i % 2
                    P.dma('sp', st32[s][:, 0:nk, :], src, ('st32', s), [], [('st32', s)])
                    ce = ceng[i % 3]
                    if ce == 'act':
                        P.I('act', 'activation', [('st32', s)], [('stb', s)], out=stb[s][:, 0:nk, :], in_=st32[s][:, 0:nk, :], func=AF.Copy)
                    else:
                        P.I(ce, 'tensor_copy', [('st32', s)], [('stb', s)], out=stb[s][:, 0:nk, :], in_=st32[s][:, 0:nk, :])
                    dst = wsc[nm][idx].rearrange("p (k c) -> p k c", c=512)[:, 0:nk, :]
                    P.dma('pool', dst, stb[s][:, 0:nk, :], ('stb', s), [('stb', s)], [])
            P.barrier()
            P.emit()

        def rms_N(pfx, xt, xkey, junk, xn, xnkey, ssq, std, rstd):
            for b in range(4):
                P.I('act', 'activation', [xkey], [pfx + 'junk', (pfx + 'ss', b)], out=junk[:], in_=xt[:, b, :], func=AF.Square,
                    accum_out=ssq[:, b:b + 1])
            P.I('act', 'activation', [(pfx + 'ss', b) for b in range(4)], [pfx + 'std'], out=std[:], in_=ssq[:], func=AF.Sqrt,
                scale=1.0 / 1024.0, bias=epst[:, 0:1])
            P.I('dve', 'reciprocal', [pfx + 'std'], [pfx + 'rstd'], out=rstd[:], in_=std[:])
            for b in range(4):
                if b % 2 == 0:
                    P.I('dve', 'tensor_scalar', [xkey, pfx + 'rstd'], [(xnkey, b)], out=xn[:, b, :], in0=xt[:, b, :],
                        scalar1=rstd[:, b:b + 1], scalar2=None, op0=ALU.mult)
                else:
                    P.I('act', 'activation', [xkey, pfx + 'rstd'], [(xnkey, b)], out=xn[:, b, :], in_=xt[:, b, :],
                        func=AF.Copy, scale=rstd[:, b:b + 1])

        def rms_T(xn, xnkey, nT, nTkey, gcols, psts):
            for kc in range(8):
                pt, pk = psts[kc % len(psts)]
                for b in range(4):
                    P.I('pe', 'transpose', [(xnkey, b)], [pk], out=pt[:, b * 128:(b + 1) * 128],
                        in_=xn[:, b, kc * 128:(kc + 1) * 128], identity=identb[:])
                P.I('act', 'activation', [pk], [(nTkey, kc)], out=nT[:, kc, :], in_=pt[:, 0:512],
                    func=AF.Copy, scale=gcols[:, kc:kc + 1])

        def rms_to_T(pfx, xt, xkey, junk, xn, xnkey, ssq, std, rstd, nT, nTkey, gcols):
            rms_N(pfx, xt, xkey, junk, xn, xnkey, ssq, std, rstd)
            rms_T(xn, xnkey, nT, nTkey, gcols, [(pst, ('pst', 0))])

        if 'A' in phases:
            with contextlib.ExitStack() as esA:
                xt1 = sb('a_xt', [128, 4, 1024], F32, esA)
                st32h = [sb('a_st32_%d' % i, [128, 4, 512], F32, esA) for i in range(2)]
                stbh = [sb('a_stb_%d' % i, [128, 4, 512], BF16, esA) for i in range(2)]
                bgc = [0]

                def bg_convert(n):
                    for _ in range(n):
                        if not bg_jobs:
                            return
                        nm, idx, src, hk, k0 = bg_jobs.pop(0)
                        sl = bgc[0] % 2
                        bgc[0] += 1
                        P.dma('sp', st32h[sl][:, 0:hk, :], src, ('a_st32', sl), [], [('a_st32', sl)])
                        P.I('pool', 'tensor_copy', [('a_st32', sl)], [('a_stb', sl)], out=stbh[sl][:, 0:hk, :], in_=st32h[sl][:, 0:hk, :])
                        dst = wsc[nm][idx].rearrange("p (k c) -> p k c", c=512)[:, k0:k0 + hk, :]
                        P.dma('pool', dst, stbh[sl][:, 0:hk, :], ('a_stbst', sl), [('a_stb', sl)], [])

                junk = sb('a_junk', [128, 1024], BF16, esA)
                xn = sb('a_xn', [128, 4, 1024], BF16, esA)
                nT = [sb('a_nT%d' % i, [128, 8, 512], BF16, esA) for i in range(2)]
                wres = [sb('a_w%d' % i, [128, 8, 512], BF16, esA) for i in range(6)]
                QTall = sb('a_QT', [64, 16, 512], BF16, esA)
                KTall = sb('a_KT', [64, 16, 512], BF16, esA)
                Vt = [sb('a_Vt%d' % i, [128, 16, 4, 65], BF16, esA) for i in range(2)]
                ssq = sb('a_ss', [128, 4], F32, esA)
                std = sb('a_std', [128, 4], F32, esA)
                rstd = sb('a_rstd', [128, 4], F32, esA)
                ffe = sb('a_ffe', [16, 512], F32, esA)
                ffl = sb('a_ffl', [16, 512], F32, esA)
                cg = [sb('a_cg%d' % i, [16, 4, 128], F32, esA) for i in range(2)]
                t32a = sb('a_t32a', [16, 512], F32, esA)
                t32b = sb('a_t32b', [16, 512], F32, esA)
                t32c = sb('a_t32c', [16, 512], F32, esA)
                afull = sb('a_afull', [16, 512], F32, esA)
                afr = sb('a_afr', [16, 512], F32, esA)
                kaug = [sb('a_kaug%d' % i, [16, 3, 512], BF16, esA) for i in range(2)]
                qaug = [sb('a_qaug%d' % i, [16, 2, 512], BF16, esA) for i in range(2)]
                ones3 = sb('a_ones3', [16, 3, 512], BF16, esA)

                P.I('dve', 'memset', [], ['a_ones3'], ap=ones3[:], constant=1.0)
                for s in range(2):
                    P.I('pool', 'memset', [], [('a_Vt', s, b_, h_) for b_ in range(4) for h_ in range(2)], ap=Vt[s][:], constant=1.0)
                P.dma('sp', xt1[:], x[0:512, :].rearrange("(b p) d -> p b d", p=128), 'a_x', [], ['a_x'])
                bank = [0]

                def nb():
                    bank[0] = (bank[0] + 1) % 6
                    return bank[0]

                apst = [(pst, ('pst', 0)), (pst2, ('pst', 1))]

                def normN(g):
                    rms_N('a_', xt1, 'a_x', junk, xn, 'a_xn', ssq, std, rstd)
                    if g + 1 < NG:
                        P.dma('sp', xt1[:], x[(g + 1) * 512:(g + 2) * 512, :].rearrange("(b p) d -> p b d", p=128), 'a_x', [], ['a_x'])

                def normT(g):
                    s_ = g % 2
                    rms_T(xn, 'a_xn', nT[s_], ('a_nT', s_), gmix, apst)
                    P.dma('pool', nTd[g].rearrange("p (k c) -> p k c", c=512), nT[s_][:], ('a_nTst', s_), [(('a_nT', s_), kc) for kc in range(8)], [])

                win_rA = w_in.rearrange("(kc p) c -> p kc c", p=128)
                cengA = ('dve', 'act')
                for kh in range(2):
                    P.dma('sp', st32h[kh][:], win_rA[:, kh * 4:(kh + 1) * 4, 10 * 512:11 * 512], ('a_st32', kh), [], [('a_st32', kh)])
                normN(0)
                normT(0)
                for i in (2, 3, 4, 5, 0, 1):
                    for kh in range(2):
                        sl = bgc[0] % 2
                        bgc[0] += 1
                        if not (i == 2):
                            P.dma('sp', st32h[sl][:], win_rA[:, kh * 4:(kh + 1) * 4, (8 + i) * 512:(9 + i) * 512], ('a_st32', sl), [], [('a_st32', sl)])
                        ce = cengA[bgc[0] % 2]
                        if ce == 'act':
                            P.I('act', 'activation', [('a_st32', sl)], [('aw', i)], out=wres[i][:, kh * 4:(kh + 1) * 4, :], in_=st32h[sl][:], func=AF.Copy)
                        else:
                            P.I(ce, 'tensor_copy', [('a_st32', sl)], [('aw', i)], out=wres[i][:, kh * 4:(kh + 1) * 4, :], in_=st32h[sl][:])
                for g in range(NG):
                    s = g % 2
                    gsl = slice(g * 512, (g + 1) * 512)
                    bg_convert(5)
                    nkeys = [(('a_nT', s), kc) for kc in range(8)]
                    for which, wbase, dst, dkey, scl in (('q', 0, QTall, 'a_QT', 0.125), ('k', 2, KTall, 'a_KT', 1.0)):
                        if which == 'q' and g < GF:
                            continue
                        for hp in range(8):
                            wi = wbase + hp // 4
                            off = (hp % 4) * 128
                            bi = nb()
                            for kc in range(8):
                                P.I('pe', 'matmul', [('aw', wi), (('a_nT', s), kc)], [('ps', bi)], out=ps[bi][:, :],
                                    lhsT=wres[wi][:, kc, off:off + 128], rhs=nT[s][:, kc, :], start=(kc == 0), stop=(kc == 7))
                            P.I('dve', 'tensor_scalar', [('ps', bi)], [(dkey, 2 * hp)], out=dst[0:64, 2 * hp, :], in0=ps[bi][0:64, :],
                                scalar1=scl, scalar2=None, op0=ALU.mult)
                            P.I('act', 'activation', [('ps', bi)], [(dkey, 2 * hp + 1)], out=dst[0:64, 2 * hp + 1, :],
                                in_=ps[bi][64:128, :], func=AF.Copy, scale=scl)
                    if g + 1 < NG:
                        normN(g + 1)
                    for b in range(4):
                        for half in range(2):
                            bi = nb()
                            for kc in range(8):
                                P.I('pe', 'matmul', [('aw', 4 + half), (('a_nT', s), kc)], [('ps', bi)], out=ps[bi][:, :],
                                    lhsT=nT[s][:, kc, b * 128:(b + 1) * 128], rhs=wres[4 + half][:, kc, :], start=(kc == 0), stop=(kc == 7))
                            e = 'dve' if half == 0 else 'act'
                            outv = Vt[s][:, half * 8:(half + 1) * 8, b, 0:64]
                            inv = ps[bi][:, :].rearrange("p (h d) -> p h d", d=64)
                            if e == 'dve':
                                P.I('dve', 'tensor_copy', [('ps', bi)], [('a_Vt', s, b, half)], out=outv, in_=inv)
                            else:
                                P.I('act', 'activation', [('ps', bi)], [('a_Vt', s, b, half)], out=outv, in_=inv, func=AF.Copy)
                    for b in range(4):
                        P.I('dve', 'tensor_scalar', ['valid', 'onesf'], [('a_Vt', s, b, 0), ('a_Vt', s, b, 1)], out=Vt[s][:, :, b, 64:65],
                            in0=onesf[:, 0:16].rearrange("p (h o) -> p h o", o=1), scalar1=valid[:, g * 4 + b:g * 4 + b + 1], scalar2=None, op0=ALU.mult)
                    bff = nb()
                    for kc in range(8):
                        P.I('pe', 'matmul', ['wffb', (('a_nT', s), kc)], [('ps', bff)], out=ps[bff][0:16, :], lhsT=wffb[:, kc, :],
                            rhs=nT[s][:, kc, :], start=(kc == 0), stop=(kc == 7))
                    P.I('act', 'activation', [('ps', bff), 'negfb'], ['a_ffe'], out=ffe[:], in_=ps[bff][0:16, :], func=AF.Exp,
                        scale=-1.0, bias=negfb[:, 0:1])
                    P.I('act', 'activation', ['a_ffe'], ['a_ffl'], out=ffl[:], in_=ffe[:], func=AF.Ln, scale=1.0, bias=onesf[0:16, 0:1])
                    cgf = cg[s][:].rearrange("p a b -> p (a b)")
                    init = 0.0 if g == 0 else cg[1 - s][:, 3, 127:128]
                    P.I('dve', 'tensor_tensor_scan', ['a_ffl', ('a_cg', 1 - s), 'onesf'], [('a_cg', s)], out=cgf, data0=onesf[0:16, :],
                        data1=ffl[:], initial=init, op0=ALU.mult, op1=ALU.subtract)
                    P.I('dve', 'tensor_scalar', [('a_cg', s)], ['a_t32a'], out=t32a[:], in0=cgf, scalar1=-1.0, scalar2=None, op0=ALU.mult)
                    P.I('dve', 'tensor_copy', ['a_t32a'], [('a_kaug0', s)], out=kaug[s][:, 0, :], in_=t32a[:])
                    P.I('dve', 'tensor_tensor', ['a_t32a', ('a_kaug0', s)], ['a_t32b'], out=t32b[:], in0=t32a[:], in1=kaug[s][:, 0, :], op=ALU.subtract)
                    P.I('dve', 'tensor_copy', ['a_t32b'], [('a_kaug1', s)], out=kaug[s][:, 1, :], in_=t32b[:])
                    P.I('dve', 'tensor_tensor', ['a_t32b', ('a_kaug1', s)], ['a_t32c'], out=t32c[:], in0=t32b[:], in1=kaug[s][:, 1, :], op=ALU.subtract)
                    P.I('dve', 'tensor_copy', ['a_t32c'], [('a_kaug2', s)], out=kaug[s][:, 2, :], in_=t32c[:])
                    for b in range(4 if g >= GF else 0):
                        P.I('dve', 'tensor_scalar', [('a_cg', s), 'onesf'], [('a_afull', b)], out=afull[:, b * 128:(b + 1) * 128],
                            in0=onesf[0:16, 0:128], scalar1=cg[s][:, b, 63:64], scalar2=None, op0=ALU.mult)
                    afk = [('a_afull', b) for b in range(4)]
                    if g >= GF:
                        P.I('dve', 'tensor_copy', afk, [('a_qaug0', s)], out=qaug[s][:, 0, :], in_=afull[:])
                        P.I('dve', 'tensor_tensor', afk + [('a_qaug0', s)], ['a_afr'], out=afr[:], in0=afull[:], in1=qaug[s][:, 0, :], op=ALU.subtract)
                        P.I('dve', 'tensor_copy', ['a_afr'], [('a_qaug1', s)], out=qaug[s][:, 1, :], in_=afr[:])
                        P.dma('pool', QTd[:, 0:64, gsl].rearrange("h r t -> r h t"), QTall[:], 'a_QTst', [('a_QT', h) for h in range(16)], [])
                        P.dma('pool', QTd[:, 64:66, gsl], qaug[s][:], ('a_qaugst', s), [('a_qaug0', s), ('a_qaug1', s)], [])
                        P.dma('pool', QTd[:, 66:69, gsl], ones3[:], 'a_o2st', ['a_ones3'], [])
                    P.dma('pool', KTd[:, 0:64, gsl].rearrange("h r t -> r h t"), KTall[:], 'a_KTst', [('a_KT', h) for h in range(16)], [])
                    P.dma('pool', KTd[:, 64:66, gsl], ones3[:, 0:2, :], 'a_o1st', ['a_ones3'], [])
                    P.dma('pool', KTd[:, 66:69, gsl], kaug[s][:], ('a_kaugst', s), [('a_kaug0', s), ('a_kaug1', s), ('a_kaug2', s)], [])
                    if g + 1 < NG:
                        normT(g + 1)
                    for hh in range(2):
                        P.dma('pool', Vd[hh * 8:(hh + 1) * 8, :, g * 4:(g + 1) * 4, :].rearrange("h p b d -> p h (b d)"),
                              Vt[s][:, hh * 8:(hh + 1) * 8, :, :].rearrange("p h b d -> p h (b d)"), ('a_Vst', s), [('a_Vt', s, b_, h_) for b_ in range(4) for h_ in range(2)], [])
                bg_convert(len(bg_jobs))
                P.barrier()
                P.emit()

        psA.close()
        if 'B' in phases:
            with contextlib.ExitStack() as esB:
                SS = [esB.enter_context(nc.psum_tensor('b_SS%d' % i, [128, 1024], F32)) for i in range(3)]
                PO = [esB.enter_context(nc.psum_tensor('b_PO%d' % i, [128, 512], F32)) for i in range(2)]
                KT = [sb('b_KT%d' % i, [69, T], BF16, esB) for i in range(2)]
                QT = [sb('b_QT%d' % i, [69, T], BF16, esB) for i in range(2)]
                V = [sb('b_V%d' % i, [128, NBLK, 65], BF16, esB) for i in range(2)]
                PP = [sb('b_P%d' % i, [128, 1024], BF16, esB) for i in range(3)]
                osb = [sb('b_osb%d' % i, [65, 512], F32, esB) for i in range(2)]
                rden = [sb('b_rden%d' % i, [65, 512], F32, esB) for i in range(2)]
                OT = [sb('b_OT%d' % i, [64, 512], BF16, esB) for i in range(2)]
                rot = [0]
                fin = [0]
                deferred = []
                LOOK = 2

                def loadhead(h):
                    s = h % 2
                    P.dma('sp', KT[s][:], KTd[h], ('b_KT', s), [], [('b_KT', s)])
                    P.dma('sp', QT[s][:, GF * 512:], QTd[h, :, GF * 512:], ('b_QT', s), [], [('b_QT', s)])
                    P.dma('sp', V[s][:], Vd[h], ('b_V', s), [], [('b_V', s)])

                gcount = {}

                def goi(h, g):
                    return (h * (NG - GF) + (g - GF)) % 2

                DEFER = max(1, min(8, 2 * GF + 1))

                def qk(h, g, p):
                    s = h % 2
                    ri = rot[0] % 3
                    rot[0] += 1
                    offs = []
                    halo = (NGP >= 1 and g == GF and p >= 1)
                    for half in range(2):
                        kb = 2 * p + half
                        j = kb - 4 * g
                        off = 384 if halo else max(0, j) * 128
                        need_mask = (j == 3) if halo else (j >= 0)
                        offs.append(off)
                        c0 = half * 512
                        P.I('pe', 'matmul', [('b_KT', s), ('b_QT', s)], [('b_ss', ri)], out=SS[ri][:, c0 + off:c0 + 512],
                            lhsT=KT[s][:, kb * 128:(kb + 1) * 128], rhs=QT[s][:, g * 512 + off:(g + 1) * 512], start=True, stop=(not need_mask))
                        if need_mask:
                            P.I('pe', 'matmul', ['identb', 'mnegb'], [('b_ss', ri)], out=SS[ri][:, c0 + off:c0 + off + 128], lhsT=identb[:],
                                rhs=mnegb[:], start=False, stop=True)
                    if halo:
                        P.I('act', 'activation', [('b_ss', ri)], [('b_P', ri)], out=PP[ri][:, :].rearrange("p (h c) -> p h c", c=512)[:, :, 384:512],
                            in_=SS[ri][:, :].rearrange("p (h c) -> p h c", c=512)[:, :, 384:512], func=AF.Exp)
                    elif 2 * p + 1 < 4 * g:
                        P.I('act', 'activation', [('b_ss', ri)], [('b_P', ri)], out=PP[ri][:, :], in_=SS[ri][:, :], func=AF.Exp)
                    else:
                        for half in range(2):
                            c0 = half * 512 + offs[half]
                            c1 = (half + 1) * 512
                            P.I('act', 'activation', [('b_ss', ri)], [('b_P', ri)], out=PP[ri][:, c0:c1], in_=SS[ri][:, c0:c1], func=AF.Exp)
                    return (h, g, p, ri, offs)

                def pv(h, g, p, ri, offs):
                    s = h % 2
                    oi = goi(h, g)
                    nkb = 4 * g + 4
                    for half in range(2):
                        kb = 2 * p + half
                        off = offs[half]
                        c0 = half * 512
                        P.I('pe', 'matmul', [('b_V', s), ('b_P', ri)], [('b_po', oi)], out=PO[oi][0:65, off:512],
                            lhsT=V[s][:, kb, :], rhs=PP[ri][:, c0 + off:c0 + 512], start=(kb == 0), stop=(kb == nkb - 1))
                    if p == nkb // 2 - 1:
                        finalize(h, g)

                def finalize(h, g):
                    oi = goi(h, g)
                    fi = fin[0] % 2
                    fin[0] += 1
                    P.I('dve', 'tensor_copy', [('b_po', oi)], [('b_osb', fi)], out=osb[fi][:], in_=PO[oi][0:65, :])
                    P.I('dve', 'tensor_scalar', [('b_osb', fi)], [('b_rden0', fi)], out=rden[fi][64:65, :], in0=osb[fi][64:65, :],
                        scalar1=1e-35, scalar2=None, op0=ALU.max)
                    P.I('dve', 'reciprocal', [('b_rden0', fi)], [('b_rden', fi)], out=rden[fi][64:65, :], in_=rden[fi][64:65, :])

                    def part2():
                        P.I('pe', 'matmul', ['onesf', ('b_rden', fi)], [('b_po', oi)], out=PO[oi][0:64, 0:512], lhsT=onesf[64:65, 0:64],
                            rhs=rden[fi][64:65, :], start=True, stop=True)
                        P.I('dve', 'tensor_tensor', [('b_po', oi), ('b_osb', fi)], [('b_OT', fi)], out=OT[fi][:], in0=osb[fi][0:64, :],
                            in1=PO[oi][0:64, 0:512], op=ALU.mult)
                        P.dma('pool', OTd[h, :, g * 512:(g + 1) * 512], OT[fi][:], ('b_OTst', fi), [('b_OT', fi)], [])

                    deferred.append([DEFER, part2])

                def tick():
                    for d in deferred:
                        d[0] -= 1
                    while deferred and deferred[0][0] <= 0:
                        deferred.pop(0)[1]()

                loadhead(0)
                pend = []
                for h in range(16):
                    cnt = 0
                    for g in range(GF, NG):
                        for p in range((4 * g + 4) // 2):
                            pend.append(qk(h, g, p))
                            if len(pend) > LOOK:
                                pv(*pend.pop(0))
                            tick()
                            cnt += 1
                            if cnt == LOOK + 1 and h + 1 < 16:
                                loadhead(h + 1)
                while pend:
                    pv(*pend.pop(0))
                    tick()
                while deferred:
                    deferred.pop(0)[1]()
                P.barrier()
                P.emit()

        if 'C' in phases:
            with contextlib.ExitStack() as esC:
                ps = [esC.enter_context(nc.psum_tensor('psc%d' % i, [128, 512], F32)) for i in range(7)]
                pst = esC.enter_context(nc.psum_tensor('pstc', [128, 1024], BF16))
                xt = sb('c_xt', [128, 4, 1024], F32, esC)
                bAs = [sb('c_bA%d' % i, [128, 8, 512], BF16, esC) for i in range(2)]
                bB = sb('c_bB', [128, 8, 512], BF16, esC)
                bC = sb('c_bC', [128, 4, 1024], BF16, esC)
                bD = sb('c_bD', [128, 8, 512], BF16, esC)
                bE = sb('c_bE', [128, 8, 512], BF16, esC)
                actT = sb('c_act', [128, 22, 512], BF16, esC)
                wb_ = [sb('c_w%d' % i, [128, 8, 512], BF16, esC) for i in range(5)]
                tmp = [sb('c_tmp%d' % i, [128, 516], F32, esC) for i in range(12)]
                sa = sb('c_sa', [128, 8, 512], BF16, esC)
                sbb = sb('c_sb', [128, 8, 512], BF16, esC)
                S = sb('c_S', [128, 8, 128], F32, esC)
                S1 = sb('c_S1', [128, 8, 128], F32, esC)
                ktok = sb('c_ktok', [128, 8, 128], BF16, esC)
                At = sb('c_At', [128, 8, 128], BF16, esC)
                Sp = sb('c_Sp', [128, 8, 128], BF16, esC)
                m01x4 = sb('c_m01x4', [128, 4, 128], F32, esC)
                rmask = sb('c_rmask', [128, 512], F32, esC)
                dc = sb('c_dc', [128, 8, 6, 4], F32, esC)
                sq = [sb('c_sq%d' % i, [128, 512], BF16, esC) for i in range(4)]
                junk = sb('c_junk', [128, 1024], BF16, esC)
                ssq = sb('c_ss', [128, 4], F32, esC)
                std = sb('c_std', [128, 4], F32, esC)
                rstd = sb('c_rstd', [128, 4], F32, esC)
                carry = sb('c_carry', [128, 44, 2], F32, esC)

                P.I('dve', 'memset', [], [('c_S', h) for h in range(8)], ap=S[:], constant=0.0)
                for i in range(4):
                    P.I('dve', 'tensor_copy', ['m01'], ['m01x4'], out=m01x4[:, i, :], in_=m01[:])
                P.I('dve', 'memset', [], ['rmask'], ap=rmask[:], constant=1.0)
                P.I('dve', 'memset', ['rmask'], ['rmask'], ap=rmask[:].rearrange("p (a c) -> p a c", c=128)[:, :, 0:1], constant=0.0)
                P.I('pool', 'memset', [], [('c_carry', ct) for ct in range(44)], ap=carry[:], constant=0.0)

                sched_g = []
                for i in (0, 2, 4, 5, 1, 3, 6, 7):
                    sched_g.append(('win', i, 8))
                sched_g += [('win', 14, 8), ('win', 15, 8), ('win', 16, 8), ('win', 17, 8), ('wb', 0, 8), ('wb', 1, 8), ('wa', 0, 8), ('wa', 1, 8)]
                sched_g += [('wo', 0, 8), ('wo', 1, 8)]
                upseq = []
                for j in range(22):
                    for pc in (j // 4, (22 + j) // 4):
                        if pc not in upseq[-3:]:
                            upseq.append(pc)
                sched_g += [('wup', pc, 8) for pc in upseq]
                for half in range(2):
                    for kg in range(3):
                        sched_g.append(('wdn', kg * 2 + half, 8 if kg < 2 else 6))
                sched_p = [('win', 2, 8), ('win', 4, 8), ('win', 3, 8), ('win', 5, 8)]
                sched_h = [e for e in sched_g if e[0] != 'wdn']
                if NGP >= 1:
                    W = WStream(P, wb_, wsc, sched_p * GF + sched_h + sched_g * (NG - GF - 1), res=3)
                else:
                    W = WStream(P, wb_, wsc, sched_g * NG, res=3)
                bank = [0]

                def nb():
                    bank[0] = (bank[0] + 1) % 7
                    return bank[0]

                for g in range(NG):
                    gsl = slice(g * 512, (g + 1) * 512)
                    full = g >= GF
                    bA = bAs[g % 2]
                    bAk = ('c_bA', g % 2)
                    if g == 0:
                        P.dma('pool', bA[:], nTd[0].rearrange("p (k c) -> p k c", c=512), ('c_nT', 0), [], [bAk])
                    if g + 1 < NG:
                        P.dma('pool', bAs[(g + 1) % 2][:], nTd[g + 1].rearrange("p (k c) -> p k c", c=512), ('c_nT', (g + 1) % 2), [], [('c_bA', (g + 1) % 2)])
                    if full:
                        P.dma('pool', xt[:], x[gsl, :].rearrange("(b p) d -> p b d", p=128), 'c_x', [], ['c_xt'])
                    if full:
                        P.dma('pool', bB[:], OTd[:, :, gsl].rearrange("(j two) d t -> (two d) j t", two=2), 'c_oT', [], ['c_bB'])
                    jobs = []
                    for half in range(2):
                        for b in range(4):
                            def vjob(half=half, b=b):
                                wt, wk = W.get('win', 4 + half)
                                bi = nb()
                                for kc in range(8):
                                    P.I('pe', 'matmul', [wk, bAk], [('ps', bi)], out=ps[bi][:, :], lhsT=bA[:, kc, b * 128:(b + 1) * 128],
                                        rhs=wt[:, kc, :], start=(kc == 0), stop=(kc == 7))
                                if b % 2 == 0:
                                    P.I('dve', 'tensor_copy', [('ps', bi)], [('c_bC', b, half)], out=bC[:, b, half * 512:(half + 1) * 512], in_=ps[bi][:, :])
                                else:
                                    P.I('act', 'activation', [('ps', bi)], [('c_bC', b, half)], out=bC[:, b, half * 512:(half + 1) * 512], in_=ps[bi][:, :], func=AF.Copy)
                            jobs.append(vjob)
                    for half in range(2 if full else 0):
                        for hh in range(4):
                            def gjob(half=half, hh=hh):
                                wt, wk = W.get('win', 6 + half)
                                h = half * 4 + hh
                                bi = nb()
                                for kc in range(8):
                                    P.I('pe', 'matmul', [wk, bAk], [('ps', bi)], out=ps[bi][:, :], lhsT=wt[:, kc, hh * 128:(hh + 1) * 128],
                                        rhs=bA[:, kc, :], start=(kc == 0), stop=(kc == 7))
                                P.I('act', 'activation', [('ps', bi)], ['c_bD'], out=bD[:, h, :], in_=ps[bi][:, :], func=AF.Silu)
                            jobs.append(gjob)
                    gate_jobs = []
                    for gi, (pbase, dstt, dkey) in enumerate(((14, sa, 'c_sa'), (16, sbb, 'c_sb')) if full else ()):
                        for c in range(8):
                            def job(pbase=pbase, dstt=dstt, dkey=dkey, c=c):
                                wt, wk = W.get('win', pbase + c // 4)
                                bi = nb()
                                for kc in range(8):
                                    P.I('pe', 'matmul', [wk, bAk], [('ps', bi)], out=ps[bi][:, :], lhsT=wt[:, kc, (c % 4) * 128:(c % 4 + 1) * 128],
                                        rhs=bA[:, kc, :], start=(kc == 0), stop=(kc == 7))
                                P.I('act', 'activation', [('ps', bi)], [(dkey, c)], out=dstt[:, c, :], in_=ps[bi][:, :], func=AF.Sigmoid)
                            gate_jobs.append(job)
                    for _ in range(6 if full else 0):
                        jobs.append(gate_jobs.pop(0))
                    fillplan = {0: [2, 1, 1, 1, 1, 1, 1], 1: [2, 2, 2, 2, 2, 2, 2]} if full else {0: [1, 1, 1, 1, 0, 0, 0], 1: [1, 1, 1, 1, 0, 0, 0]}

                    def fill(hb, step):
                        for _ in range(fillplan[hb][step]):
                            if jobs:
                                jobs.pop(0)()

                    for hb in range(2):
                        hs = [hb * 4 + i for i in range(4)]
                        Tt = {h: [tmp[(h % 4) * 3 + i][:, 0:512] for i in range(3)] for h in hs}
                        Tk = {h: ['c_tmp%d' % ((h % 4) * 3 + i) for i in range(3)] for h in hs}
                        T3b = {h: tmp[(h % 4) * 3 + 2][:, 0:512].rearrange("p (a c) -> p a c", c=128) for h in hs}
                        for h in hs:
                            off = (h % 4) * 128
                            if full:
                                wq_, wqk = W.get('win', 0 + h // 4)
                                bq = nb()
                                for kc in range(8):
                                    P.I('pe', 'matmul', [wqk, bAk], [('ps', bq)], out=ps[bq][:, :], lhsT=wq_[:, kc, off:off + 128], rhs=bA[:, kc, :],
                                        start=(kc == 0), stop=(kc == 7))
                                P.I('act', 'activation', [('ps', bq)], [('c_act', h)], out=actT[:, h, :], in_=ps[bq][:, :], func=AF.Silu)
                            wf_, wfk = W.get('win', 2 + h // 4)
                            bf = nb()
                            for kc in range(8):
                                P.I('pe', 'matmul', [wfk, bAk], [('ps', bf)], out=ps[bf][:, :], lhsT=wf_[:, kc, off:off + 128], rhs=bA[:, kc, :],
                                    start=(kc == 0), stop=(kc == 7))
                            P.I('act', 'activation', [('ps', bf)], [Tk[h][0]], out=Tt[h][0], in_=ps[bf][:, :], func=AF.Sigmoid)
                        for h in hs:
                            P.I('dve', 'tensor_scalar', [Tk[h][0], 'oml', 'lb'], [Tk[h][0]], out=Tt[h][0], in0=Tt[h][0], scalar1=oml[:, h:h + 1],
                                scalar2=lb[:, h:h + 1], op0=ALU.mult, op1=ALU.add)
                        fill(hb, 0)
                        for h in hs:
                            P.I('act', 'activation', [Tk[h][0]], [Tk[h][1]], out=Tt[h][1], in_=Tt[h][0], func=AF.Ln)
                            P.I('dve', 'tensor_scalar', [Tk[h][0]], [('c_act', 8 + h)], out=actT[:, 8 + h, :], in0=Tt[h][0], scalar1=-1.0, scalar2=1.0,
                                op0=ALU.mult, op1=ALU.add)
                        fill(hb, 1)
                        for h in hs:
                            P.I('dve', 'tensor_tensor_scan', [Tk[h][1], 'rmask'], [Tk[h][2]], out=Tt[h][2], data0=rmask[:, :], data1=Tt[h][1], initial=0.0,
                                op0=ALU.mult, op1=ALU.add)
                        fill(hb, 2)
                        for h in hs:
                            dk = ('c_dc', h)
                            P.I('dve', 'tensor_copy', [Tk[h][2]], [(dk, 0)], out=dc[:, h, 0, :], in_=T3b[h][:, :, 63])
                            P.I('dve', 'tensor_copy', [Tk[h][2]], [(dk, 5)], out=dc[:, h, 5, :], in_=T3b[h][:, :, 127])
                            P.I('dve', 'tensor_tensor', [Tk[h][2], (dk, 0)], [Tk[h][2]], out=T3b[h], in0=T3b[h],
                                in1=dc[:, h, 0, :].unsqueeze(2).to_broadcast([128, 4, 128]), op=ALU.subtract)
                        fill(hb, 3)
                        for h in hs:
                            if full:
                                P.I('act', 'activation', [Tk[h][2]], [Tk[h][1]], out=Tt[h][1], in_=Tt[h][2], func=AF.Exp)
                            P.I('act', 'activation', [Tk[h][2]], [Tk[h][2]], out=Tt[h][2], in_=Tt[h][2], func=AF.Exp, scale=-1.0)
                        fill(hb, 4)
                        for h in hs:
                            if full:
                                P.I('dve', 'tensor_tensor', [('c_act', h), Tk[h][1]], [('c_act', h)], out=actT[:, h, :], in0=actT[:, h, :], in1=Tt[h][1], op=ALU.mult)
                            P.I('dve', 'tensor_tensor', [('c_act', 8 + h), Tk[h][2]], [('c_act', 8 + h)], out=actT[:, 8 + h, :], in0=actT[:, 8 + h, :], in1=Tt[h][2],
                                op=ALU.mult)
                        fill(hb, 5)
                        for h in hs:
                            dk = ('c_dc', h)
                            P.I('dve', 'tensor_tensor', [(dk, 5), (dk, 0)], [(dk, 3)], out=dc[:, h, 3, :], in0=dc[:, h, 5, :], in1=dc[:, h, 0, :], op=ALU.subtract)
                            if full:
                                P.I('act', 'activation', [(dk, 0)], [(dk, 1)], out=dc[:, h, 1, :], in_=dc[:, h, 0, :], func=AF.Exp)
                            P.I('act', 'activation', [(dk, 5)], [(dk, 2)], out=dc[:, h, 2, :], in_=dc[:, h, 5, :], func=AF.Exp)
                            P.I('act', 'activation', [(dk, 3)], [(dk, 4)], out=dc[:, h, 4, :], in_=dc[:, h, 3, :], func=AF.Exp)
                        fill(hb, 6)
                    while jobs:
                        jobs.pop(0)()
                    for b in range(4):
                        bs = slice(b * 128, (b + 1) * 128)
                        for h in range(8):
                            P.I('pe', 'transpose', [('c_act', 8 + h)], [('pst', 0)], out=pst[:, h * 128:(h + 1) * 128], in_=actT[:, 8 + h, bs], identity=identb[:])
                        P.I('act', 'activation', [('pst', 0)], ['c_ktok'], out=ktok[:].rearrange("p h c -> p (h c)"), in_=pst[:, :], func=AF.Copy)
                        for half in range(2 if full else 0):
                            bx = nb()
                            for hh in range(4):
                                h = half * 4 + hh
                                P.I('pe', 'matmul', [('c_act', 8 + h), ('c_act', h)], [('ps', bx)], out=ps[bx][:, hh * 128:(hh + 1) * 128],
                                    lhsT=actT[:, 8 + h, bs], rhs=actT[:, h, bs], start=True, stop=True)
                            P.I('dve', 'tensor_tensor', [('ps', bx), 'm01x4'], [('c_At', half)], out=At[:, half * 4:(half + 1) * 4, :],
                                in0=ps[bx][:, :].rearrange("p (h c) -> p h c", c=128), in1=m01x4[:], op=ALU.mult)
                        for h in range(8):
                            if full:
                                P.I('act', 'activation', [('c_S', h), (('c_dc', h), 1)], [('c_Sp', h)], out=Sp[:, h, :], in_=S[:, h, :], func=AF.Copy,
                                    scale=dc[:, h, 1, b:b + 1])
                            P.I('pool', 'tensor_scalar', [('c_S', h), (('c_dc', h), 2)], [('c_S1', h)], out=S1[:, h, :], in0=S[:, h, :],
                                scalar1=dc[:, h, 2, b:b + 1], scalar2=1.0, op0=ALU.mult, op1=ALU.mult)
                        for _ in range((3 if b < 2 else 2) if full else 0):
                            gate_jobs.pop(0)()
                        for half in range(2 if full else 0):
                            by = nb()
                            for hh in range(4):
                                h = half * 4 + hh
                                vv = bC[:, b, h * 128:(h + 1) * 128]
                                P.I('pe', 'matmul', [('c_bC', b, half), ('c_At', half)], [('ps', by)], out=ps[by][:, hh * 128:(hh + 1) * 128],
                                    lhsT=vv, rhs=At[:, h, :], start=True, stop=False)
                                P.I('pe', 'matmul', [('c_Sp', h), ('c_act', h)], [('ps', by)], out=ps[by][:, hh * 128:(hh + 1) * 128],
                                    lhsT=Sp[:, h, :], rhs=actT[:, h, bs], start=False, stop=True)
                            oute = bE[:, half * 4:(half + 1) * 4, bs]
                            ine = ps[by][:, :].rearrange("p (h c) -> p h c", c=128)
                            if half == 0:
                                P.I('act', 'activation', [('ps', by)], [('c_bEr', half, b)], out=oute, in_=ine, func=AF.Copy)
                            else:
                                P.I('dve', 'tensor_copy', [('ps', by)], [('c_bEr', half, b)], out=oute, in_=ine)
                        for half in range(2):
                            bz = nb()
                            for hh in range(4):
                                h = half * 4 + hh
                                vv = bC[:, b, h * 128:(h + 1) * 128]
                                P.I('pe', 'matmul', ['c_ktok', ('c_bC', b, half)], [('ps', bz)], out=ps[bz][:, hh * 128:(hh + 1) * 128],
                                    lhsT=ktok[:, h, :], rhs=vv, start=True, stop=True)
                            for hh in range(4):
                                h = half * 4 + hh
                                P.I('dve', 'scalar_tensor_tensor', [('ps', bz), (('c_dc', h), 4), ('c_S1', h)], [('c_S', h)], out=S[:, h, :],
                                    in0=ps[bz][:, hh * 128:(hh + 1) * 128], scalar=dc[:, h, 4, b:b + 1], in1=S1[:, h, :], op0=ALU.mult, op1=ALU.add)
                    if not full:
                        continue
                    def norm_batch(hb):
                        hs = [hb * 4 + i for i in range(4)]
                        ek = {h: [('c_bEr', h // 4, b) for b in range(4)] for h in hs}
                        bns = {}
                        for h in hs:
                            P.I('act', 'activation', ek[h], [('c_sq', h % 4)], out=sq[h % 4][:], in_=bE[:, h, :], func=AF.Square)
                        for h in hs:
                            bns[h] = nb()
                            P.I('pe', 'matmul', ['onesb', ('c_sq', h % 4)], [('ps', bns[h])], out=ps[bns[h]][:, :], lhsT=onesb[:], rhs=sq[h % 4][:], start=True, stop=True)
                        for h in hs:
                            P.I('act', 'activation', [('ps', bns[h])], ['c_tmp%d' % (h % 4)], out=tmp[h % 4][:, 0:512], in_=ps[bns[h]][:, :], func=AF.Ln,
                                scale=1.0 / 128.0, bias=epst[:, 0:1])
                        for h in hs:
                            P.I('act', 'activation', ['c_tmp%d' % (h % 4)], ['c_tmp%d' % (h % 4)], out=tmp[h % 4][:, 0:512], in_=tmp[h % 4][:, 0:512], func=AF.Exp,
                                scale=-0.5)
                        for h in hs:
                            P.I('dve', 'tensor_tensor', ek[h] + ['c_tmp%d' % (h % 4)], ['c_tmp%d' % (4 + h % 4)], out=tmp[4 + h % 4][:, 0:512], in0=bE[:, h, :],
                                in1=tmp[h % 4][:, 0:512], op=ALU.mult)
                        for h in hs:
                            P.I('dve', 'scalar_tensor_tensor', ['c_tmp%d' % (4 + h % 4), 'gnorm', 'c_bD'], [('c_bE', h)], out=bE[:, h, :], in0=tmp[4 + h % 4][:, 0:512],
                                scalar=gnorm[:, 0:1], in1=bD[:, h, :], op0=ALU.mult, op1=ALU.mult)
                    bEk = [('c_bE', h) for h in range(8)]
                    def wb_half(half):
                        wt, wk = W.get('wb', half)
                        for c in range(4):
                            cc = half * 4 + c
                            bi = nb()
                            for kc in range(8):
                                P.I('pe', 'matmul', [wk, 'c_bB'], [('ps', bi)], out=ps[bi][:, :], lhsT=wt[:, kc, c * 128:(c + 1) * 128], rhs=bB[:, kc, :],
                                    start=(kc == 0), stop=(kc == 7))
                            P.I('dve', 'tensor_tensor', [('ps', bi), ('c_sb', cc)], [('c_sb', cc)], out=sbb[:, cc, :], in0=ps[bi][:, :], in1=sbb[:, cc, :], op=ALU.mult)
                    wb_half(0)
                    norm_batch(0)
                    wb_half(1)
                    norm_batch(1)
                    for half in range(2):
                        wt, wk = W.get('wa', half)
                        for c in range(4):
                            cc = half * 4 + c
                            bi = nb()
                            for kc in range(8):
                                P.I('pe', 'matmul', [wk] + bEk, [('ps', bi)], out=ps[bi][:, :], lhsT=wt[:, kc, c * 128:(c + 1) * 128], rhs=bE[:, kc, :],
                                    start=(kc == 0), stop=(kc == 7))
                            P.I('dve', 'tensor_tensor', [('ps', bi), ('c_sa', cc)], [('c_sa', cc)], out=sa[:, cc, :], in0=ps[bi][:, :], in1=sa[:, cc, :], op=ALU.mult)
                            P.I('pool', 'tensor_tensor', [('c_sb', cc), ('c_sa', cc)], ['c_bD'], out=bD[:, cc, :], in0=sbb[:, cc, :], in1=sa[:, cc, :], op=ALU.add)
                    for half in range(2):
                        wt, wk = W.get('wo', half)
                        for b in range(4):
                            bi = nb()
                            for kc in range(8):
                                P.I('pe', 'matmul', [wk, 'c_bD'], [('ps', bi)], out=ps[bi][:, :], lhsT=bD[:, kc, b * 128:(b + 1) * 128], rhs=wt[:, kc, :],
                                    start=(kc == 0), stop=(kc == 7))
                            P.I('dve', 'tensor_tensor', [('ps', bi), 'c_xt'], ['c_xt'], out=xt[:, b, half * 512:(half + 1) * 512], in0=ps[bi][:, :],
                                in1=xt[:, b, half * 512:(half + 1) * 512], op=ALU.add)
                    rms_to_T('c_', xt, 'c_xt', junk, bC, 'c_bCn', ssq, std, rstd, bB, 'c_bBn', gffn)
                    n2keys = [('c_bBn', kc) for kc in range(8)] + ['c_bB']
                    if NGP >= 1 and g == GF:
                        for j in range(22):
                            for ct in (j, 22 + j):
                                wt, wk = W.get('wup', ct // 4)
                                off = (ct % 4) * 128
                                bi = nb()
                                for kc in range(8):
                                    P.I('pe', 'matmul', [wk] + n2keys, [('ps', bi)], out=ps[bi][:, 0:128], lhsT=wt[:, kc, off:off + 128], rhs=bB[:, kc, 384:512],
                                        start=(kc == 0), stop=(kc == 7))
                                P.I('dve', 'tensor_copy', [('ps', bi)], [('c_carry', ct)], out=carry[:, ct, :], in_=ps[bi][:, 126:128])
                        continue
                    for jb in range(0, 22, 4):
                        js = list(range(jb, min(jb + 4, 22)))
                        for j in js:
                            for which, ct in (('g', j), ('v', 22 + j)):
                                wt, wk = W.get('wup', ct // 4)
                                off = (ct % 4) * 128
                                bi = nb()
                                for kc in range(8):
                                    P.I('pe', 'matmul', [wk] + n2keys, [('ps', bi)], out=ps[bi][:, :], lhsT=wt[:, kc, off:off + 128], rhs=bB[:, kc, :],
                                        start=(kc == 0), stop=(kc == 7))
                                ai = (j % 4) + (0 if which == 'g' else 4)
                                ac, ak = tmp[ai], 'c_tmp%d' % ai
                                ck = ('c_carry', ct)
                                P.I('act', 'activation', [('ps', bi), 'cw', 'cb'], [ak], out=ac[:, 0:512], in_=ps[bi][:, :], func=AF.Identity,
                                    scale=cw[:, ct, 2:3], bias=cb[:, ct:ct + 1])
                                P.I('dve', 'scalar_tensor_tensor', [('ps', bi), ak, 'cw'], [ak], out=ac[:, 1:512], in0=ps[bi][:, 0:511], scalar=cw[:, ct, 1:2],
                                    in1=ac[:, 1:512], op0=ALU.mult, op1=ALU.add)
                                P.I('dve', 'scalar_tensor_tensor', [('ps', bi), ak, 'cw'], [ak], out=ac[:, 2:512], in0=ps[bi][:, 0:510], scalar=cw[:, ct, 0:1],
                                    in1=ac[:, 2:512], op0=ALU.mult, op1=ALU.add)
                                P.I('dve', 'scalar_tensor_tensor', [ck, ak, 'cw'], [ak], out=ac[:, 0:2], in0=carry[:, ct, 0:2], scalar=cw[:, ct, 0:1],
                                    in1=ac[:, 0:2], op0=ALU.mult, op1=ALU.add)
                                P.I('dve', 'scalar_tensor_tensor', [ck, ak, 'cw'], [ak], out=ac[:, 0:1], in0=carry[:, ct, 1:2], scalar=cw[:, ct, 1:2],
                                    in1=ac[:, 0:1], op0=ALU.mult, op1=ALU.add)
                                P.I('dve', 'tensor_copy', [('ps', bi)], [ck], out=carry[:, ct, :], in_=ps[bi][:, 510:512])
                        for j in js:
                            gk, vk = 'c_tmp%d' % (j % 4), 'c_tmp%d' % (4 + j % 4)
                            P.I('act', 'activation', [gk], [gk], out=tmp[j % 4][:, 0:512], in_=tmp[j % 4][:, 0:512], func=AF.Gelu)
                            P.I('pool', 'tensor_tensor', [gk, vk], [('c_act', j)], out=actT[:, j, :], in0=tmp[j % 4][:, 0:512], in1=tmp[4 + j % 4][:, 0:512], op=ALU.mult)
                    for half in range(2):
                        banks = [nb() for _ in range(4)]
                        for kg in range(3):
                            nk = 8 if kg < 2 else 6
                            wt, wk = W.get('wdn', kg * 2 + half)
                            for b in range(4):
                                for kc in range(nk):
                                    j = kg * 8 + kc
                                    P.I('pe', 'matmul', [wk, ('c_act', j)], [('ps', banks[b])], out=ps[banks[b]][:, :], lhsT=actT[:, j, b * 128:(b + 1) * 128],
                                        rhs=wt[:, kc, :], start=(j == 0), stop=(j == 21))
                        for b in range(4):
                            P.I('dve', 'tensor_tensor', [('ps', banks[b]), 'c_xt'], ['c_xt'], out=xt[:, b, half * 512:(half + 1) * 512], in0=ps[banks[b]][:, :],
                                in1=xt[:, b, half * 512:(half + 1) * 512], op=ALU.add)
                    for b in range(4):
                        P.I('act', 'activation', ['c_xt'], ['c_junk', ('c_ss', b)], out=junk[:], in_=xt[:, b, :], func=AF.Square, accum_out=ssq[:, b:b + 1])
                    P.I('act', 'activation', [('c_ss', b) for b in range(4)], ['c_std'], out=std[:], in_=ssq[:], func=AF.Sqrt, scale=1.0 / 1024.0,
                        bias=epst[:, 0:1])
                    P.I('dve', 'reciprocal', ['c_std'], ['c_rstd'], out=rstd[:], in_=std[:])
                    for b in range(4):
                        P.I('dve', 'scalar_tensor_tensor', ['c_xt', 'c_rstd', 'gfinb'], ['c_xt'], out=xt[:, b, :], in0=xt[:, b, :], scalar=rstd[:, b:b + 1],
                            in1=gfinb[:], op0=ALU.mult, op1=ALU.mult)
                    if g >= NGP:
                        P.dma('pool', y[(g - NGP) * 512:(g - NGP + 1) * 512, :].rearrange("(b p) d -> p b d", p=128), xt[:], 'c_yst', ['c_xt'], [])
                P.barrier()
                P.emit()
        ninstr = P.n
    return nc, ninstr


_CACHE = {}


def _host_consts():
    s = np.arange(128)[:, None]
    t = np.arange(128)[None, :]
    m01 = (s <= t).astype(np.float32)
    mneg = np.where(s <= t, 0.0, -30000.0).astype(np.float32)
    return np.eye(128, dtype=np.float32), mneg, m01


def make_in_maps(inputs):
    f = lambda a: np.ascontiguousarray(np.asarray(a, dtype=np.float32))
    x = f(inputs['x'])
    ident, mneg, m01 = _host_consts()
    shared = {
        'w_in': f(inputs['w_in'][0]), 'w_a': f(inputs['w_branch_a'][0]), 'w_b': f(inputs['w_branch_b'][0]),
        'w_o': f(inputs['w_out'][0]), 'w_up': f(inputs['w_up'][0]), 'w_dn': f(inputs['w_down'][0]),
        'gmix': f(np.asarray(inputs['norm_mix'])[0].reshape(8, 128).T),
        'gffn': f(np.asarray(inputs['norm_ffn'])[0].reshape(8, 128).T),
        'gfinb': f(np.broadcast_to(np.asarray(inputs['norm_final']).reshape(1, 1024), (128, 1024))),
        'ffb': f(np.asarray(inputs['fox_f_bias'])[0].reshape(16, 1)),
        'lbl': f(np.asarray(inputs['hg_lb_logits']).reshape(2, 8, 128).transpose(2, 0, 1)),
        'gnorm': f(np.asarray(inputs['hg_norm'])[0].reshape(128, 1)),
        'cw': f(np.asarray(inputs['conv_w'])[0].reshape(3, 44, 128).transpose(2, 1, 0)),
        'cb': f(np.asarray(inputs['conv_b'])[0].reshape(44, 128).T),
        'ident': ident, 'mneg': mneg, 'm01': m01,
    }
    maps = []
    T = x.shape[1]
    H = T // 2
    for c in range(8):
        m = dict(shared)
        b, half = c % 4, c // 4
        if half == 0:
            xl = np.zeros((T, 1024), np.float32)
            xl[H:] = x[b, :H]
            vl = np.zeros(T, np.float32)
            vl[H:] = 1.0
        else:
            xl = x[b]
            vl = np.ones(T, np.float32)
        m['x'] = np.ascontiguousarray(xl)
        m['valid'] = np.ascontiguousarray(vl.reshape(T // 128, 128).T)
        maps.append(m)
    return maps


def kernel(**inputs):
    if 'nc' not in _CACHE:
        _CACHE['nc'] = build()[0]
    nc = _CACHE['nc']
    in_maps = make_in_maps(inputs)
    res = run_bass_kernel_spmd(nc, in_maps, core_ids=list(range(8)))
    out = np.stack([np.concatenate([np.asarray(res.results[b]['y'], dtype=np.float32),
                                    np.asarray(res.results[4 + b]['y'], dtype=np.float32)], axis=0) for b in range(4)], axis=0)
    return out
```

```python
import contextlib
import numpy as np
import concourse.bass as bass
import concourse.mybir as mybir
from concourse.bass_utils import run_bass_kernel_spmd

F32 = mybir.dt.float32
BF16 = mybir.dt.bfloat16
AF = mybir.ActivationFunctionType
ALU = mybir.AluOpType
ENG = ('pe', 'act', 'dve', 'pool', 'sp')
EPS = 1e-6


class Prog:
    def __init__(self, nc, es):
        self.nc = nc
        self.es = es
        self.streams = {e: [] for e in ENG}
        self.sem = {}
        self.cnt = {}
        for e in ('pe', 'act', 'dve', 'pool'):
            self.sem['c_' + e] = es.enter_context(nc.semaphore('c_' + e))
            self.cnt['c_' + e] = 0
        self.waited = {e: {} for e in ENG}
        self.lastw = {}
        self.readers = {}
        self.n = 0
        self.pkeys = {}

    def _dsem(self, key, eng):
        key = (eng, key)
        if key not in self.pkeys:
            self.pkeys[key] = sum(1 for k in self.pkeys if k[0] == eng)
        sid = 'd%s%d' % (eng, self.pkeys[key])
        if sid not in self.sem:
            self.sem[sid] = self.es.enter_context(self.nc.semaphore(sid))
            self.cnt[sid] = 0
        return sid

    def _wait(self, eng, sid, val):
        if self.waited[eng].get(sid, 0) >= val:
            return
        self.waited[eng][sid] = val
        h = self.sem[sid]
        self.streams[eng].append(lambda e: e.wait_ge(h, val))

    def op(self, eng, fn, reads=(), writes=(), dma=None):
        own = 'c_' + eng
        deps = {}
        for k in reads:
            ev = self.lastw.get(k)
            if ev is not None:
                deps[ev[0]] = max(deps.get(ev[0], 0), ev[1])
        for k in writes:
            ev = self.lastw.get(k)
            if ev is not None:
                deps[ev[0]] = max(deps.get(ev[0], 0), ev[1])
            for sid, val in self.readers.get(k, {}).items():
                deps[sid] = max(deps.get(sid, 0), val)
        if eng == 'pe':
            deps.pop(own, None)
        for sid, val in deps.items():
            self._wait(eng, sid, val)
        if dma is None:
            sid = own
            self.cnt[sid] += 1
            inc = 1
        else:
            sid = self._dsem(dma, eng)
            self.cnt[sid] += 16
            inc = 16
        val = self.cnt[sid]
        h = self.sem[sid]
        self.streams[eng].append(lambda e: fn(e).then_inc(h, inc))
        for k in writes:
            self.lastw[k] = (sid, val)
            self.readers[k] = {}
        for k in reads:
            d = self.readers.setdefault(k, {})
            d[sid] = max(d.get(sid, 0), val)
        self.n += 1

    def I(self, eng, method, r, w, **kw):
        self.op(eng, lambda e: getattr(e, method)(**kw), r, w)

    def dma(self, eng, out, in_, key, r, w):
        self.op(eng, lambda e: e.dma_start(out=out, in_=in_), r, w, dma=key)

    def barrier(self):
        for e in ENG:
            for sid, c in self.cnt.items():
                if c > 0:
                    self._wait(e, sid, c)

    def emit(self):
        st = self.streams
        with self.nc.Block() as block:
            if st['pe']:
                @block.tensor
                def _(e):
                    for f in st['pe']:
                        f(e)
            if st['act']:
                @block.scalar
                def _(e):
                    for f in st['act']:
                        f(e)
            if st['dve']:
                @block.vector
                def _(e):
                    for f in st['dve']:
                        f(e)
            if st['pool']:
                @block.gpsimd
                def _(e):
                    for f in st['pool']:
                        f(e)
            if st['sp']:
                @block.sync
                def _(e):
                    for f in st['sp']:
                        f(e)
        self.streams = {e: [] for e in ENG}
        self.pkeys = {}


class WStream:
    def __init__(self, P, bufs, wsc, schedule, res=3):
        self.P = P
        self.bufs = bufs
        self.wsc = wsc
        self.sched = schedule
        self.res = res
        self.issued = 0
        self.ptr = -1

    def _issue(self, i):
        name, idx, nk = self.sched[i]
        b = i % len(self.bufs)
        src = self.wsc[name][idx].rearrange("p (k c) -> p k c", c=512)[:, 0:nk, :]
        self.P.dma('sp', self.bufs[b][:, 0:nk, :], src, ('w', b), [], [('w', b)])

    def get(self, name, idx):
        for back in range(0, self.res):
            i = self.ptr - back
            if i >= 0 and self.sched[i][0] == name and self.sched[i][1] == idx:
                return self.bufs[i % len(self.bufs)], ('w', i % len(self.bufs))
        self.ptr += 1
        assert self.sched[self.ptr][0] == name and self.sched[self.ptr][1] == idx, (self.sched[self.ptr], name, idx)
        look = len(self.bufs) - self.res
        while self.issued < len(self.sched) and self.issued <= self.ptr + look:
            self._issue(self.issued)
            self.issued += 1
        i = self.ptr
        return self.bufs[i % len(self.bufs)], ('w', i % len(self.bufs))


def build(T=8192, NGP=8, debug=False, phases=('0', 'A', 'B', 'C')):
    nc = bass.Bass("TRN2", target_bir_lowering=False)
    NG = T // 512
    GF = max(NGP - 1, 0)
    TO = (NG - NGP) * 512
    NBLK = T // 128

    def din(name, shape, dt=F32):
        return nc.dram_tensor(name, list(shape), dt, kind="ExternalInput").ap()

    x = din('x', [T, 1024])
    w_in = din('w_in', [1024, 9232])
    w_a = din('w_a', [1024, 1024])
    w_b = din('w_b', [1024, 1024])
    w_o = din('w_o', [1024, 1024])
    w_up = din('w_up', [1024, 5632])
    w_dn = din('w_dn', [2816, 1024])
    gmix_d = din('gmix', [128, 8])
    gffn_d = din('gffn', [128, 8])
    gfinb_d = din('gfinb', [128, 1024])
    ffb_d = din('ffb', [16, 1])
    lbl_d = din('lbl', [128, 2, 8])
    gnorm_d = din('gnorm', [128, 1])
    cw_d = din('cw', [128, 44, 3])
    cb_d = din('cb', [128, 44])
    ident_d = din('ident', [128, 128])
    mneg_d = din('mneg', [128, 128])
    m01_d = din('m01', [128, 128])
    valid_d = din('valid', [128, T // 128])
    y = nc.dram_tensor('y', [TO, 1024], F32, kind='ExternalOutput').ap()

    skind = 'ExternalOutput' if debug else 'Internal'

    def dscr(name, shape, dt=BF16):
        return nc.dram_tensor(name, list(shape), dt, kind=skind).ap()

    wsc = {
        'win': dscr('s_win', [18, 128, 4096]),
        'wa': dscr('s_wa', [2, 128, 4096]),
        'wb': dscr('s_wb', [2, 128, 4096]),
        'wo': dscr('s_wo', [2, 128, 4096]),
        'wup': dscr('s_wup', [11, 128, 4096]),
        'wdn': dscr('s_wdn', [6, 128, 4096]),
    }
    nTd = dscr('s_nT', [NG, 128, 4096])
    QTd = dscr('s_QT', [16, 69, T])
    KTd = dscr('s_KT', [16, 69, T])
    Vd = dscr('s_V', [16, 128, NBLK, 65])
    OTd = dscr('s_OT', [16, 64, T])

    with contextlib.ExitStack() as es:
        P = Prog(nc, es)

        def sb(name, shape, dt, stack=es):
            return stack.enter_context(nc.sbuf_tensor('t_' + name, list(shape), dt))

        psA = contextlib.ExitStack()
        ps = [psA.enter_context(nc.psum_tensor('ps%d' % i, [128, 512], F32)) for i in range(6)]
        pst = psA.enter_context(nc.psum_tensor('pst', [128, 1024], BF16))
        pst2 = psA.enter_context(nc.psum_tensor('pst2', [128, 1024], BF16))

        identf = sb('identf', [128, 128], F32)
        identb = sb('identb', [128, 128], BF16)
        mnegb = sb('mnegb', [128, 128], BF16)
        m01 = sb('m01', [128, 128], F32)
        onesf = sb('onesf', [128, 512], F32)
        onesb = sb('onesb', [128, 128], BF16)
        gmix = sb('gmix', [128, 8], F32)
        gffn = sb('gffn', [128, 8], F32)
        gfinb = sb('gfinb', [128, 1024], F32)
        ffb = sb('ffb', [16, 1], F32)
        negfb = sb('negfb', [16, 1], F32)
        lbl = sb('lbl', [128, 2, 8], F32)
        lbd = sb('lbd', [128, 8], F32)
        lb = sb('lb', [128, 8], F32)
        oml = sb('oml', [128, 8], F32)
        gnorm = sb('gnorm', [128, 1], F32)
        cw = sb('cw', [128, 44, 3], F32)
        cb = sb('cb', [128, 44], F32)
        epst = sb('epst', [128, 1], F32)
        wff32 = sb('wff32', [128, 8, 16], F32)
        wffb = sb('wffb', [128, 8, 16], BF16)
        valid = sb('valid', [128, T // 128], F32)

        bg_jobs = []
        cl = [('identf', identf, ident_d), ('mneg32', None, None), ('m01', m01, m01_d), ('gmix', gmix, gmix_d),
              ('gffn', gffn, gffn_d), ('gfinb', gfinb, gfinb_d), ('ffb', ffb, ffb_d), ('lbl', lbl, lbl_d),
              ('gnorm', gnorm, gnorm_d), ('cw', cw, cw_d), ('cb', cb, cb_d), ('valid', valid, valid_d)]
        with contextlib.ExitStack() as es0:
            mneg32 = sb('mneg32', [128, 128], F32, es0)
            for name, t, d in cl:
                if t is None:
                    t, d = mneg32, mneg_d
                P.dma('sp', t[:], d, 'c', [], [name])
            P.dma('sp', wff32[:], w_in.rearrange("(kc p) c -> p kc c", p=128)[:, :, 7168:7184], 'c', [], ['wff32'])
            P.barrier()
            P.I('dve', 'tensor_copy', ['identf'], ['identb'], out=identb[:], in_=identf[:])
            P.I('dve', 'tensor_copy', ['mneg32'], ['mnegb'], out=mnegb[:], in_=mneg32[:])
            P.I('dve', 'tensor_copy', ['wff32'], ['wffb'], out=wffb[:], in_=wff32[:])
            P.I('dve', 'memset', [], ['onesf'], ap=onesf[:], constant=1.0)
            P.I('dve', 'memset', [], ['onesb'], ap=onesb[:], constant=1.0)
            P.I('dve', 'memset', [], ['epst'], ap=epst[:], constant=EPS)
            P.I('dve', 'tensor_scalar', ['ffb'], ['negfb'], out=negfb[:], in0=ffb[:], scalar1=-1.0, scalar2=None, op0=ALU.mult)
            P.I('dve', 'tensor_tensor', ['lbl'], ['lbd'], out=lbd[:], in0=lbl[:, 0, :], in1=lbl[:, 1, :], op=ALU.subtract)
            P.I('act', 'activation', ['lbd'], ['lb'], out=lb[:], in_=lbd[:], func=AF.Sigmoid)
            P.I('dve', 'tensor_scalar', ['lb'], ['oml'], out=oml[:], in0=lb[:], scalar1=-1.0, scalar2=1.0, op0=ALU.mult, op1=ALU.add)

            if '0' in phases:
                st32 = [sb('st32_%d' % i, [128, 8, 512], F32, es0) for i in range(2)]
                stb = [sb('stb_%d' % i, [128, 8, 512], BF16, es0) for i in range(2)]
                pieces = []
                win_r = w_in.rearrange("(kc p) c -> p kc c", p=128)
                for i in range(14):
                    pieces.append(('win', i, win_r[:, :, i * 512:(i + 1) * 512], 8))
                for i in range(4):
                    pieces.append(('win', 14 + i, win_r[:, :, 7184 + i * 512:7184 + (i + 1) * 512], 8))
                for nm, ap in (('wa', w_a), ('wb', w_b), ('wo', w_o)):
                    r = ap.rearrange("(kc p) c -> p kc c", p=128)
                    for i in range(2):
                        pieces.append((nm, i, r[:, :, i * 512:(i + 1) * 512], 8))
                r = w_up.rearrange("(kc p) c -> p kc c", p=128)
                for i in range(11):
                    pieces.append(('wup', i, r[:, :, i * 512:(i + 1) * 512], 8))
                r = w_dn.rearrange("(kc p) c -> p kc c", p=128)
                for kg in range(3):
                    nk = 8 if kg < 2 else 6
                    for half in range(2):
                        pieces.append(('wdn', kg * 2 + half, r[:, kg * 8:kg * 8 + nk, half * 512:(half + 1) * 512], nk))
                ceng = ('dve', 'pool', 'act')
                bg_pieces = [p_ for p_ in pieces if not (p_[0] == 'win' and 8 <= p_[1] <= 13)]
                pieces = []
                for nm, idx, src, nk in bg_pieces:
                    hk = nk // 2
                    for kh in range(2):
                        bg_jobs.append((nm, idx, src[:, kh * hk:(kh + 1) * hk, :], hk, kh * hk))
                for i, (nm, idx, src, nk) in enumerate(pieces):
                    s = i % 2
                    P.dma('sp', st32[s][:, 0:nk, :], src, ('st32', s), [], [('st32', s)])
                    ce = ceng[i % 3]
                    if ce == 'act':
                        P.I('act', 'activation', [('st32', s)], [('stb', s)], out=stb[s][:, 0:nk, :], in_=st32[s][:, 0:nk, :], func=AF.Copy)
                    else:
                        P.I(ce, 'tensor_copy', [('st32', s)], [('stb', s)], out=stb[s][:, 0:nk, :], in_=st32[s][:, 0:nk, :])
                    dst = wsc[nm][idx].rearrange("p (k c) -> p k c", c=512)[:, 0:nk, :]
                    P.dma('pool', dst, stb[s][:, 0:nk, :], ('stb', s), [('stb', s)], [])
            P.barrier()
            P.emit()

        def rms_N(pfx, xt, xkey, junk, xn, xnkey, ssq, std, rstd):
            for b in range(4):
                P.I('act', 'activation', [xkey], [pfx + 'junk', (pfx + 'ss', b)], out=junk[:], in_=xt[:, b, :], func=AF.Square,
                    accum_out=ssq[:, b:b + 1])
            P.I('act', 'activation', [(pfx + 'ss', b) for b in range(4)], [pfx + 'std'], out=std[:], in_=ssq[:], func=AF.Sqrt,
                scale=1.0 / 1024.0, bias=epst[:, 0:1])
            P.I('dve', 'reciprocal', [pfx + 'std'], [pfx + 'rstd'], out=rstd[:], in_=std[:])
            for b in range(4):
                if b % 2 == 0:
                    P.I('dve', 'tensor_scalar', [xkey, pfx + 'rstd'], [(xnkey, b)], out=xn[:, b, :], in0=xt[:, b, :],
                        scalar1=rstd[:, b:b + 1], scalar2=None, op0=ALU.mult)
                else:
                    P.I('act', 'activation', [xkey, pfx + 'rstd'], [(xnkey, b)], out=xn[:, b, :], in_=xt[:, b, :],
                        func=AF.Copy, scale=rstd[:, b:b + 1])

        def rms_T(xn, xnkey, nT, nTkey, gcols, psts):
            for kc in range(8):
                pt, pk = psts[kc % len(psts)]
                for b in range(4):
                    P.I('pe', 'transpose', [(xnkey, b)], [pk], out=pt[:, b * 128:(b + 1) * 128],
                        in_=xn[:, b, kc * 128:(kc + 1) * 128], identity=identb[:])
                P.I('act', 'activation', [pk], [(nTkey, kc)], out=nT[:, kc, :], in_=pt[:, 0:512],
                    func=AF.Copy, scale=gcols[:, kc:kc + 1])

        def rms_to_T(pfx, xt, xkey, junk, xn, xnkey, ssq, std, rstd, nT, nTkey, gcols):
            rms_N(pfx, xt, xkey, junk, xn, xnkey, ssq, std, rstd)
            rms_T(xn, xnkey, nT, nTkey, gcols, [(pst, ('pst', 0))])

        if 'A' in phases:
            with contextlib.ExitStack() as esA:
                xt1 = sb('a_xt', [128, 4, 1024], F32, esA)
                st32h = [sb('a_st32_%d' % i, [128, 4, 512], F32, esA) for i in range(2)]
                stbh = [sb('a_stb_%d' % i, [128, 4, 512], BF16, esA) for i in range(2)]
                bgc = [0]

                def bg_convert(n):
                    for _ in range(n):
                        if not bg_jobs:
                            return
                        nm, idx, src, hk, k0 = bg_jobs.pop(0)
                        sl = bgc[0] % 2
                        bgc[0] += 1
                        P.dma('sp', st32h[sl][:, 0:hk, :], src, ('a_st32', sl), [], [('a_st32', sl)])
                        P.I('pool', 'tensor_copy', [('a_st32', sl)], [('a_stb', sl)], out=stbh[sl][:, 0:hk, :], in_=st32h[sl][:, 0:hk, :])
                        dst = wsc[nm][idx].rearrange("p (k c) -> p k c", c=512)[:, k0:k0 + hk, :]
                        P.dma('pool', dst, stbh[sl][:, 0:hk, :], ('a_stbst', sl), [('a_stb', sl)], [])

                junk = sb('a_junk', [128, 1024], BF16, esA)
                xn = sb('a_xn', [128, 4, 1024], BF16, esA)
                nT = [sb('a_nT%d' % i, [128, 8, 512], BF16, esA) for i in range(2)]
                wres = [sb('a_w%d' % i, [128, 8, 512], BF16, esA) for i in range(6)]
                QTall = sb('a_QT', [64, 16, 512], BF16, esA)
                KTall = sb('a_KT', [64, 16, 512], BF16, esA)
                Vt = [sb('a_Vt%d' % i, [128, 16, 4, 65], BF16, esA) for i in range(2)]
                ssq = sb('a_ss', [128, 4], F32, esA)
                std = sb('a_std', [128, 4], F32, esA)
                rstd = sb('a_rstd', [128, 4], F32, esA)
                ffe = sb('a_ffe', [16, 512], F32, esA)
                ffl = sb('a_ffl', [16, 512], F32, esA)
                cg = [sb('a_cg%d' % i, [16, 4, 128], F32, esA) for i in range(2)]
                t32a = sb('a_t32a', [16, 512], F32, esA)
                t32b = sb('a_t32b', [16, 512], F32, esA)
                t32c = sb('a_t32c', [16, 512], F32, esA)
                afull = sb('a_afull', [16, 512], F32, esA)
                afr = sb('a_afr', [16, 512], F32, esA)
                kaug = [sb('a_kaug%d' % i, [16, 3, 512], BF16, esA) for i in range(2)]
                qaug = [sb('a_qaug%d' % i, [16, 2, 512], BF16, esA) for i in range(2)]
                ones3 = sb('a_ones3', [16, 3, 512], BF16, esA)

                P.I('dve', 'memset', [], ['a_ones3'], ap=ones3[:], constant=1.0)
                for s in range(2):
                    P.I('pool', 'memset', [], [('a_Vt', s, b_, h_) for b_ in range(4) for h_ in range(2)], ap=Vt[s][:], constant=1.0)
                P.dma('sp', xt1[:], x[0:512, :].rearrange("(b p) d -> p b d", p=128), 'a_x', [], ['a_x'])
                bank = [0]

                def nb():
                    bank[0] = (bank[0] + 1) % 6
                    return bank[0]

                apst = [(pst, ('pst', 0)), (pst2, ('pst', 1))]

                def normN(g):
                    rms_N('a_', xt1, 'a_x', junk, xn, 'a_xn', ssq, std, rstd)
                    if g + 1 < NG:
                        P.dma('sp', xt1[:], x[(g + 1) * 512:(g + 2) * 512, :].rearrange("(b p) d -> p b d", p=128), 'a_x', [], ['a_x'])

                def normT(g):
                    s_ = g % 2
                    rms_T(xn, 'a_xn', nT[s_], ('a_nT', s_), gmix, apst)
                    P.dma('pool', nTd[g].rearrange("p (k c) -> p k c", c=512), nT[s_][:], ('a_nTst', s_), [(('a_nT', s_), kc) for kc in range(8)], [])

                win_rA = w_in.rearrange("(kc p) c -> p kc c", p=128)
                cengA = ('dve', 'act')
                for kh in range(2):
                    P.dma('sp', st32h[kh][:], win_rA[:, kh * 4:(kh + 1) * 4, 10 * 512:11 * 512], ('a_st32', kh), [], [('a_st32', kh)])
                normN(0)
                normT(0)
                for i in (2, 3, 4, 5, 0, 1):
                    for kh in range(2):
                        sl = bgc[0] % 2
                        bgc[0] += 1
                        if not (i == 2):
                            P.dma('sp', st32h[sl][:], win_rA[:, kh * 4:(kh + 1) * 4, (8 + i) * 512:(9 + i) * 512], ('a_st32', sl), [], [('a_st32', sl)])
                        ce = cengA[bgc[0] % 2]
                        if ce == 'act':
                            P.I('act', 'activation', [('a_st32', sl)], [('aw', i)], out=wres[i][:, kh * 4:(kh + 1) * 4, :], in_=st32h[sl][:], func=AF.Copy)
                        else:
                            P.I(ce, 'tensor_copy', [('a_st32', sl)], [('aw', i)], out=wres[i][:, kh * 4:(kh + 1) * 4, :], in_=st32h[sl][:])
                for g in range(NG):
                    s = g % 2
                    gsl = slice(g * 512, (g + 1) * 512)
                    bg_convert(5)
                    nkeys = [(('a_nT', s), kc) for kc in range(8)]
                    for which, wbase, dst, dkey, scl in (('q', 0, QTall, 'a_QT', 0.125), ('k', 2, KTall, 'a_KT', 1.0)):
                        if which == 'q' and g < GF:
                            continue
                        for hp in range(8):
                            wi = wbase + hp // 4
                            off = (hp % 4) * 128
                            bi = nb()
                            for kc in range(8):
                                P.I('pe', 'matmul', [('aw', wi), (('a_nT', s), kc)], [('ps', bi)], out=ps[bi][:, :],
                                    lhsT=wres[wi][:, kc, off:off + 128], rhs=nT[s][:, kc, :], start=(kc == 0), stop=(kc == 7))
                            P.I('dve', 'tensor_scalar', [('ps', bi)], [(dkey, 2 * hp)], out=dst[0:64, 2 * hp, :], in0=ps[bi][0:64, :],
                                scalar1=scl, scalar2=None, op0=ALU.mult)
                            P.I('act', 'activation', [('ps', bi)], [(dkey, 2 * hp + 1)], out=dst[0:64, 2 * hp + 1, :],
                                in_=ps[bi][64:128, :], func=AF.Copy, scale=scl)
                    if g + 1 < NG:
                        normN(g + 1)
                    for b in range(4):
                        for half in range(2):
                            bi = nb()
                            for kc in range(8):
                                P.I('pe', 'matmul', [('aw', 4 + half), (('a_nT', s), kc)], [('ps', bi)], out=ps[bi][:, :],
                                    lhsT=nT[s][:, kc, b * 128:(b + 1) * 128], rhs=wres[4 + half][:, kc, :], start=(kc == 0), stop=(kc == 7))
                            e = 'dve' if half == 0 else 'act'
                            outv = Vt[s][:, half * 8:(half + 1) * 8, b, 0:64]
                            inv = ps[bi][:, :].rearrange("p (h d) -> p h d", d=64)
                            if e == 'dve':
                                P.I('dve', 'tensor_copy', [('ps', bi)], [('a_Vt', s, b, half)], out=outv, in_=inv)
                            else:
                                P.I('act', 'activation', [('ps', bi)], [('a_Vt', s, b, half)], out=outv, in_=inv, func=AF.Copy)
                    for b in range(4):
                        P.I('dve', 'tensor_scalar', ['valid', 'onesf'], [('a_Vt', s, b, 0), ('a_Vt', s, b, 1)], out=Vt[s][:, :, b, 64:65],
                            in0=onesf[:, 0:16].rearrange("p (h o) -> p h o", o=1), scalar1=valid[:, g * 4 + b:g * 4 + b + 1], scalar2=None, op0=ALU.mult)
                    bff = nb()
                    for kc in range(8):
                        P.I('pe', 'matmul', ['wffb', (('a_nT', s), kc)], [('ps', bff)], out=ps[bff][0:16, :], lhsT=wffb[:, kc, :],
                            rhs=nT[s][:, kc, :], start=(kc == 0), stop=(kc == 7))
                    P.I('act', 'activation', [('ps', bff), 'negfb'], ['a_ffe'], out=ffe[:], in_=ps[bff][0:16, :], func=AF.Exp,
                        scale=-1.0, bias=negfb[:, 0:1])
                    P.I('act', 'activation', ['a_ffe'], ['a_ffl'], out=ffl[:], in_=ffe[:], func=AF.Ln, scale=1.0, bias=onesf[0:16, 0:1])
                    cgf = cg[s][:].rearrange("p a b -> p (a b)")
                    init = 0.0 if g == 0 else cg[1 - s][:, 3, 127:128]
                    P.I('dve', 'tensor_tensor_scan', ['a_ffl', ('a_cg', 1 - s), 'onesf'], [('a_cg', s)], out=cgf, data0=onesf[0:16, :],
                        data1=ffl[:], initial=init, op0=ALU.mult, op1=ALU.subtract)
                    P.I('dve', 'tensor_scalar', [('a_cg', s)], ['a_t32a'], out=t32a[:], in0=cgf, scalar1=-1.0, scalar2=None, op0=ALU.mult)
                    P.I('dve', 'tensor_copy', ['a_t32a'], [('a_kaug0', s)], out=kaug[s][:, 0, :], in_=t32a[:])
                    P.I('dve', 'tensor_tensor', ['a_t32a', ('a_kaug0', s)], ['a_t32b'], out=t32b[:], in0=t32a[:], in1=kaug[s][:, 0, :], op=ALU.subtract)
                    P.I('dve', 'tensor_copy', ['a_t32b'], [('a_kaug1', s)], out=kaug[s][:, 1, :], in_=t32b[:])
                    P.I('dve', 'tensor_tensor', ['a_t32b', ('a_kaug1', s)], ['a_t32c'], out=t32c[:], in0=t32b[:], in1=kaug[s][:, 1, :], op=ALU.subtract)
                    P.I('dve', 'tensor_copy', ['a_t32c'], [('a_kaug2', s)], out=kaug[s][:, 2, :], in_=t32c[:])
                    for b in range(4 if g >= GF else 0):
                        P.I('dve', 'tensor_scalar', [('a_cg', s), 'onesf'], [('a_afull', b)], out=afull[:, b * 128:(b + 1) * 128],
                            in0=onesf[0:16, 0:128], scalar1=cg[s][:, b, 63:64], scalar2=None, op0=ALU.mult)
                    afk = [('a_afull', b) for b in range(4)]
                    if g >= GF:
                        P.I('dve', 'tensor_copy', afk, [('a_qaug0', s)], out=qaug[s][:, 0, :], in_=afull[:])
                        P.I('dve', 'tensor_tensor', afk + [('a_qaug0', s)], ['a_afr'], out=afr[:], in0=afull[:], in1=qaug[s][:, 0, :], op=ALU.subtract)
                        P.I('dve', 'tensor_copy', ['a_afr'], [('a_qaug1', s)], out=qaug[s][:, 1, :], in_=afr[:])
                        P.dma('pool', QTd[:, 0:64, gsl].rearrange("h r t -> r h t"), QTall[:], 'a_QTst', [('a_QT', h) for h in range(16)], [])
                        P.dma('pool', QTd[:, 64:66, gsl], qaug[s][:], ('a_qaugst', s), [('a_qaug0', s), ('a_qaug1', s)], [])
                        P.dma('pool', QTd[:, 66:69, gsl], ones3[:], 'a_o2st', ['a_ones3'], [])
                    P.dma('pool', KTd[:, 0:64, gsl].rearrange("h r t -> r h t"), KTall[:], 'a_KTst', [('a_KT', h) for h in range(16)], [])
                    P.dma('pool', KTd[:, 64:66, gsl], ones3[:, 0:2, :], 'a_o1st', ['a_ones3'], [])
                    P.dma('pool', KTd[:, 66:69, gsl], kaug[s][:], ('a_kaugst', s), [('a_kaug0', s), ('a_kaug1', s), ('a_kaug2', s)], [])
                    if g + 1 < NG:
                        normT(g + 1)
                    for hh in range(2):
                        P.dma('pool', Vd[hh * 8:(hh + 1) * 8, :, g * 4:(g + 1) * 4, :].rearrange("h p b d -> p h (b d)"),
                              Vt[s][:, hh * 8:(hh + 1) * 8, :, :].rearrange("p h b d -> p h (b d)"), ('a_Vst', s), [('a_Vt', s, b_, h_) for b_ in range(4) for h_ in range(2)], [])
                bg_convert(len(bg_jobs))
                P.barrier()
                P.emit()

        psA.close()
        if 'B' in phases:
            with contextlib.ExitStack() as esB:
                SS = [esB.enter_context(nc.psum_tensor('b_SS%d' % i, [128, 1024], F32)) for i in range(3)]
                PO = [esB.enter_context(nc.psum_tensor('b_PO%d' % i, [128, 512], F32)) for i in range(2)]
                KT = [sb('b_KT%d' % i, [69, T], BF16, esB) for i in range(2)]
                QT = [sb('b_QT%d' % i, [69, T], BF16, esB) for i in range(2)]
                V = [sb('b_V%d' % i, [128, NBLK, 65], BF16, esB) for i in range(2)]
                PP = [sb('b_P%d' % i, [128, 1024], BF16, esB) for i in range(3)]
                osb = [sb('b_osb%d' % i, [65, 512], F32, esB) for i in range(2)]
                rden = [sb('b_rden%d' % i, [65, 512], F32, esB) for i in range(2)]
                OT = [sb('b_OT%d' % i, [64, 512], BF16, esB) for i in range(2)]
                rot = [0]
                fin = [0]
                deferred = []
                LOOK = 2

                def loadhead(h):
                    s = h % 2
                    P.dma('sp', KT[s][:], KTd[h], ('b_KT', s), [], [('b_KT', s)])
                    P.dma('sp', QT[s][:, GF * 512:], QTd[h, :, GF * 512:], ('b_QT', s), [], [('b_QT', s)])
                    P.dma('pool', V[s][:], Vd[h], ('b_V', s), [], [('b_V', s)])

                gcount = {}

                def goi(h, g):
                    return (h * (NG - GF) + (g - GF)) % 2

                DEFER = max(1, min(8, 2 * GF + 1))

                def qk(h, g, p):
                    s = h % 2
                    ri = rot[0] % 3
                    rot[0] += 1
                    offs = []
                    halo = (NGP >= 1 and g == GF and p >= 1)
                    for half in range(2):
                        kb = 2 * p + half
                        j = kb - 4 * g
                        off = 384 if halo else max(0, j) * 128
                        need_mask = (j == 3) if halo else (j >= 0)
                        offs.append(off)
                        c0 = half * 512
                        P.I('pe', 'matmul', [('b_KT', s), ('b_QT', s)], [('b_ss', ri)], out=SS[ri][:, c0 + off:c0 + 512],
                            lhsT=KT[s][:, kb * 128:(kb + 1) * 128], rhs=QT[s][:, g * 512 + off:(g + 1) * 512], start=True, stop=(not need_mask))
                        if need_mask:
                            P.I('pe', 'matmul', ['identb', 'mnegb'], [('b_ss', ri)], out=SS[ri][:, c0 + off:c0 + off + 128], lhsT=identb[:],
                                rhs=mnegb[:], start=False, stop=True)
                    if halo:
                        P.I('act', 'activation', [('b_ss', ri)], [('b_P', ri)], out=PP[ri][:, :].rearrange("p (h c) -> p h c", c=512)[:, :, 384:512],
                            in_=SS[ri][:, :].rearrange("p (h c) -> p h c", c=512)[:, :, 384:512], func=AF.Exp)
                    elif 2 * p + 1 < 4 * g:
                        P.I('act', 'activation', [('b_ss', ri)], [('b_P', ri)], out=PP[ri][:, :], in_=SS[ri][:, :], func=AF.Exp)
                    else:
                        for half in range(2):
                            c0 = half * 512 + offs[half]
                            c1 = (half + 1) * 512
                            P.I('act', 'activation', [('b_ss', ri)], [('b_P', ri)], out=PP[ri][:, c0:c1], in_=SS[ri][:, c0:c1], func=AF.Exp)
                    return (h, g, p, ri, offs)

                def pv(h, g, p, ri, offs):
                    s = h % 2
                    oi = goi(h, g)
                    nkb = 4 * g + 4
                    for half in range(2):
                        kb = 2 * p + half
                        off = offs[half]
                        c0 = half * 512
                        P.I('pe', 'matmul', [('b_V', s), ('b_P', ri)], [('b_po', oi)], out=PO[oi][0:65, off:512],
                            lhsT=V[s][:, kb, :], rhs=PP[ri][:, c0 + off:c0 + 512], start=(kb == 0), stop=(kb == nkb - 1))
                    if p == nkb // 2 - 1:
                        finalize(h, g)

                def finalize(h, g):
                    oi = goi(h, g)
                    fi = fin[0] % 2
                    fin[0] += 1
                    P.I('dve', 'tensor_copy', [('b_po', oi)], [('b_osb', fi)], out=osb[fi][:], in_=PO[oi][0:65, :])
                    P.I('dve', 'tensor_scalar', [('b_osb', fi)], [('b_rden0', fi)], out=rden[fi][64:65, :], in0=osb[fi][64:65, :],
                        scalar1=1e-35, scalar2=None, op0=ALU.max)
                    P.I('dve', 'reciprocal', [('b_rden0', fi)], [('b_rden', fi)], out=rden[fi][64:65, :], in_=rden[fi][64:65, :])

                    def part2():
                        P.I('pe', 'matmul', ['onesf', ('b_rden', fi)], [('b_po', oi)], out=PO[oi][0:64, 0:512], lhsT=onesf[64:65, 0:64],
                            rhs=rden[fi][64:65, :], start=True, stop=True)
                        P.I('dve', 'tensor_tensor', [('b_po', oi), ('b_osb', fi)], [('b_OT', fi)], out=OT[fi][:], in0=osb[fi][0:64, :],
                            in1=PO[oi][0:64, 0:512], op=ALU.mult)
                        P.dma('pool', OTd[h, :, g * 512:(g + 1) * 512], OT[fi][:], ('b_OTst', fi), [('b_OT', fi)], [])

                    deferred.append([DEFER, part2])

                def tick():
                    for d in deferred:
                        d[0] -= 1
                    while deferred and deferred[0][0] <= 0:
                        deferred.pop(0)[1]()

                loadhead(0)
                pend = []
                for h in range(16):
                    cnt = 0
                    for g in range(GF, NG):
                        for p in range((4 * g + 4) // 2):
                            pend.append(qk(h, g, p))
                            if len(pend) > LOOK:
                                pv(*pend.pop(0))
                            tick()
                            cnt += 1
                            if cnt == LOOK + 1 and h + 1 < 16:
                                loadhead(h + 1)
                while pend:
                    pv(*pend.pop(0))
                    tick()
                while deferred:
                    deferred.pop(0)[1]()
                P.barrier()
                P.emit()

        if 'C' in phases:
            with contextlib.ExitStack() as esC:
                ps = [esC.enter_context(nc.psum_tensor('psc%d' % i, [128, 512], F32)) for i in range(7)]
                pst = esC.enter_context(nc.psum_tensor('pstc', [128, 1024], BF16))
                xt = sb('c_xt', [128, 4, 1024], F32, esC)
                bAs = [sb('c_bA%d' % i, [128, 8, 512], BF16, esC) for i in range(2)]
                bB = sb('c_bB', [128, 8, 512], BF16, esC)
                bC = sb('c_bC', [128, 4, 1024], BF16, esC)
                bD = sb('c_bD', [128, 8, 512], BF16, esC)
                bE = sb('c_bE', [128, 8, 512], BF16, esC)
                actT = sb('c_act', [128, 22, 512], BF16, esC)
                wb_ = [sb('c_w%d' % i, [128, 8, 512], BF16, esC) for i in range(5)]
                tmp = [sb('c_tmp%d' % i, [128, 516], F32, esC) for i in range(12)]
                sa = sb('c_sa', [128, 8, 512], BF16, esC)
                sbb = sb('c_sb', [128, 8, 512], BF16, esC)
                S = sb('c_S', [128, 8, 128], F32, esC)
                S1 = sb('c_S1', [128, 8, 128], F32, esC)
                ktok = sb('c_ktok', [128, 8, 128], BF16, esC)
                At = sb('c_At', [128, 8, 128], BF16, esC)
                Sp = sb('c_Sp', [128, 8, 128], BF16, esC)
                m01x4 = sb('c_m01x4', [128, 4, 128], F32, esC)
                rmask = sb('c_rmask', [128, 512], F32, esC)
                dc = sb('c_dc', [128, 8, 6, 4], F32, esC)
                sq = [sb('c_sq%d' % i, [128, 512], BF16, esC) for i in range(4)]
                junk = sb('c_junk', [128, 1024], BF16, esC)
                ssq = sb('c_ss', [128, 4], F32, esC)
                std = sb('c_std', [128, 4], F32, esC)
                rstd = sb('c_rstd', [128, 4], F32, esC)
                carry = sb('c_carry', [128, 44, 2], F32, esC)

                P.I('dve', 'memset', [], [('c_S', h) for h in range(8)], ap=S[:], constant=0.0)
                for i in range(4):
                    P.I('dve', 'tensor_copy', ['m01'], ['m01x4'], out=m01x4[:, i, :], in_=m01[:])
                P.I('dve', 'memset', [], ['rmask'], ap=rmask[:], constant=1.0)
                P.I('dve', 'memset', ['rmask'], ['rmask'], ap=rmask[:].rearrange("p (a c) -> p a c", c=128)[:, :, 0:1], constant=0.0)
                P.I('pool', 'memset', [], [('c_carry', ct) for ct in range(44)], ap=carry[:], constant=0.0)

                sched_g = []
                for i in (0, 2, 4, 5, 1, 3, 6, 7):
                    sched_g.append(('win', i, 8))
                sched_g += [('win', 14, 8), ('win', 15, 8), ('win', 16, 8), ('win', 17, 8), ('wb', 0, 8), ('wb', 1, 8), ('wa', 0, 8), ('wa', 1, 8)]
                sched_g += [('wo', 0, 8), ('wo', 1, 8)]
                upseq = []
                for j in range(22):
                    for pc in (j // 4, (22 + j) // 4):
                        if pc not in upseq[-3:]:
                            upseq.append(pc)
                sched_g += [('wup', pc, 8) for pc in upseq]
                for half in range(2):
                    for kg in range(3):
                        sched_g.append(('wdn', kg * 2 + half, 8 if kg < 2 else 6))
                sched_p = [('win', 2, 8), ('win', 4, 8), ('win', 3, 8), ('win', 5, 8)]
                sched_h = [e for e in sched_g if e[0] != 'wdn']
                if NGP >= 1:
                    W = WStream(P, wb_, wsc, sched_p * GF + sched_h + sched_g * (NG - GF - 1), res=3)
                else:
                    W = WStream(P, wb_, wsc, sched_g * NG, res=3)
                bank = [0]

                def nb():
                    bank[0] = (bank[0] + 1) % 7
                    return bank[0]

                for g in range(NG):
                    gsl = slice(g * 512, (g + 1) * 512)
                    full = g >= GF
                    bA = bAs[g % 2]
                    bAk = ('c_bA', g % 2)
                    if g == 0:
                        P.dma('pool', bA[:], nTd[0].rearrange("p (k c) -> p k c", c=512), ('c_nT', 0), [], [bAk])
                    if g + 1 < NG:
                        P.dma('pool', bAs[(g + 1) % 2][:], nTd[g + 1].rearrange("p (k c) -> p k c", c=512), ('c_nT', (g + 1) % 2), [], [('c_bA', (g + 1) % 2)])
                    if full:
                        P.dma('pool', xt[:], x[gsl, :].rearrange("(b p) d -> p b d", p=128), 'c_x', [], ['c_xt'])
                    if full:
                        P.dma('pool', bB[:], OTd[:, :, gsl].rearrange("(j two) d t -> (two d) j t", two=2), 'c_oT', [], ['c_bB'])
                    jobs = []
                    for half in range(2):
                        for b in range(4):
                            def vjob(half=half, b=b):
                                wt, wk = W.get('win', 4 + half)
                                bi = nb()
                                for kc in range(8):
                                    P.I('pe', 'matmul', [wk, bAk], [('ps', bi)], out=ps[bi][:, :], lhsT=bA[:, kc, b * 128:(b + 1) * 128],
                                        rhs=wt[:, kc, :], start=(kc == 0), stop=(kc == 7))
                                if b % 2 == 0:
                                    P.I('dve', 'tensor_copy', [('ps', bi)], [('c_bC', b, half)], out=bC[:, b, half * 512:(half + 1) * 512], in_=ps[bi][:, :])
                                else:
                                    P.I('act', 'activation', [('ps', bi)], [('c_bC', b, half)], out=bC[:, b, half * 512:(half + 1) * 512], in_=ps[bi][:, :], func=AF.Copy)
                            jobs.append(vjob)
                    for half in range(2 if full else 0):
                        for hh in range(4):
                            def gjob(half=half, hh=hh):
                                wt, wk = W.get('win', 6 + half)
                                h = half * 4 + hh
                                bi = nb()
                                for kc in range(8):
                                    P.I('pe', 'matmul', [wk, bAk], [('ps', bi)], out=ps[bi][:, :], lhsT=wt[:, kc, hh * 128:(hh + 1) * 128],
                                        rhs=bA[:, kc, :], start=(kc == 0), stop=(kc == 7))
                                P.I('act', 'activation', [('ps', bi)], ['c_bD'], out=bD[:, h, :], in_=ps[bi][:, :], func=AF.Silu)
                            jobs.append(gjob)
                    gate_jobs = []
                    for gi, (pbase, dstt, dkey) in enumerate(((14, sa, 'c_sa'), (16, sbb, 'c_sb')) if full else ()):
                        for c in range(8):
                            def job(pbase=pbase, dstt=dstt, dkey=dkey, c=c):
                                wt, wk = W.get('win', pbase + c // 4)
                                bi = nb()
                                for kc in range(8):
                                    P.I('pe', 'matmul', [wk, bAk], [('ps', bi)], out=ps[bi][:, :], lhsT=wt[:, kc, (c % 4) * 128:(c % 4 + 1) * 128],
                                        rhs=bA[:, kc, :], start=(kc == 0), stop=(kc == 7))
                                P.I('act', 'activation', [('ps', bi)], [(dkey, c)], out=dstt[:, c, :], in_=ps[bi][:, :], func=AF.Sigmoid)
                            gate_jobs.append(job)
                    for _ in range(6 if full else 0):
                        jobs.append(gate_jobs.pop(0))
                    fillplan = {0: [2, 1, 1, 1, 1, 1, 1], 1: [2, 2, 2, 2, 2, 2, 2]} if full else {0: [1, 1, 1, 1, 0, 0, 0], 1: [1, 1, 1, 1, 0, 0, 0]}

                    def fill(hb, step):
                        for _ in range(fillplan[hb][step]):
                            if jobs:
                                jobs.pop(0)()

                    for hb in range(2):
                        hs = [hb * 4 + i for i in range(4)]
                        Tt = {h: [tmp[(h % 4) * 3 + i][:, 0:512] for i in range(3)] for h in hs}
                        Tk = {h: ['c_tmp%d' % ((h % 4) * 3 + i) for i in range(3)] for h in hs}
                        T3b = {h: tmp[(h % 4) * 3 + 2][:, 0:512].rearrange("p (a c) -> p a c", c=128) for h in hs}
                        for h in hs:
                            off = (h % 4) * 128
                            if full:
                                wq_, wqk = W.get('win', 0 + h // 4)
                                bq = nb()
                                for kc in range(8):
                                    P.I('pe', 'matmul', [wqk, bAk], [('ps', bq)], out=ps[bq][:, :], lhsT=wq_[:, kc, off:off + 128], rhs=bA[:, kc, :],
                                        start=(kc == 0), stop=(kc == 7))
                                P.I('act', 'activation', [('ps', bq)], [('c_act', h)], out=actT[:, h, :], in_=ps[bq][:, :], func=AF.Silu)
                            wf_, wfk = W.get('win', 2 + h // 4)
                            bf = nb()
                            for kc in range(8):
                                P.I('pe', 'matmul', [wfk, bAk], [('ps', bf)], out=ps[bf][:, :], lhsT=wf_[:, kc, off:off + 128], rhs=bA[:, kc, :],
                                    start=(kc == 0), stop=(kc == 7))
                            P.I('act', 'activation', [('ps', bf)], [Tk[h][0]], out=Tt[h][0], in_=ps[bf][:, :], func=AF.Sigmoid)
                        for h in hs:
                            P.I('dve', 'tensor_scalar', [Tk[h][0], 'oml', 'lb'], [Tk[h][0]], out=Tt[h][0], in0=Tt[h][0], scalar1=oml[:, h:h + 1],
                                scalar2=lb[:, h:h + 1], op0=ALU.mult, op1=ALU.add)
                        fill(hb, 0)
                        for h in hs:
                            P.I('act', 'activation', [Tk[h][0]], [Tk[h][1]], out=Tt[h][1], in_=Tt[h][0], func=AF.Ln)
                            P.I('dve', 'tensor_scalar', [Tk[h][0]], [('c_act', 8 + h)], out=actT[:, 8 + h, :], in0=Tt[h][0], scalar1=-1.0, scalar2=1.0,
                                op0=ALU.mult, op1=ALU.add)
                        fill(hb, 1)
                        for h in hs:
                            P.I('dve', 'tensor_tensor_scan', [Tk[h][1], 'rmask'], [Tk[h][2]], out=Tt[h][2], data0=rmask[:, :], data1=Tt[h][1], initial=0.0,
                                op0=ALU.mult, op1=ALU.add)
                        fill(hb, 2)
                        for h in hs:
                            dk = ('c_dc', h)
                            P.I('dve', 'tensor_copy', [Tk[h][2]], [(dk, 0)], out=dc[:, h, 0, :], in_=T3b[h][:, :, 63])
                            P.I('dve', 'tensor_copy', [Tk[h][2]], [(dk, 5)], out=dc[:, h, 5, :], in_=T3b[h][:, :, 127])
                            P.I('dve', 'tensor_tensor', [Tk[h][2], (dk, 0)], [Tk[h][2]], out=T3b[h], in0=T3b[h],
                                in1=dc[:, h, 0, :].unsqueeze(2).to_broadcast([128, 4, 128]), op=ALU.subtract)
                        fill(hb, 3)
                        for h in hs:
                            if full:
                                P.I('act', 'activation', [Tk[h][2]], [Tk[h][1]], out=Tt[h][1], in_=Tt[h][2], func=AF.Exp)
                            P.I('act', 'activation', [Tk[h][2]], [Tk[h][2]], out=Tt[h][2], in_=Tt[h][2], func=AF.Exp, scale=-1.0)
                        fill(hb, 4)
                        for h in hs:
                            if full:
                                P.I('dve', 'tensor_tensor', [('c_act', h), Tk[h][1]], [('c_act', h)], out=actT[:, h, :], in0=actT[:, h, :], in1=Tt[h][1], op=ALU.mult)
                            P.I('dve', 'tensor_tensor', [('c_act', 8 + h), Tk[h][2]], [('c_act', 8 + h)], out=actT[:, 8 + h, :], in0=actT[:, 8 + h, :], in1=Tt[h][2],
                                op=ALU.mult)
                        fill(hb, 5)
                        for h in hs:
                            dk = ('c_dc', h)
                            P.I('dve', 'tensor_tensor', [(dk, 5), (dk, 0)], [(dk, 3)], out=dc[:, h, 3, :], in0=dc[:, h, 5, :], in1=dc[:, h, 0, :], op=ALU.subtract)
                            if full:
                                P.I('act', 'activation', [(dk, 0)], [(dk, 1)], out=dc[:, h, 1, :], in_=dc[:, h, 0, :], func=AF.Exp)
                            P.I('act', 'activation', [(dk, 5)], [(dk, 2)], out=dc[:, h, 2, :], in_=dc[:, h, 5, :], func=AF.Exp)
                            P.I('act', 'activation', [(dk, 3)], [(dk, 4)], out=dc[:, h, 4, :], in_=dc[:, h, 3, :], func=AF.Exp)
                        fill(hb, 6)
                    while jobs:
                        jobs.pop(0)()
                    for b in range(4):
                        bs = slice(b * 128, (b + 1) * 128)
                        for h in range(8):
                            P.I('pe', 'transpose', [('c_act', 8 + h)], [('pst', 0)], out=pst[:, h * 128:(h + 1) * 128], in_=actT[:, 8 + h, bs], identity=identb[:])
                        P.I('act', 'activation', [('pst', 0)], ['c_ktok'], out=ktok[:].rearrange("p h c -> p (h c)"), in_=pst[:, :], func=AF.Copy)
                        for half in range(2 if full else 0):
                            bx = nb()
                            for hh in range(4):
                                h = half * 4 + hh
                                P.I('pe', 'matmul', [('c_act', 8 + h), ('c_act', h)], [('ps', bx)], out=ps[bx][:, hh * 128:(hh + 1) * 128],
                                    lhsT=actT[:, 8 + h, bs], rhs=actT[:, h, bs], start=True, stop=True)
                            P.I('dve', 'tensor_tensor', [('ps', bx), 'm01x4'], [('c_At', half)], out=At[:, half * 4:(half + 1) * 4, :],
                                in0=ps[bx][:, :].rearrange("p (h c) -> p h c", c=128), in1=m01x4[:], op=ALU.mult)
                        for h in range(8):
                            if full:
                                P.I('act', 'activation', [('c_S', h), (('c_dc', h), 1)], [('c_Sp', h)], out=Sp[:, h, :], in_=S[:, h, :], func=AF.Copy,
                                    scale=dc[:, h, 1, b:b + 1])
                            P.I('pool', 'tensor_scalar', [('c_S', h), (('c_dc', h), 2)], [('c_S1', h)], out=S1[:, h, :], in0=S[:, h, :],
                                scalar1=dc[:, h, 2, b:b + 1], scalar2=1.0, op0=ALU.mult, op1=ALU.mult)
                        for _ in range((3 if b < 2 else 2) if full else 0):
                            gate_jobs.pop(0)()
                        for half in range(2 if full else 0):
                            by = nb()
                            for hh in range(4):
                                h = half * 4 + hh
                                vv = bC[:, b, h * 128:(h + 1) * 128]
                                P.I('pe', 'matmul', [('c_bC', b, half), ('c_At', half)], [('ps', by)], out=ps[by][:, hh * 128:(hh + 1) * 128],
                                    lhsT=vv, rhs=At[:, h, :], start=True, stop=False)
                                P.I('pe', 'matmul', [('c_Sp', h), ('c_act', h)], [('ps', by)], out=ps[by][:, hh * 128:(hh + 1) * 128],
                                    lhsT=Sp[:, h, :], rhs=actT[:, h, bs], start=False, stop=True)
                            oute = bE[:, half * 4:(half + 1) * 4, bs]
                            ine = ps[by][:, :].rearrange("p (h c) -> p h c", c=128)
                            if half == 0:
                                P.I('act', 'activation', [('ps', by)], [('c_bEr', half, b)], out=oute, in_=ine, func=AF.Copy)
                            else:
                                P.I('dve', 'tensor_copy', [('ps', by)], [('c_bEr', half, b)], out=oute, in_=ine)
                        for half in range(2):
                            bz = nb()
                            for hh in range(4):
                                h = half * 4 + hh
                                vv = bC[:, b, h * 128:(h + 1) * 128]
                                P.I('pe', 'matmul', ['c_ktok', ('c_bC', b, half)], [('ps', bz)], out=ps[bz][:, hh * 128:(hh + 1) * 128],
                                    lhsT=ktok[:, h, :], rhs=vv, start=True, stop=True)
                            for hh in range(4):
                                h = half * 4 + hh
                                P.I('dve', 'scalar_tensor_tensor', [('ps', bz), (('c_dc', h), 4), ('c_S1', h)], [('c_S', h)], out=S[:, h, :],
                                    in0=ps[bz][:, hh * 128:(hh + 1) * 128], scalar=dc[:, h, 4, b:b + 1], in1=S1[:, h, :], op0=ALU.mult, op1=ALU.add)
                    if not full:
                        continue
                    def norm_batch(hb):
                        hs = [hb * 4 + i for i in range(4)]
                        ek = {h: [('c_bEr', h // 4, b) for b in range(4)] for h in hs}
                        bns = {}
                        for h in hs:
                            P.I('act', 'activation', ek[h], [('c_sq', h % 4)], out=sq[h % 4][:], in_=bE[:, h, :], func=AF.Square)
                        for h in hs:
                            bns[h] = nb()
                            P.I('pe', 'matmul', ['onesb', ('c_sq', h % 4)], [('ps', bns[h])], out=ps[bns[h]][:, :], lhsT=onesb[:], rhs=sq[h % 4][:], start=True, stop=True)
                        for h in hs:
                            P.I('act', 'activation', [('ps', bns[h])], ['c_tmp%d' % (h % 4)], out=tmp[h % 4][:, 0:512], in_=ps[bns[h]][:, :], func=AF.Ln,
                                scale=1.0 / 128.0, bias=epst[:, 0:1])
                        for h in hs:
                            P.I('act', 'activation', ['c_tmp%d' % (h % 4)], ['c_tmp%d' % (h % 4)], out=tmp[h % 4][:, 0:512], in_=tmp[h % 4][:, 0:512], func=AF.Exp,
                                scale=-0.5)
                        for h in hs:
                            P.I('dve', 'tensor_tensor', ek[h] + ['c_tmp%d' % (h % 4)], ['c_tmp%d' % (4 + h % 4)], out=tmp[4 + h % 4][:, 0:512], in0=bE[:, h, :],
                                in1=tmp[h % 4][:, 0:512], op=ALU.mult)
                        for h in hs:
                            P.I('dve', 'scalar_tensor_tensor', ['c_tmp%d' % (4 + h % 4), 'gnorm', 'c_bD'], [('c_bE', h)], out=bE[:, h, :], in0=tmp[4 + h % 4][:, 0:512],
                                scalar=gnorm[:, 0:1], in1=bD[:, h, :], op0=ALU.mult, op1=ALU.mult)
                    bEk = [('c_bE', h) for h in range(8)]
                    def wb_half(half):
                        wt, wk = W.get('wb', half)
                        for c in range(4):
                            cc = half * 4 + c
                            bi = nb()
                            for kc in range(8):
                                P.I('pe', 'matmul', [wk, 'c_bB'], [('ps', bi)], out=ps[bi][:, :], lhsT=wt[:, kc, c * 128:(c + 1) * 128], rhs=bB[:, kc, :],
                                    start=(kc == 0), stop=(kc == 7))
                            P.I('dve', 'tensor_tensor', [('ps', bi), ('c_sb', cc)], [('c_sb', cc)], out=sbb[:, cc, :], in0=ps[bi][:, :], in1=sbb[:, cc, :], op=ALU.mult)
                    wb_half(0)
                    norm_batch(0)
                    wb_half(1)
                    norm_batch(1)
                    for half in range(2):
                        wt, wk = W.get('wa', half)
                        for c in range(4):
                            cc = half * 4 + c
                            bi = nb()
                            for kc in range(8):
                                P.I('pe', 'matmul', [wk] + bEk, [('ps', bi)], out=ps[bi][:, :], lhsT=wt[:, kc, c * 128:(c + 1) * 128], rhs=bE[:, kc, :],
                                    start=(kc == 0), stop=(kc == 7))
                            P.I('dve', 'tensor_tensor', [('ps', bi), ('c_sa', cc)], [('c_sa', cc)], out=sa[:, cc, :], in0=ps[bi][:, :], in1=sa[:, cc, :], op=ALU.mult)
                            P.I('pool', 'tensor_tensor', [('c_sb', cc), ('c_sa', cc)], ['c_bD'], out=bD[:, cc, :], in0=sbb[:, cc, :], in1=sa[:, cc, :], op=ALU.add)
                    for half in range(2):
                        wt, wk = W.get('wo', half)
                        for b in range(4):
                            bi = nb()
                            for kc in range(8):
                                P.I('pe', 'matmul', [wk, 'c_bD'], [('ps', bi)], out=ps[bi][:, :], lhsT=bD[:, kc, b * 128:(b + 1) * 128], rhs=wt[:, kc, :],
                                    start=(kc == 0), stop=(kc == 7))
                            P.I('dve', 'tensor_tensor', [('ps', bi), 'c_xt'], ['c_xt'], out=xt[:, b, half * 512:(half + 1) * 512], in0=ps[bi][:, :],
                                in1=xt[:, b, half * 512:(half + 1) * 512], op=ALU.add)
                    rms_to_T('c_', xt, 'c_xt', junk, bC, 'c_bCn', ssq, std, rstd, bB, 'c_bBn', gffn)
                    n2keys = [('c_bBn', kc) for kc in range(8)] + ['c_bB']
                    if NGP >= 1 and g == GF:
                        for j in range(22):
                            for ct in (j, 22 + j):
                                wt, wk = W.get('wup', ct // 4)
                                off = (ct % 4) * 128
                                bi = nb()
                                for kc in range(8):
                                    P.I('pe', 'matmul', [wk] + n2keys, [('ps', bi)], out=ps[bi][:, 0:128], lhsT=wt[:, kc, off:off + 128], rhs=bB[:, kc, 384:512],
                                        start=(kc == 0), stop=(kc == 7))
                                P.I('dve', 'tensor_copy', [('ps', bi)], [('c_carry', ct)], out=carry[:, ct, :], in_=ps[bi][:, 126:128])
                        continue
                    for jb in range(0, 22, 4):
                        js = list(range(jb, min(jb + 4, 22)))
                        for j in js:
                            for which, ct in (('g', j), ('v', 22 + j)):
                                wt, wk = W.get('wup', ct // 4)
                                off = (ct % 4) * 128
                                bi = nb()
                                for kc in range(8):
                                    P.I('pe', 'matmul', [wk] + n2keys, [('ps', bi)], out=ps[bi][:, :], lhsT=wt[:, kc, off:off + 128], rhs=bB[:, kc, :],
                                        start=(kc == 0), stop=(kc == 7))
                                ai = (j % 4) + (0 if which == 'g' else 4)
                                ac, ak = tmp[ai], 'c_tmp%d' % ai
                                ck = ('c_carry', ct)
                                P.I('act', 'activation', [('ps', bi), 'cw', 'cb'], [ak], out=ac[:, 0:512], in_=ps[bi][:, :], func=AF.Identity,
                                    scale=cw[:, ct, 2:3], bias=cb[:, ct:ct + 1])
                                P.I('dve', 'scalar_tensor_tensor', [('ps', bi), ak, 'cw'], [ak], out=ac[:, 1:512], in0=ps[bi][:, 0:511], scalar=cw[:, ct, 1:2],
                                    in1=ac[:, 1:512], op0=ALU.mult, op1=ALU.add)
                                P.I('dve', 'scalar_tensor_tensor', [('ps', bi), ak, 'cw'], [ak], out=ac[:, 2:512], in0=ps[bi][:, 0:510], scalar=cw[:, ct, 0:1],
                                    in1=ac[:, 2:512], op0=ALU.mult, op1=ALU.add)
                                P.I('dve', 'scalar_tensor_tensor', [ck, ak, 'cw'], [ak], out=ac[:, 0:2], in0=carry[:, ct, 0:2], scalar=cw[:, ct, 0:1],
                                    in1=ac[:, 0:2], op0=ALU.mult, op1=ALU.add)
                                P.I('dve', 'scalar_tensor_tensor', [ck, ak, 'cw'], [ak], out=ac[:, 0:1], in0=carry[:, ct, 1:2], scalar=cw[:, ct, 1:2],
                                    in1=ac[:, 0:1], op0=ALU.mult, op1=ALU.add)
                                P.I('dve', 'tensor_copy', [('ps', bi)], [ck], out=carry[:, ct, :], in_=ps[bi][:, 510:512])
                        for j in js:
                            gk, vk = 'c_tmp%d' % (j % 4), 'c_tmp%d' % (4 + j % 4)
                            P.I('act', 'activation', [gk], [gk], out=tmp[j % 4][:, 0:512], in_=tmp[j % 4][:, 0:512], func=AF.Gelu)
                            P.I('pool', 'tensor_tensor', [gk, vk], [('c_act', j)], out=actT[:, j, :], in0=tmp[j % 4][:, 0:512], in1=tmp[4 + j % 4][:, 0:512], op=ALU.mult)
                    for half in range(2):
                        banks = [nb() for _ in range(4)]
                        for kg in range(3):
                            nk = 8 if kg < 2 else 6
                            wt, wk = W.get('wdn', kg * 2 + half)
                            for b in range(4):
                                for kc in range(nk):
                                    j = kg * 8 + kc
                                    P.I('pe', 'matmul', [wk, ('c_act', j)], [('ps', banks[b])], out=ps[banks[b]][:, :], lhsT=actT[:, j, b * 128:(b + 1) * 128],
                                        rhs=wt[:, kc, :], start=(j == 0), stop=(j == 21))
                        for b in range(4):
                            P.I('dve', 'tensor_tensor', [('ps', banks[b]), 'c_xt'], ['c_xt'], out=xt[:, b, half * 512:(half + 1) * 512], in0=ps[banks[b]][:, :],
                                in1=xt[:, b, half * 512:(half + 1) * 512], op=ALU.add)
                    for b in range(4):
                        P.I('act', 'activation', ['c_xt'], ['c_junk', ('c_ss', b)], out=junk[:], in_=xt[:, b, :], func=AF.Square, accum_out=ssq[:, b:b + 1])
                    P.I('act', 'activation', [('c_ss', b) for b in range(4)], ['c_std'], out=std[:], in_=ssq[:], func=AF.Sqrt, scale=1.0 / 1024.0,
                        bias=epst[:, 0:1])
                    P.I('dve', 'reciprocal', ['c_std'], ['c_rstd'], out=rstd[:], in_=std[:])
                    for b in range(4):
                        P.I('dve', 'scalar_tensor_tensor', ['c_xt', 'c_rstd', 'gfinb'], ['c_xt'], out=xt[:, b, :], in0=xt[:, b, :], scalar=rstd[:, b:b + 1],
                            in1=gfinb[:], op0=ALU.mult, op1=ALU.mult)
                    if g >= NGP:
                        P.dma('pool', y[(g - NGP) * 512:(g - NGP + 1) * 512, :].rearrange("(b p) d -> p b d", p=128), xt[:], 'c_yst', ['c_xt'], [])
                P.barrier()
                P.emit()
        ninstr = P.n
    return nc, ninstr


_CACHE = {}


def _host_consts():
    s = np.arange(128)[:, None]
    t = np.arange(128)[None, :]
    m01 = (s <= t).astype(np.float32)
    mneg = np.where(s <= t, 0.0, -30000.0).astype(np.float32)
    return np.eye(128, dtype=np.float32), mneg, m01


def make_in_maps(inputs):
    f = lambda a: np.ascontiguousarray(np.asarray(a, dtype=np.float32))
    x = f(inputs['x'])
    ident, mneg, m01 = _host_consts()
    shared = {
        'w_in': f(inputs['w_in'][0]), 'w_a': f(inputs['w_branch_a'][0]), 'w_b': f(inputs['w_branch_b'][0]),
        'w_o': f(inputs['w_out'][0]), 'w_up': f(inputs['w_up'][0]), 'w_dn': f(inputs['w_down'][0]),
        'gmix': f(np.asarray(inputs['norm_mix'])[0].reshape(8, 128).T),
        'gffn': f(np.asarray(inputs['norm_ffn'])[0].reshape(8, 128).T),
        'gfinb': f(np.broadcast_to(np.asarray(inputs['norm_final']).reshape(1, 1024), (128, 1024))),
        'ffb': f(np.asarray(inputs['fox_f_bias'])[0].reshape(16, 1)),
        'lbl': f(np.asarray(inputs['hg_lb_logits']).reshape(2, 8, 128).transpose(2, 0, 1)),
        'gnorm': f(np.asarray(inputs['hg_norm'])[0].reshape(128, 1)),
        'cw': f(np.asarray(inputs['conv_w'])[0].reshape(3, 44, 128).transpose(2, 1, 0)),
        'cb': f(np.asarray(inputs['conv_b'])[0].reshape(44, 128).T),
        'ident': ident, 'mneg': mneg, 'm01': m01,
    }
    maps = []
    T = x.shape[1]
    H = T // 2
    for c in range(8):
        m = dict(shared)
        b, half = c % 4, c // 4
        if half == 0:
            xl = np.zeros((T, 1024), np.float32)
            xl[H:] = x[b, :H]
            vl = np.zeros(T, np.float32)
            vl[H:] = 1.0
        else:
            xl = x[b]
            vl = np.ones(T, np.float32)
        m['x'] = np.ascontiguousarray(xl)
        m['valid'] = np.ascontiguousarray(vl.reshape(T // 128, 128).T)
        maps.append(m)
    return maps


def kernel(**inputs):
    if 'nc' not in _CACHE:
        _CACHE['nc'] = build()[0]
    nc = _CACHE['nc']
    in_maps = make_in_maps(inputs)
    res = run_bass_kernel_spmd(nc, in_maps, core_ids=list(range(8)))
    out = np.stack([np.concatenate([np.asarray(res.results[b]['y'], dtype=np.float32),
                                    np.asarray(res.results[4 + b]['y'], dtype=np.float32)], axis=0) for b in range(4)], axis=0)
    return out
```
